# Optimizing a Trainium2 kernel written in Bass

```python
import math
import jax, jax.numpy as jnp
from jax import lax
import numpy as np

D_MODEL = 1024
BATCH = 2
SEQ = 8192
DEPTH = 4

SSM_GROUP = 16
N_GROUPS = D_MODEL // SSM_GROUP
SSM_STATE = 64
N_HEADS = 16
HEAD_DIM = D_MODEL // N_HEADS
ATTN_DIM = N_HEADS * HEAD_DIM
D_FF = 4 * D_MODEL
Q_BLOCK = 128
N_A_LAYERS = DEPTH // 2
N_B_LAYERS = DEPTH - N_A_LAYERS
RMS_EPS = 1e-6
DT_MIN = 1e-3
DT_MAX = 1e-1

kernel_name = "s5_fox_yoco_hybrid_trunk"


def rmsnorm(x, g):
    xf = x.astype(jnp.float32)
    y = xf * lax.rsqrt(jnp.mean(xf * xf, axis=-1, keepdims=True) + RMS_EPS)
    return (y * g.astype(jnp.float32)).astype(x.dtype)


def sqrelu_mlp(h, w1, w2):
    a = jnp.square(jax.nn.relu(h @ w1))
    return a @ w2


def _ssm_binop(e1, e2):
    a1, b1 = e1
    a2, b2 = e2
    return a2 * a1, a2 * b1 + b2


def s5_mixer(u, log_dt, a_re, a_im, b_re, b_im, c_re, c_im, d_skip, w_glu):
    f32 = jnp.float32
    bsz, length, _ = u.shape
    uf = u.astype(f32).reshape(bsz, length, N_GROUPS, SSM_GROUP)
    lam = lax.complex(a_re.astype(f32), a_im.astype(f32))
    dt = jnp.exp(log_dt.astype(f32))[:, None]
    lam_bar = jnp.exp(lam * dt)
    b = lax.complex(b_re.astype(f32), b_im.astype(f32))
    b_bar = ((lam_bar - 1.0) / lam)[..., None] * b
    bu = jnp.einsum('blgc,gpc->blgp', uf.astype(jnp.complex64), b_bar)
    a_elems = jnp.broadcast_to(lam_bar, bu.shape)
    _, states = lax.associative_scan(_ssm_binop, (a_elems, bu), axis=1)
    c = lax.complex(c_re.astype(f32), c_im.astype(f32))
    y = jnp.real(jnp.einsum('blgp,gcp->blgc', states, c))
    y = y + d_skip.astype(f32).reshape(N_GROUPS, SSM_GROUP) * uf
    z = jax.nn.gelu(y.reshape(bsz, length, D_MODEL)).astype(u.dtype)
    zw = z @ w_glu
    val, gate = zw[..., :D_MODEL], zw[..., D_MODEL:]
    return val * jax.nn.sigmoid(gate)


def shared_kv(h, kv_norm, w_kvf, b_f):
    bsz, length, _ = h.shape
    kvf = rmsnorm(h, kv_norm) @ w_kvf
    k = kvf[..., :ATTN_DIM].reshape(bsz, length, N_HEADS, HEAD_DIM).transpose(0, 2, 1, 3)
    v = kvf[..., ATTN_DIM:2 * ATTN_DIM].reshape(bsz, length, N_HEADS, HEAD_DIM).transpose(0, 2, 1, 3)
    f_logit = kvf[..., 2 * ATTN_DIM:].astype(jnp.float32) + b_f.astype(jnp.float32)
    log_f = jax.nn.log_sigmoid(f_logit)
    cum_log_f = jnp.cumsum(log_f, axis=1).transpose(0, 2, 1)
    return k, v, cum_log_f


def fox_attention(hn, wq, wo, k, v, cum_log_f):
    bsz, length, _ = hn.shape
    nb = length // Q_BLOCK
    q = (hn @ wq) * (HEAD_DIM ** -0.5)
    q_blocks = q.reshape(bsz, nb, Q_BLOCK, N_HEADS, HEAD_DIM).transpose(1, 0, 3, 2, 4)
    f_blocks = cum_log_f.reshape(bsz, N_HEADS, nb, Q_BLOCK).transpose(2, 0, 1, 3)
    pos_q = jnp.arange(length, dtype=jnp.int32).reshape(nb, Q_BLOCK)
    pos_k = jnp.arange(length, dtype=jnp.int32)

    def one_block(args):
        qb, fq, pq = args
        s = jnp.einsum('bhqd,bhkd->bhqk', qb, k).astype(jnp.float32)
        s = s + fq[..., None] - cum_log_f[:, :, None, :]
        mask = pq[:, None] >= pos_k[None, :]
        s = jnp.where(mask[None, None], s, -jnp.inf)
        p = jax.nn.softmax(s, axis=-1)
        return jnp.einsum('bhqk,bhkd->bhqd', p.astype(v.dtype), v)

    o = lax.map(one_block, (q_blocks, f_blocks, pos_q))
    o = o.transpose(1, 0, 3, 2, 4).reshape(bsz, length, ATTN_DIM)
    return o @ wo


def setup_inputs(seed: int = 0) -> dict:
    key = jax.random.key(seed)
    ks = jax.random.split(key, 20)
    f32 = jnp.float32

    def nrm(k, shape, scale):
        return scale * jax.random.normal(k, shape, f32)

    x = nrm(ks[0], (BATCH, SEQ, D_MODEL), 1.0)
    mix_norm = 1.0 + nrm(ks[1], (DEPTH, D_MODEL), 0.05)
    mlp_norm = 1.0 + nrm(ks[2], (DEPTH, D_MODEL), 0.05)
    mlp_w1 = nrm(ks[3], (DEPTH, D_MODEL, D_FF), D_MODEL ** -0.5)
    mlp_w2 = nrm(ks[4], (DEPTH, D_FF, D_MODEL), 0.5 * D_FF ** -0.5)
    ssm_log_dt = jax.random.uniform(ks[5], (N_A_LAYERS, N_GROUPS), f32,
                                    math.log(DT_MIN), math.log(DT_MAX))
    ssm_a_re = -0.5 + nrm(ks[6], (N_A_LAYERS, N_GROUPS, SSM_STATE), 0.01)
    ssm_a_im = math.pi * jnp.arange(SSM_STATE, dtype=f32) + nrm(ks[7], (N_A_LAYERS, N_GROUPS, SSM_STATE), 0.01)
    ssm_b_re = nrm(ks[8], (N_A_LAYERS, N_GROUPS, SSM_STATE, SSM_GROUP), (2 * SSM_GROUP) ** -0.5)
    ssm_b_im = nrm(ks[9], (N_A_LAYERS, N_GROUPS, SSM_STATE, SSM_GROUP), (2 * SSM_GROUP) ** -0.5)
    ssm_c_re = nrm(ks[10], (N_A_LAYERS, N_GROUPS, SSM_GROUP, SSM_STATE), (2 * SSM_STATE) ** -0.5)
    ssm_c_im = nrm(ks[11], (N_A_LAYERS, N_GROUPS, SSM_GROUP, SSM_STATE), (2 * SSM_STATE) ** -0.5)
    ssm_d = nrm(ks[12], (N_A_LAYERS, D_MODEL), 1.0)
    ssm_w_glu = nrm(ks[13], (N_A_LAYERS, D_MODEL, 2 * D_MODEL), D_MODEL ** -0.5)
    kv_norm = 1.0 + nrm(ks[14], (D_MODEL,), 0.05)
    w_kvf = nrm(ks[15], (D_MODEL, 2 * ATTN_DIM + N_HEADS), D_MODEL ** -0.5)
    b_f = jax.random.uniform(ks[16], (N_HEADS,), f32, 0.5, 3.0)
    attn_wq = nrm(ks[17], (N_B_LAYERS, D_MODEL, ATTN_DIM), D_MODEL ** -0.5)
    attn_wo = nrm(ks[18], (N_B_LAYERS, ATTN_DIM, D_MODEL), ATTN_DIM ** -0.5)
    final_norm = 1.0 + nrm(ks[19], (D_MODEL,), 0.05)
    return {"x": x, "mix_norm": mix_norm, "mlp_norm": mlp_norm, "mlp_w1": mlp_w1, "mlp_w2": mlp_w2,
            "ssm_log_dt": ssm_log_dt, "ssm_a_re": ssm_a_re, "ssm_a_im": ssm_a_im,
            "ssm_b_re": ssm_b_re, "ssm_b_im": ssm_b_im, "ssm_c_re": ssm_c_re, "ssm_c_im": ssm_c_im,
            "ssm_d": ssm_d, "ssm_w_glu": ssm_w_glu, "kv_norm": kv_norm, "w_kvf": w_kvf, "b_f": b_f,
            "attn_wq": attn_wq, "attn_wo": attn_wo, "final_norm": final_norm}


def reference(x, mix_norm, mlp_norm, mlp_w1, mlp_w2, ssm_log_dt, ssm_a_re, ssm_a_im,
              ssm_b_re, ssm_b_im, ssm_c_re, ssm_c_im, ssm_d, ssm_w_glu, kv_norm, w_kvf, b_f,
              attn_wq, attn_wo, final_norm):
    h = x
    k = v = cum_log_f = None
    for i in range(DEPTH):
        hn = rmsnorm(h, mix_norm[i])
        if i < N_A_LAYERS:
            h = h + s5_mixer(hn, ssm_log_dt[i], ssm_a_re[i], ssm_a_im[i], ssm_b_re[i], ssm_b_im[i],
                             ssm_c_re[i], ssm_c_im[i], ssm_d[i], ssm_w_glu[i])
        else:
            j = i - N_A_LAYERS
            h = h + fox_attention(hn, attn_wq[j], attn_wo[j], k, v, cum_log_f)
        h = h + sqrelu_mlp(rmsnorm(h, mlp_norm[i]), mlp_w1[i], mlp_w2[i])
        if i == N_A_LAYERS - 1:
            k, v, cum_log_f = shared_kv(h, kv_norm, w_kvf, b_f)
    return rmsnorm(h, final_norm)
```

```python
import math
from contextlib import ExitStack
import numpy as np
import ml_dtypes
import concourse.bass as bass
import concourse.mybir as mybir
from concourse.bass_utils import run_bass_kernel_spmd

F32 = mybir.dt.float32
BF16 = mybir.dt.bfloat16
I32 = mybir.dt.int32
ALU = mybir.AluOpType
AF = mybir.ActivationFunctionType

ENGS = ["pe", "act", "dve", "pool", "sp"]
NDMA_SEM = 16
NEG = -30000.0
TWO_PI_LO = 6.283185
POWS = [0, 1, 2, 3, 4, 5, 6, 7, 8, 16, 32, 64, 128, 256, 512, 1024, 2048, 4096]
PIDX = {n: j for j, n in enumerate(POWS)}
NPW = len(POWS)


class V:
    __slots__ = ("buf", "ap")

    def __init__(self, buf, ap):
        self.buf = buf
        self.ap = ap


class Buf:
    __slots__ = ("t", "name", "lastw", "readers")

    def __init__(self, t, name):
        self.t = t
        self.name = name
        self.lastw = None
        self.readers = []

    def __getitem__(self, idx):
        return V(self, self.t[idx])

    def v(self, ap):
        return V(self, ap)

    def sub(self, name=None):
        return Buf(self.t, name or self.name)


class Prog:
    def __init__(self, nc, stack):
        self.nc = nc
        self.stack = stack
        self.q = {e: [] for e in ENGS}
        self.cnt = {e: 0 for e in ENGS}
        self.csem = {e: [stack.enter_context(nc.semaphore("s_" + e))] for e in ENGS}
        self.dsem = {e: [stack.enter_context(nc.semaphore("d_%s%d" % (e, i))) for i in range(NDMA_SEM)]
                     for e in ENGS}
        self.dcnt = {e: 0 for e in ENGS}
        self.known = {e: {} for e in ENGS}
        self.nbuf = 0
        self.scopes = [stack]

    def push(self):
        st = ExitStack()
        self.scopes.append(st)
        return st

    def pop(self):
        self.barrier()
        st = self.scopes.pop()
        st.close()

    def sbuf(self, shape, dtype, name=None):
        self.nbuf += 1
        name = (name or "sb") + "_%d" % self.nbuf
        t = self.scopes[-1].enter_context(self.nc.sbuf_tensor(name, list(shape), dtype))
        return Buf(t, name)

    def psum(self, shape, dtype, name=None):
        self.nbuf += 1
        name = (name or "ps") + "_%d" % self.nbuf
        t = self.scopes[-1].enter_context(self.nc.psum_tensor(name, list(shape), dtype))
        return Buf(t, name)

    def dram(self, name, shape, dtype, kind):
        t = self.nc.dram_tensor(name, list(shape), dtype, kind=kind)
        return Buf(t.ap(), name)

    EPOCH = 24000

    def _ctok(self, eng):
        self.cnt[eng] += 1
        n = self.cnt[eng]
        ep = (n - 1) // self.EPOCH
        sems = self.csem[eng]
        while len(sems) <= ep:
            sems.append(self.stack.enter_context(self.nc.semaphore("s_%s_%d" % (eng, len(sems)))))
        return ((eng, "c"), sems[ep], (n - 1) % self.EPOCH + 1, n)

    def _need(self, eng, tok, waits):
        if tok is None:
            return
        key, sem, val, absn = tok
        k = self.known[eng]
        if k.get(key, 0) >= absn:
            return
        k[key] = absn
        waits.append((sem, val))

    def _deps(self, eng, reads, writes, pe_ok):
        waits = []

        def skip(tok):
            return pe_ok and tok[0] == ("pe", "c")
        for b in reads:
            if b.lastw is not None and not skip(b.lastw):
                self._need(eng, b.lastw, waits)
        for b in writes:
            if b.lastw is not None and not skip(b.lastw):
                self._need(eng, b.lastw, waits)
            for r in b.readers:
                if not skip(r):
                    self._need(eng, r, waits)
        return waits

    def _record(self, tok, reads, writes):
        for b in writes:
            b.lastw = tok
            b.readers = []
        for b in reads:
            if b in writes:
                continue
            if tok[0][1] == "c":
                b.readers = [r for r in b.readers if r[0] != tok[0]]
            b.readers.append(tok)

    def I(self, eng, meth, xr=(), xw=(), **kw):
        reads, writes, args = list(xr), list(xw), {}
        for k, v in kw.items():
            if isinstance(v, V):
                (writes if k in ("out", "accum_out") else reads).append(v.buf)
                args[k] = v.ap
            else:
                args[k] = v
        waits = self._deps(eng, reads, writes, eng == "pe")
        tok = self._ctok(eng)
        sem = tok[1]
        self._record(tok, reads, writes)

        def emit(e, meth=meth, args=args, waits=waits, sem=sem):
            for (s, v) in waits:
                e.wait_ge(s, v)
            getattr(e, meth)(**args).then_inc(sem, 1)
        self.q[eng].append(emit)

    def dma(self, eng, out, in_, **kw):
        reads, writes = [in_.buf], [out.buf]
        waits = self._deps(eng, reads, writes, False)
        i = self.dcnt[eng]
        self.dcnt[eng] += 1
        slot, rnd = i % NDMA_SEM, i // NDMA_SEM
        sem = self.dsem[eng][slot]
        key = (eng, "d", slot)
        if rnd > 0:
            self._need(eng, (key, sem, 16 * rnd, 16 * rnd), waits)
        tok = (key, sem, 16 * (rnd + 1), 16 * (rnd + 1))
        self._record(tok, reads, writes)

        def emit(e, waits=waits, sem=sem, o=out.ap, i_=in_.ap, kw=kw):
            for (s, v) in waits:
                e.wait_ge(s, v)
            e.dma_start(out=o, in_=i_, **kw).then_inc(sem, 16)
        self.q[eng].append(emit)

    def barrier(self):
        toks = []
        for e in ENGS:
            if self.cnt[e] > 0:
                n_ = self.cnt[e]
                ep_ = (n_ - 1) // self.EPOCH
                toks.append(((e, "c"), self.csem[e][ep_], (n_ - 1) % self.EPOCH + 1, n_))
            for slot in range(NDMA_SEM):
                n = (self.dcnt[e] - slot + NDMA_SEM - 1) // NDMA_SEM
                if n > 0:
                    toks.append(((e, "d", slot), self.dsem[e][slot], 16 * n, 16 * n))
        for e in ENGS:
            waits = []
            for t in toks:
                self._need(e, t, waits)

            def emit(en, waits=waits):
                for (s, v) in waits:
                    en.wait_ge(s, v)
            self.q[e].append(emit)

    def finish(self):
        self.barrier()
        nc = self.nc
        with nc.Block() as block:
            @block.tensor
            def _(e):
                for f in self.q["pe"]:
                    f(e)

            @block.scalar
            def _(e):
                for f in self.q["act"]:
                    f(e)

            @block.vector
            def _(e):
                for f in self.q["dve"]:
                    f(e)

            @block.gpsimd
            def _(e):
                for f in self.q["pool"]:
                    f(e)

            @block.sync
            def _(e):
                for f in self.q["sp"]:
                    f(e)


class Ctx:
    pass


def make_consts(P, C):
    identf = P.sbuf([128, 128], F32, "identf")
    C.ident = P.sbuf([128, 128], BF16, "ident")
    _memset(P, "pool", identf[:], 0.0)
    P.I("pool", "affine_select", out=identf[:], in_=identf[:], pattern=[[-1, 128]], base=0,
        channel_multiplier=1, compare_op=ALU.not_equal, fill=1.0)
    P.I("pool", "tensor_copy", out=C.ident[:], in_=identf[:])
    C.identf = identf


def _memset(P, eng, v, val):
    buf = v.buf
    waits = P._deps(eng, [], [buf], False)
    tok = P._ctok(eng)
    sem = tok[1]
    P._record(tok, [], [buf])

    def emit(e, ap=v.ap, val=val, waits=waits, sem=sem):
        for (s, x) in waits:
            e.wait_ge(s, x)
        e.memset(ap, val).then_inc(sem, 1)
    P.q[eng].append(emit)


def load_h(P, C, src):
    if not hasattr(C, "hb"):
        C.hb = [P.sbuf([128, 8, 1024], F32, "h%d" % b) for b in range(2)]
        C.h = [[C.hb[b].sub("h%d_%d" % (b, i)) for i in range(8)] for b in range(2)]
    sv = src.t.rearrange("(b k i) d -> b k i d", b=2, k=128)
    for b in range(2):
        for i in range(8):
            P.dma("sp" if i % 2 == 0 else "act", V(C.h[b][i], C.hb[b].t[:, i, :]), V(src, sv[b, :, i, :]))


def store_h(P, C, dst):
    dv = dst.t.rearrange("(b k i) d -> b k i d", b=2, k=128)
    for b in range(2):
        for i in range(8):
            P.dma("sp", V(dst, dv[b, :, i, :]), V(C.h[b][i], C.hb[b].t[:, i, :]))


def load_bcast(P, dram_vec_ap, dram_buf, n, name):
    t = P.sbuf([128, n], F32, name)
    P.dma("act", t[:], V(dram_buf, dram_vec_ap.partition_broadcast(128)))
    return t


def rmsnorm_T(P, C, gain, xT, xTr, pst, out_tm=None):
    ss = P.sbuf([128, 16], F32, "ss")
    rs = P.sbuf([128, 16], F32, "rs")
    junk = P.sbuf([128, 1024], F32, "junk")
    hn = [P.sbuf([128, 1024], BF16, "hn%d" % j) for j in range(2)]
    for b in range(2):
        for i in range(8):
            j = b * 8 + i
            hv = V(C.h[b][i], C.hb[b].t[:, i, :])
            P.I("act", "activation", out=junk[:], in_=hv, func=AF.Square, accum_out=ss[:, j:j + 1])
    P.I("dve", "tensor_scalar", out=rs[:], in0=ss[:], scalar1=1.0 / 1024, scalar2=1e-6, op0=ALU.mult, op1=ALU.add)
    P.I("act", "activation", out=rs[:], in_=rs[:], func=AF.Sqrt)
    P.I("dve", "reciprocal", out=rs[:], in_=rs[:])
    C.rs = rs
    for b in range(2):
        for i in range(8):
            j = b * 8 + i
            hv = V(C.h[b][i], C.hb[b].t[:, i, :])
            if out_tm is not None:
                P.I("dve", "scalar_tensor_tensor", out=out_tm[j][:], in0=hv, scalar=rs[:, j:j + 1], in1=gain[:],
                    op0=ALU.mult, op1=ALU.mult)
                continue
            hb = hn[j % 2]
            P.I("dve", "scalar_tensor_tensor", out=hb[:], in0=hv, scalar=rs[:, j:j + 1], in1=gain[:],
                op0=ALU.mult, op1=ALU.mult)
            pt = pst[j % 2]
            for c in range(8):
                P.I("pe", "transpose", out=pt[:, c, :], in_=hb[:, c * 128:(c + 1) * 128], identity=C.ident[:])
            P.I("act" if j % 2 else "dve", "activation" if j % 2 else "tensor_copy",
                out=V(xTr[j], xT.t[:, :, j * 128:(j + 1) * 128]), in_=pt[:], **({"func": AF.Copy} if j % 2 else {}))


def mlp(P, C, l, D, xT, xTr, pst, pm):
    gain = load_bcast(P, D.mlp_norm.t[l, :], D.mlp_norm, 1024, "mg")
    rmsnorm_T(P, C, gain, xT, xTr, pst)
    w1b = [P.sbuf([128, 8, 512], BF16, "w1b%d" % j) for j in range(2)]
    w2b = [P.sbuf([128, 4, 1024], BF16, "w2b%d" % j) for j in range(2)]
    hid = P.sbuf([128, 4, 2048], BF16, "hid")
    hidr = [[hid.sub() for tg in range(4)] for fc in range(4)]
    rl = [P.sbuf([128, 512], F32, "rl%d" % j) for j in range(2)]
    w1v = D.mlp_w1.t[l].rearrange("(c p) f -> p c f", p=128)
    w2v = D.mlp_w2.t[l].rearrange("(c p) n -> p c n", p=128)
    n = 0
    for fb in range(8):
        a, bb = w1b[fb % 2], w2b[fb % 2]
        P.dma("pool", a[:], V(D.mlp_w1, w1v[:, :, fb * 512:(fb + 1) * 512]))
        P.dma("pool", bb[:], V(D.mlp_w2, w2v[:, fb * 4:(fb + 1) * 4, :]))
        for fc in range(4):
            for tg in range(4):
                ps = pm[n % len(pm)]
                r = rl[n % 2]
                n += 1
                for c in range(8):
                    P.I("pe", "matmul", out=ps[:], lhsT=a[:, c, fc * 128:(fc + 1) * 128],
                        rhs=xT[:, c, tg * 512:(tg + 1) * 512], start=(c == 0), stop=(c == 7), xr=xTr[tg * 4:tg * 4 + 4])
                P.I("act", "activation", out=r[:], in_=ps[:], func=AF.Relu)
                P.I("pool", "tensor_tensor", out=V(hidr[fc][tg], hid.t[:, fc, tg * 512:(tg + 1) * 512]), in0=r[:], in1=r[:],
                    op=ALU.mult)
        for tt in range(16):
            b, i = tt // 8, tt % 8
            for half in range(2):
                ps = pm[n % len(pm)]
                n += 1
                for fc in range(4):
                    P.I("pe", "matmul", out=ps[:], lhsT=V(hidr[fc][tt // 4], hid.t[:, fc, tt * 128:(tt + 1) * 128]),
                        rhs=bb[:, fc, half * 512:(half + 1) * 512], start=(fc == 0), stop=(fc == 3))
                hv = V(C.h[b][i], C.hb[b].t[:, i, half * 512:(half + 1) * 512])
                P.I("dve", "tensor_tensor", out=hv, in0=hv, in1=ps[:], op=ALU.add)


def s5_params(P, C, l, D):
    S = Ctx()
    nc_slow = dict(allow_slow_non_contiguous=True)
    S.LR = P.sbuf([128, NPW, 32], F32, "LR")
    S.LI = P.sbuf([128, NPW, 32], F32, "LI")
    S.NLI = P.sbuf([128, NPW, 32], F32, "NLI")
    S.BR = P.sbuf([128, 32, 16], F32, "BR")
    S.BI = P.sbuf([128, 32, 16], F32, "BI")
    cr = P.sbuf([128, 32, 16], F32, "cr")
    ci = P.sbuf([128, 32, 16], F32, "ci")
    S.dcol = P.sbuf([128, 8], F32, "dcol")
    P.push()
    are = P.sbuf([128, 32], F32, "are")
    aim = P.sbuf([128, 32], F32, "aim")
    ldt = P.sbuf([128, 32], F32, "ldt")
    br = P.sbuf([128, 32, 16], F32, "br")
    bi = P.sbuf([128, 32, 16], F32, "bi")
    for g2 in range(2):
        ps_ = slice(g2 * 64, (g2 + 1) * 64)
        for (dst, src) in ((are, D.ssm_a_re), (aim, D.ssm_a_im)):
            P.dma("sp", dst[ps_, :], V(src, src.t[l].rearrange("(gp g2) p -> g2 p gp", g2=2)[g2]), **nc_slow)
        P.dma("sp", ldt[ps_, :], V(D.ssm_log_dt, D.ssm_log_dt.t[l].rearrange("(gp g2) -> g2 gp", g2=2)[g2]
                                    .partition_broadcast(64)), **nc_slow)
        for (dst, src) in ((br, D.ssm_b_re), (bi, D.ssm_b_im)):
            P.dma("act", dst[ps_, :, :], V(src, src.t[l].rearrange("(gp g2) p c -> g2 p gp c", g2=2)[g2]))
    P.dma("sp", S.dcol[:], V(D.ssm_d, D.ssm_d.t[l].rearrange("(c p) -> p c", p=128)), **nc_slow)
    pct = P.psum([128, 4, 128], F32, "pct")
    for ti, (dst, src) in enumerate(((cr, D.ssm_c_re), (ci, D.ssm_c_im))):
        ct2 = P.sbuf([128, 8, 2, 64], F32, "ct2_%d" % ti)
        sv = src.t[l].rearrange("(gb gl) c p -> (gl c) gb p", gl=8)
        for dup in range(2):
            P.dma("act" if dup else "sp", ct2[:, :, dup, :], V(src, sv))
        for half in range(2):
            for k4 in range(4):
                gb = half * 4 + k4
                P.I("pe", "transpose", out=pct[:, k4, :], in_=ct2.v(ct2.t[:, gb, :, :].rearrange("q d p -> q (d p)")),
                    identity=C.identf[:])
            for g2 in range(2):
                ps_ = slice(g2 * 64, (g2 + 1) * 64)
                srcv = pct.v(pct.t[ps_, :, :].rearrange("q k (gpl g c) -> q k gpl g c", g=2, c=16)[:, :, :, g2, :])
                dstv = dst.v(dst.t[ps_, half * 16:(half + 1) * 16, :].rearrange("q (k gpl) c -> q k gpl c", k=4))
                P.I("dve" if g2 else "act", "tensor_copy" if g2 else "activation", out=dstv, in_=srcv,
                    **({} if g2 else {"func": AF.Copy}))
    dt = P.sbuf([128, 32], F32, "dt")
    P.I("act", "activation", out=dt[:], in_=ldt[:], func=AF.Exp)
    xr = P.sbuf([128, 32], F32, "xr")
    xi = P.sbuf([128, 32], F32, "xi")
    P.I("dve", "tensor_tensor", out=xr[:], in0=are[:], in1=dt[:], op=ALU.mult)
    P.I("dve", "tensor_tensor", out=xi[:], in0=aim[:], in1=dt[:], op=ALU.mult)
    ncst = P.sbuf([128, NPW, 32], F32, "ncst")
    for j, n in enumerate(POWS):
        _memset(P, "pool", ncst[:, j, :], float(n))
    R = P.sbuf([128, NPW, 32], F32, "R")
    E = P.sbuf([128, NPW, 32], F32, "E")
    Ki = P.sbuf([128, NPW, 32], I32, "Ki")
    Kf = P.sbuf([128, NPW, 32], F32, "Kf")
    T1 = P.sbuf([128, NPW, 32], F32, "T1")
    T2 = P.sbuf([128, NPW, 32], F32, "T2")
    xib = xi.v(xi.t[:, :].unsqueeze(1).to_broadcast([128, NPW, 32]))
    xrb = xr.v(xr.t[:, :].unsqueeze(1).to_broadcast([128, NPW, 32]))
    P.I("dve", "tensor_tensor", out=R[:], in0=ncst[:], in1=xib, op=ALU.mult)
    P.I("dve", "tensor_scalar", out=R[:], in0=R[:], scalar1=1.0 / (2 * math.pi), scalar2=None, op0=ALU.mult)
    P.I("dve", "tensor_tensor", out=E[:], in0=ncst[:], in1=xrb, op=ALU.mult)
    P.I("dve", "tensor_copy", out=Ki[:], in_=R[:])
    P.I("dve", "tensor_copy", out=Kf[:], in_=Ki[:])
    P.I("dve", "tensor_tensor", out=R[:], in0=R[:], in1=Kf[:], op=ALU.subtract)
    P.I("dve", "scalar_tensor_tensor", out=T1[:], in0=R[:], scalar=0.5, in1=R[:], op0=ALU.is_gt, op1=ALU.subtract)
    P.I("dve", "scalar_tensor_tensor", out=T2[:], in0=T1[:], scalar=0.5, in1=T1[:], op0=ALU.is_gt, op1=ALU.subtract)
    SN = P.sbuf([128, NPW, 32], F32, "SN")
    CS = P.sbuf([128, NPW, 32], F32, "CS")
    MG = P.sbuf([128, NPW, 32], F32, "MG")
    P.I("act", "activation", out=SN[:], in_=T2[:], func=AF.Sin, scale=TWO_PI_LO)
    P.I("dve", "tensor_scalar", out=T1[:], in0=T2[:], scalar1=0.25, scalar2=None, op0=ALU.add)
    P.I("dve", "scalar_tensor_tensor", out=T2[:], in0=T1[:], scalar=0.5, in1=T1[:], op0=ALU.is_gt, op1=ALU.subtract)
    P.I("act", "activation", out=CS[:], in_=T2[:], func=AF.Sin, scale=-TWO_PI_LO)
    P.I("act", "activation", out=MG[:], in_=E[:], func=AF.Exp)
    P.I("dve", "tensor_tensor", out=S.LR[:], in0=MG[:], in1=CS[:], op=ALU.mult)
    P.I("dve", "tensor_tensor", out=S.LI[:], in0=MG[:], in1=SN[:], op=ALU.mult)
    P.I("dve", "tensor_scalar", out=S.NLI[:], in0=S.LI[:], scalar1=-1.0, scalar2=None, op0=ALU.mult)
    nr = P.sbuf([128, 32], F32, "nr")
    den = P.sbuf([128, 32], F32, "den")
    t1 = P.sbuf([128, 32], F32, "t1")
    t2 = P.sbuf([128, 32], F32, "t2")
    cfr = P.sbuf([128, 32], F32, "cfr")
    cfi = P.sbuf([128, 32], F32, "cfi")
    l1r, l1i = S.LR[:, PIDX[1], :], S.LI[:, PIDX[1], :]
    P.I("dve", "tensor_scalar", out=nr[:], in0=l1r, scalar1=-1.0, scalar2=None, op0=ALU.add)
    P.I("dve", "tensor_tensor", out=den[:], in0=are[:], in1=are[:], op=ALU.mult)
    P.I("dve", "tensor_tensor", out=t1[:], in0=aim[:], in1=aim[:], op=ALU.mult)
    P.I("dve", "tensor_tensor", out=den[:], in0=den[:], in1=t1[:], op=ALU.add)
    P.I("dve", "reciprocal", out=den[:], in_=den[:])
    P.I("dve", "tensor_tensor", out=t1[:], in0=nr[:], in1=are[:], op=ALU.mult)
    P.I("dve", "tensor_tensor", out=t2[:], in0=l1i, in1=aim[:], op=ALU.mult)
    P.I("dve", "tensor_tensor", out=t1[:], in0=t1[:], in1=t2[:], op=ALU.add)
    P.I("dve", "tensor_tensor", out=cfr[:], in0=t1[:], in1=den[:], op=ALU.mult)
    P.I("dve", "tensor_tensor", out=t1[:], in0=l1i, in1=are[:], op=ALU.mult)
    P.I("dve", "tensor_tensor", out=t2[:], in0=nr[:], in1=aim[:], op=ALU.mult)
    P.I("dve", "tensor_tensor", out=t1[:], in0=t1[:], in1=t2[:], op=ALU.subtract)
    P.I("dve", "tensor_tensor", out=cfi[:], in0=t1[:], in1=den[:], op=ALU.mult)
    u1 = P.sbuf([128, 32, 16], F32, "u1")
    u2 = P.sbuf([128, 32, 16], F32, "u2")
    cfrb = cfr.v(cfr.t[:, :].unsqueeze(2).to_broadcast([128, 32, 16]))
    cfib = cfi.v(cfi.t[:, :].unsqueeze(2).to_broadcast([128, 32, 16]))
    P.I("dve", "tensor_tensor", out=u1[:], in0=br[:], in1=cfrb, op=ALU.mult)
    P.I("dve", "tensor_tensor", out=u2[:], in0=bi[:], in1=cfib, op=ALU.mult)
    P.I("dve", "tensor_tensor", out=S.BR[:], in0=u1[:], in1=u2[:], op=ALU.subtract)
    P.I("dve", "tensor_tensor", out=u1[:], in0=bi[:], in1=cfrb, op=ALU.mult)
    P.I("dve", "tensor_tensor", out=u2[:], in0=br[:], in1=cfib, op=ALU.mult)
    P.I("dve", "tensor_tensor", out=S.BI[:], in0=u1[:], in1=u2[:], op=ALU.add)
    S.CR, S.CI = cr, ci
    P.pop()
    return S


def s5_core(P, C, l, D, S, xT, xTr, mode, Fout=None, Fall=None, selS=None, carry=None):
    pw = P.psum([128, 16, 128], BF16, "pw")
    pk = P.psum([128, 8, 32], F32, "pk")
    pz = [P.psum([128, 2, 256], F32, "pz%d" % j) for j in range(2 if mode == "A" else 1)]
    py = [P.psum([128, 8, 128], F32, "py%d" % j) for j in range(2)] if mode == "B" else None
    XB = P.sbuf([128, 8, 2, 4, 32], BF16, "XB")
    CB = P.sbuf([128, 9, 2, 4, 32], BF16, "CB")
    _memset(P, "pool", XB[:], 0.0)
    _memset(P, "pool", CB[:], 0.0)
    WT = P.sbuf([128, 16, 128], BF16, "WT")
    KT = P.sbuf([128, 8, 32], BF16, "KT")
    v1 = P.sbuf([128, 9, 4, 16], F32, "v1")
    v2 = P.sbuf([128, 9, 4, 16], F32, "v2")
    ZW = [[P.sbuf([128, 2, 256], F32, "zw%d_%d" % (m, j)) for j in range(2)] for m in range(4)]
    tt_ = [P.sbuf([128, 256], F32, "tt%d" % m) for m in range(4)]
    tu_ = [P.sbuf([128, 256], F32, "tu%d" % m) for m in range(4)]
    if mode == "A":
        Fsb = P.sbuf([128, 32, 2], F32, "Fsb")
    else:
        SP = P.sbuf([128, 4, 2, 256], BF16, "SP")
        SPr = [SP.sub() for m in range(4)]
        e1 = P.sbuf([128, 1024], F32, "e1")
        e2 = P.sbuf([128, 1024], F32, "e2")
        e3 = P.sbuf([128, 1024], F32, "e3")
        if carry is None:
            FA = P.sbuf([128, 8, 64], F32, "FA")
            P.dma("sp", FA[:], V(Fall, Fall.t.rearrange("c p f -> p c f")))
            SL = P.sbuf([128, 24], F32, "SL")
            P.dma("sp", SL[:], selS[:])
            acc = [P.sbuf([128, 32, 2], F32, "acc%d" % j) for j in range(3)]
            for mm in range(3):
                av = acc[mm].v(acc[mm].t[:].rearrange("p g r -> p (g r)"))
                for c in range(8):
                    if c == 0:
                        P.I("dve", "tensor_scalar", out=av, in0=FA[:, 0, :], scalar1=SL[:, mm:mm + 1], scalar2=None,
                            op0=ALU.mult)
                    else:
                        P.I("dve", "scalar_tensor_tensor", out=av, in0=FA[:, c, :],
                            scalar=SL[:, c * 3 + mm:c * 3 + mm + 1], in1=av, op0=ALU.mult, op1=ALU.add)
        SIN = P.sbuf([128, 32, 2], F32, "SIN")
        w1 = P.sbuf([128, 32], F32, "w1")
        w2 = P.sbuf([128, 32], F32, "w2")

        def cmul_acc(dst, src, pw_, first):
            lr, li = S.LR[:, PIDX[pw_], :], S.LI[:, PIDX[pw_], :]
            P.I("dve", "tensor_tensor", out=w1[:], in0=src[:, :, 0], in1=lr, op=ALU.mult)
            P.I("dve", "tensor_tensor", out=w2[:], in0=src[:, :, 1], in1=li, op=ALU.mult)
            P.I("dve", "tensor_tensor", out=w1[:], in0=w1[:], in1=w2[:], op=ALU.subtract)
            if first:
                P.I("dve", "tensor_copy", out=dst[:, :, 0], in_=w1[:])
            else:
                P.I("dve", "tensor_tensor", out=dst[:, :, 0], in0=dst[:, :, 0], in1=w1[:], op=ALU.add)
            P.I("dve", "tensor_tensor", out=w1[:], in0=src[:, :, 1], in1=lr, op=ALU.mult)
            P.I("dve", "tensor_tensor", out=w2[:], in0=src[:, :, 0], in1=li, op=ALU.mult)
            P.I("dve", "tensor_tensor", out=w1[:], in0=w1[:], in1=w2[:], op=ALU.add)
            if first:
                P.I("dve", "tensor_copy", out=dst[:, :, 1], in_=w1[:])
            else:
                P.I("dve", "tensor_tensor", out=dst[:, :, 1], in0=dst[:, :, 1], in1=w1[:], op=ALU.add)
        if carry is None:
            P.I("dve", "tensor_copy", out=SIN[:], in_=acc[0][:])
            cmul_acc(SIN, acc[1], 2048, False)
            cmul_acc(SIN, acc[2], 4096, False)
        else:
            P.I("dve", "tensor_copy", out=SIN[:], in_=carry[:])
        INJ = P.sbuf([128, 32, 2], F32, "INJ")
        cmul_acc(INJ, SIN, 8, True)

    for ch in range(8):
        gs = slice(ch * 4, ch * 4 + 4)
        xreg = [xTr[j] for j in range(16)]
        for half in range(2):
            pr = slice(half * 64, half * 64 + 64)
            hs = slice(half * 16, half * 16 + 16)
            for (tab, n0, nn, A_r, A_i, conjC) in ((XB, 0, 8, S.BR, S.BI, False), (CB, 0, 9, S.CR, S.CI, True)):
                ar = A_r.v(A_r.t[pr, gs, :].unsqueeze(1).to_broadcast([64, nn, 4, 16]))
                ai = A_i.v(A_i.t[pr, gs, :].unsqueeze(1).to_broadcast([64, nn, 4, 16]))
                lr = S.LR.v(S.LR.t[pr, n0:n0 + nn, gs].unsqueeze(3).to_broadcast([64, nn, 4, 16]))
                li = S.LI.v(S.LI.t[pr, n0:n0 + nn, gs].unsqueeze(3).to_broadcast([64, nn, 4, 16]))
                a1, a2 = v1[pr, 0:nn], v2[pr, 0:nn]
                P.I("dve", "tensor_tensor", out=a1, in0=ar, in1=lr, op=ALU.mult)
                P.I("pool", "tensor_tensor", out=a2, in0=ai, in1=li, op=ALU.mult)
                P.I("dve", "tensor_tensor", out=tab[pr, 0:nn, 0, :, hs], in0=a1, in1=a2, op=ALU.subtract)
                P.I("dve", "tensor_tensor", out=a1, in0=ar, in1=li, op=ALU.mult)
                P.I("pool", "tensor_tensor", out=a2, in0=ai, in1=lr, op=ALU.mult)
                if conjC:
                    P.I("dve", "tensor_tensor", out=a1, in0=a1, in1=a2, op=ALU.add)
                    P.I("dve", "tensor_scalar", out=tab[pr, 0:nn, 1, :, hs], in0=a1, scalar1=-1.0, scalar2=None,
                        op0=ALU.mult)
                else:
                    P.I("dve", "tensor_tensor", out=tab[pr, 0:nn, 1, :, hs], in0=a1, in1=a2, op=ALU.add)
        for n in range(8):
            for r in range(2):
                P.I("pe", "transpose", out=pw[:, n * 2 + r, :],
                    in_=XB.v(XB.t[:, n, r, :, :].rearrange("p m c -> p (m c)")), identity=C.ident[:])
        P.I("act", "activation", out=WT[:], in_=pw[:], func=AF.Copy)
        if mode == "B":
            for tau in range(8):
                for m in range(4):
                    for r in range(2):
                        P.I("pe", "matmul", out=pk[m * 32:(m + 1) * 32, tau, :], lhsT=XB[:, tau, r, m, :],
                            rhs=CB[:, 0, r, m, :], start=(r == 0), stop=(r == 1), tile_position=(0, m * 32))
            P.I("act", "activation", out=KT[:], in_=pk[:], func=AF.Copy)
        for m in range(4):
            pzz = pz[m % len(pz)]
            rs_ = slice(m * 32, m * 32 + 32)
            for r in range(2):
                for i in range(8):
                    rhs = xT.v(xT.t[rs_, ch, :].rearrange("p (b i k) -> p b i k", b=2, i=8)[:, :, i, :])
                    P.I("pe", "matmul", out=pzz[:, r, :], lhsT=WT[rs_, (7 - i) * 2 + r, :], rhs=rhs,
                        start=(i == 0), stop=(i == 7), tile_position=(m * 32, 0), xr=xreg)
            P.I("act", "activation", out=ZW[m][0][:], in_=pzz[:], func=AF.Copy)
            if mode == "B":
                gp = ch * 4 + m
                P.I("pool", "tensor_tensor", out=ZW[m][0][:, :, 0], in0=ZW[m][0][:, :, 0], in1=INJ[:, gp, :], op=ALU.add)
        if mode == "A":
            cur = [0, 0, 0, 0]
            ln = 256
            for s in range(8):
                pidx = PIDX[8 << s]
                hf = ln // 2
                for m in range(4):
                    gp = ch * 4 + m
                    a = ZW[m][cur[m]]
                    lr = S.LR[:, pidx, gp:gp + 1]
                    P.I("dve", "scalar_tensor_tensor", out=tt_[m][:, 0:hf], in0=a[:, 0, 0:ln:2], scalar=lr,
                        in1=a[:, 0, 1:ln:2], op0=ALU.mult, op1=ALU.add)
                    P.I("dve", "scalar_tensor_tensor", out=tu_[m][:, 0:hf], in0=a[:, 1, 0:ln:2], scalar=lr,
                        in1=a[:, 1, 1:ln:2], op0=ALU.mult, op1=ALU.add)
                for m in range(4):
                    gp = ch * 4 + m
                    a, b_ = ZW[m][cur[m]], ZW[m][1 - cur[m]]
                    li, nli = S.LI[:, pidx, gp:gp + 1], S.NLI[:, pidx, gp:gp + 1]
                    P.I("dve", "scalar_tensor_tensor", out=b_[:, 0, 0:hf], in0=a[:, 1, 0:ln:2], scalar=nli,
                        in1=tt_[m][:, 0:hf], op0=ALU.mult, op1=ALU.add)
                    P.I("dve", "scalar_tensor_tensor", out=b_[:, 1, 0:hf], in0=a[:, 0, 0:ln:2], scalar=li,
                        in1=tu_[m][:, 0:hf], op0=ALU.mult, op1=ALU.add)
                    cur[m] = 1 - cur[m]
                ln = hf
            for m in range(4):
                gp = ch * 4 + m
                P.I("pool", "tensor_copy", out=Fsb[:, gp, :], in_=ZW[m][cur[m]][:, :, 0])
            continue
        cur = [0, 0, 0, 0]
        for s in range(8):
            d = 1 << s
            pidx = PIDX[8 * d]
            for m in range(4):
                gp = ch * 4 + m
                a, b_ = ZW[m][cur[m]], ZW[m][1 - cur[m]]
                lr, li, nli = S.LR[:, pidx, gp:gp + 1], S.LI[:, pidx, gp:gp + 1], S.NLI[:, pidx, gp:gp + 1]
                P.I("pool", "tensor_copy", out=b_[:, :, 0:d], in_=a[:, :, 0:d])
                P.I("dve", "scalar_tensor_tensor", out=tt_[m][:, 0:256 - d], in0=a[:, 0, 0:256 - d], scalar=lr,
                    in1=a[:, 0, d:256], op0=ALU.mult, op1=ALU.add)
                P.I("dve", "scalar_tensor_tensor", out=tu_[m][:, 0:256 - d], in0=a[:, 1, 0:256 - d], scalar=lr,
                    in1=a[:, 1, d:256], op0=ALU.mult, op1=ALU.add)
            for m in range(4):
                gp = ch * 4 + m
                a, b_ = ZW[m][cur[m]], ZW[m][1 - cur[m]]
                li, nli = S.LI[:, pidx, gp:gp + 1], S.NLI[:, pidx, gp:gp + 1]
                P.I("dve", "scalar_tensor_tensor", out=b_[:, 0, d:256], in0=a[:, 1, 0:256 - d], scalar=nli,
                    in1=tt_[m][:, 0:256 - d], op0=ALU.mult, op1=ALU.add)
                P.I("dve", "scalar_tensor_tensor", out=b_[:, 1, d:256], in0=a[:, 0, 0:256 - d], scalar=li,
                    in1=tu_[m][:, 0:256 - d], op0=ALU.mult, op1=ALU.add)
                cur[m] = 1 - cur[m]
        if mode == "A":
            for m in range(4):
                gp = ch * 4 + m
                P.I("pool", "tensor_copy", out=Fsb[:, gp, :], in_=ZW[m][cur[m]][:, :, 255])
            continue
        for m in range(4):
            gp = ch * 4 + m
            fin = ZW[m][cur[m]]
            if carry is not None:
                P.I("pool", "tensor_copy", out=carry[:, gp, :], in_=fin[:, :, 255])
            P.I("act", "activation", out=V(SPr[m], SP.t[:, m, :, 1:256]), in_=fin[:, :, 0:255], func=AF.Copy)
            P.I("pool", "tensor_copy", out=V(SPr[m], SP.t[:, m, :, 0]), in_=SIN[:, gp, :])
        for b in range(2):
            pyy = py[b]
            for m in range(4):
                rs_ = slice(m * 32, m * 32 + 32)
                for j in range(8):
                    nmm = (j + 1) + 2
                    k_ = 0
                    for i in range(j + 1):
                        P.I("pe", "matmul", out=pyy[rs_, j, :], lhsT=KT[rs_, j - i, :],
                            rhs=xT[rs_, ch, b * 1024 + i * 128: b * 1024 + (i + 1) * 128],
                            start=(k_ == 0), stop=False, tile_position=(m * 32, m * 32), xr=xreg)
                        k_ += 1
                    for r in range(2):
                        P.I("pe", "matmul", out=pyy[rs_, j, :], lhsT=CB[:, j + 1, r, m, :],
                            rhs=V(SPr[m], SP.t[:, m, r, b * 128:(b + 1) * 128]),
                            start=False, stop=(r == 1), tile_position=(0, m * 32))
            uv = xT.v(xT.t[:, ch, b * 1024:(b + 1) * 1024])
            yv = pyy.v(pyy.t[:].rearrange("p j k -> p (j k)"))
            P.I("dve", "scalar_tensor_tensor", out=e1[:], in0=uv, scalar=S.dcol[:, ch:ch + 1], in1=yv,
                op0=ALU.mult, op1=ALU.add, xr=xreg)
            P.I("pool", "tensor_tensor", out=e2[:], in0=e1[:], in1=e1[:], op=ALU.mult)
            P.I("pool", "tensor_scalar", out=e2[:], in0=e2[:], scalar1=0.044715, scalar2=1.0, op0=ALU.mult, op1=ALU.add)
            P.I("pool", "tensor_tensor", out=e2[:], in0=e2[:], in1=e1[:], op=ALU.mult)
            P.I("act", "activation", out=e3[:], in_=e2[:], func=AF.Sigmoid, scale=1.5957691216)
            for i in range(8):
                P.I("dve", "tensor_tensor", out=V(xTr[b * 8 + i], xT.t[:, ch, b * 1024 + i * 128:b * 1024 + (i + 1) * 128]),
                    in0=e3[:, i * 128:(i + 1) * 128], in1=e1[:, i * 128:(i + 1) * 128], op=ALU.mult)
    if mode == "A":
        P.dma("sp", Fout[:], Fsb.v(Fsb.t[:].rearrange("p g r -> p (g r)")))


def s5_tables(P, C, S, TabW, TabC, TabK):
    P.push()
    pw = P.psum([128, 16, 128], BF16, "pw")
    pk = P.psum([128, 8, 32], F32, "pk")
    XB = P.sbuf([128, 8, 2, 4, 32], BF16, "XB")
    CBs = [P.sbuf([128, 9, 2, 4, 32], BF16, "CB%d" % j) for j in range(2)]
    WTs = [P.sbuf([128, 16, 128], BF16, "WT%d" % j) for j in range(2)]
    KTs = [P.sbuf([128, 8, 32], BF16, "KT%d" % j) for j in range(2)]
    _memset(P, "pool", XB[:], 0.0)
    for j in range(2):
        _memset(P, "pool", CBs[j][:], 0.0)
    v1 = P.sbuf([128, 9, 4, 16], F32, "v1")
    v2 = P.sbuf([128, 9, 4, 16], F32, "v2")
    for ch in range(8):
        gs = slice(ch * 4, ch * 4 + 4)
        CB, WT, KT = CBs[ch % 2], WTs[ch % 2], KTs[ch % 2]
        for half in range(2):
            pr = slice(half * 64, half * 64 + 64)
            hs = slice(half * 16, half * 16 + 16)
            for (tab, nn, A_r, A_i, conjC) in ((XB, 8, S.BR, S.BI, False), (CB, 9, S.CR, S.CI, True)):
                ar = A_r.v(A_r.t[pr, gs, :].unsqueeze(1).to_broadcast([64, nn, 4, 16]))
                ai = A_i.v(A_i.t[pr, gs, :].unsqueeze(1).to_broadcast([64, nn, 4, 16]))
                lr = S.LR.v(S.LR.t[pr, 0:nn, gs].unsqueeze(3).to_broadcast([64, nn, 4, 16]))
                li = S.LI.v(S.LI.t[pr, 0:nn, gs].unsqueeze(3).to_broadcast([64, nn, 4, 16]))
                a1, a2 = v1[pr, 0:nn], v2[pr, 0:nn]
                P.I("dve", "tensor_tensor", out=a1, in0=ar, in1=lr, op=ALU.mult)
                P.I("pool", "tensor_tensor", out=a2, in0=ai, in1=li, op=ALU.mult)
                P.I("dve", "tensor_tensor", out=tab[pr, 0:nn, 0, :, hs], in0=a1, in1=a2, op=ALU.subtract)
                P.I("dve", "tensor_tensor", out=a1, in0=ar, in1=li, op=ALU.mult)
                P.I("pool", "tensor_tensor", out=a2, in0=ai, in1=lr, op=ALU.mult)
                if conjC:
                    P.I("dve", "tensor_tensor", out=a1, in0=a1, in1=a2, op=ALU.add)
                    P.I("dve", "tensor_scalar", out=tab[pr, 0:nn, 1, :, hs], in0=a1, scalar1=-1.0, scalar2=None,
                        op0=ALU.mult)
                else:
                    P.I("dve", "tensor_tensor", out=tab[pr, 0:nn, 1, :, hs], in0=a1, in1=a2, op=ALU.add)
        for n in range(8):
            for r in range(2):
                P.I("pe", "transpose", out=pw[:, n * 2 + r, :],
                    in_=XB.v(XB.t[:, n, r, :, :].rearrange("p m c -> p (m c)")), identity=C.ident[:])
        P.I("act", "activation", out=WT[:], in_=pw[:], func=AF.Copy)
        for tau in range(8):
            for m in range(4):
                for r in range(2):
                    P.I("pe", "matmul", out=pk[m * 32:(m + 1) * 32, tau, :], lhsT=XB[:, tau, r, m, :],
                        rhs=CB[:, 0, r, m, :], start=(r == 0), stop=(r == 1), tile_position=(0, m * 32))
        P.I("act", "activation", out=KT[:], in_=pk[:], func=AF.Copy)
        P.dma("sp", V(TabW, TabW.t[ch]), WT[:])
        P.dma("act", V(TabC, TabC.t[ch]), CB.v(CB.t[:].rearrange("p n r m c -> p (n r m c)")))
        P.dma("sp", V(TabK, TabK.t[ch]), KT.v(KT.t[:].rearrange("p t c -> p (t c)")))
    P.pop()


def s5_core_pipe(P, C, l, D, S, xT, carry, TabW, TabC, TabK):
    pzs = [P.psum([128, 2, 256], F32, "pz%d" % j) for j in range(2)]
    py = P.psum([128, 8, 2, 128], F32, "py")
    CBs = [P.sbuf([128, 9, 2, 4, 32], BF16, "CB%d" % j) for j in range(2)]
    WTs = [P.sbuf([128, 16, 128], BF16, "WT%d" % j) for j in range(2)]
    KTs = [P.sbuf([128, 8, 32], BF16, "KT%d" % j) for j in range(2)]
    ZWs = [[[P.sbuf([128, 2, 256], F32, "zw%d_%d_%d" % (pp, m, j)) for j in range(2)] for m in range(4)] for pp in range(2)]
    tt_ = [P.sbuf([128, 256], F32, "tt%d" % m) for m in range(4)]
    tu_ = [P.sbuf([128, 256], F32, "tu%d" % m) for m in range(4)]
    SP = P.sbuf([128, 4, 2, 256], BF16, "SP")
    SPr = [SP.sub() for m in range(4)]
    e1 = P.sbuf([128, 1024], F32, "e1")
    e2 = P.sbuf([128, 1024], F32, "e2")
    e3 = P.sbuf([128, 1024], BF16, "e3")
    SIN = P.sbuf([128, 32, 2], F32, "SIN")
    INJ = P.sbuf([128, 32, 2], F32, "INJ")
    w1 = P.sbuf([128, 32], F32, "w1")
    w2 = P.sbuf([128, 32], F32, "w2")
    xTc = [xT.sub("xTc%d" % ch) for ch in range(8)]
    P.I("dve", "tensor_copy", out=SIN[:], in_=carry[:])
    lr8, li8 = S.LR[:, PIDX[8], :], S.LI[:, PIDX[8], :]
    P.I("dve", "tensor_tensor", out=w1[:], in0=SIN[:, :, 0], in1=lr8, op=ALU.mult)
    P.I("dve", "tensor_tensor", out=w2[:], in0=SIN[:, :, 1], in1=li8, op=ALU.mult)
    P.I("dve", "tensor_tensor", out=INJ[:, :, 0], in0=w1[:], in1=w2[:], op=ALU.subtract)
    P.I("dve", "tensor_tensor", out=w1[:], in0=SIN[:, :, 1], in1=lr8, op=ALU.mult)
    P.I("dve", "tensor_tensor", out=w2[:], in0=SIN[:, :, 0], in1=li8, op=ALU.mult)
    P.I("dve", "tensor_tensor", out=INJ[:, :, 1], in0=w1[:], in1=w2[:], op=ALU.add)
    curs = {}

    def T(ch):
        CB, KT, ZW, WT = CBs[ch % 2], KTs[ch % 2], ZWs[ch % 2], WTs[ch % 2]
        P.dma("sp", WT[:], V(TabW, TabW.t[ch]))
        P.dma("act", CB.v(CB.t[:].rearrange("p n r m c -> p (n r m c)")), V(TabC, TabC.t[ch]))
        P.dma("sp", KT.v(KT.t[:].rearrange("p t c -> p (t c)")), V(TabK, TabK.t[ch]))
        for m in range(4):
            rs_ = slice(m * 32, m * 32 + 32)
            for r in range(2):
                for i in range(8):
                    rhs = xT.v(xT.t[rs_, ch, :].rearrange("p (b i k) -> p b i k", b=2, i=8)[:, :, i, :])
                    P.I("pe", "matmul", out=pzs[m % 2][:, r, :], lhsT=WT[rs_, (7 - i) * 2 + r, :], rhs=rhs,
                        start=(i == 0), stop=(i == 7), tile_position=(m * 32, 0), xr=[xTc[ch]])
            P.I("act", "activation", out=ZW[m][0][:], in_=pzs[m % 2][:], func=AF.Copy)

    def Sx(ch):
        ZW = ZWs[ch % 2]
        cur = [0, 0, 0, 0]
        for m in range(4):
            gp = ch * 4 + m
            P.I("pool", "tensor_tensor", out=ZW[m][0][:, :, 0], in0=ZW[m][0][:, :, 0], in1=INJ[:, gp, :], op=ALU.add)
        for s_ in range(8):
            d = 1 << s_
            pidx = PIDX[8 * d]
            for m in range(4):
                gp = ch * 4 + m
                a, b_ = ZW[m][cur[m]], ZW[m][1 - cur[m]]
                lr = S.LR[:, pidx, gp:gp + 1]
                P.I("pool", "tensor_copy", out=b_[:, :, 0:d], in_=a[:, :, 0:d])
                P.I("dve", "scalar_tensor_tensor", out=tt_[m][:, 0:256 - d], in0=a[:, 0, 0:256 - d], scalar=lr,
                    in1=a[:, 0, d:256], op0=ALU.mult, op1=ALU.add)
                P.I("dve", "scalar_tensor_tensor", out=tu_[m][:, 0:256 - d], in0=a[:, 1, 0:256 - d], scalar=lr,
                    in1=a[:, 1, d:256], op0=ALU.mult, op1=ALU.add)
            for m in range(4):
                gp = ch * 4 + m
                a, b_ = ZW[m][cur[m]], ZW[m][1 - cur[m]]
                li, nli = S.LI[:, pidx, gp:gp + 1], S.NLI[:, pidx, gp:gp + 1]
                P.I("dve", "scalar_tensor_tensor", out=b_[:, 0, d:256], in0=a[:, 1, 0:256 - d], scalar=nli,
                    in1=tt_[m][:, 0:256 - d], op0=ALU.mult, op1=ALU.add)
                P.I("dve", "scalar_tensor_tensor", out=b_[:, 1, d:256], in0=a[:, 0, 0:256 - d], scalar=li,
                    in1=tu_[m][:, 0:256 - d], op0=ALU.mult, op1=ALU.add)
                cur[m] = 1 - cur[m]
        for m in range(4):
            gp = ch * 4 + m
            fin = ZW[m][cur[m]]
            P.I("pool", "tensor_copy", out=carry[:, gp, :], in_=fin[:, :, 255])
            P.I("act", "activation", out=V(SPr[m], SP.t[:, m, :, 1:256]), in_=fin[:, :, 0:255], func=AF.Copy)
            P.I("pool", "tensor_copy", out=V(SPr[m], SP.t[:, m, :, 0]), in_=SIN[:, gp, :])

    def Y(ch):
        CB, KT = CBs[ch % 2], KTs[ch % 2]
        for m in range(4):
            rs_ = slice(m * 32, m * 32 + 32)
            for j in range(8):
                for i in range(j + 1):
                    rhs = xT.v(xT.t[rs_, ch, :].rearrange("p (b i k) -> p b i k", b=2, i=8)[:, :, i, :])
                    P.I("pe", "matmul", out=py[rs_, j, :, :], lhsT=KT[rs_, j - i, :], rhs=rhs,
                        start=(i == 0), stop=False, tile_position=(m * 32, m * 32), xr=[xTc[ch]])
                for r in range(2):
                    P.I("pe", "matmul", out=py[rs_, j, :, :], lhsT=CB[:, j + 1, r, m, :],
                        rhs=V(SPr[m], SP.t[:, m, r, :].rearrange("p (b k) -> p b k", b=2)),
                        start=False, stop=(r == 1), tile_position=(0, m * 32))

    def E(ch, b):
        uv = xT.v(xT.t[:, ch, b * 1024:(b + 1) * 1024].rearrange("p (j k) -> p j k", j=8))
        yv = py[:, :, b, :]
        e1v = e1.v(e1.t[:, :].rearrange("p (j k) -> p j k", j=8))
        P.I("dve", "scalar_tensor_tensor", out=e1v, in0=uv, scalar=S.dcol[:, ch:ch + 1], in1=yv,
            op0=ALU.mult, op1=ALU.add, xr=[xTc[ch]])
        P.I("pool", "tensor_tensor", out=e2[:], in0=e1[:], in1=e1[:], op=ALU.mult)
        P.I("pool", "tensor_scalar", out=e2[:], in0=e2[:], scalar1=0.044715, scalar2=1.0, op0=ALU.mult, op1=ALU.add)
        P.I("pool", "tensor_tensor", out=e2[:], in0=e2[:], in1=e1[:], op=ALU.mult)
        P.I("act", "activation", out=e3[:], in_=e2[:], func=AF.Sigmoid, scale=1.5957691216)
        P.I("dve", "tensor_tensor", out=V(xTc[ch], xT.t[:, ch, b * 1024:(b + 1) * 1024]), in0=e3[:], in1=e1[:], op=ALU.mult)

    T(0)
    for ch in range(8):
        if ch + 1 < 8:
            T(ch + 1)
        Sx(ch)
        if ch >= 1:
            E(ch - 1, 0)
            E(ch - 1, 1)
        Y(ch)
    E(7, 0)
    E(7, 1)


def glu(P, C, l, D, xT, xTr, pm):
    wv = D.ssm_w_glu.t[l].rearrange("(c p) n -> p c n", p=128)
    wb = [[P.sbuf([128, 8, 512], BF16, "wg%d%d" % (a, j)) for j in range(2)] for a in range(2)]
    sg = [P.sbuf([128, 512], F32, "sg%d" % j) for j in range(2)]
    n = 0
    for half in range(2):
        P.dma("pool", wb[half][0][:], V(D.ssm_w_glu, wv[:, :, half * 512:(half + 1) * 512]))
        P.dma("pool", wb[half][1][:], V(D.ssm_w_glu, wv[:, :, 1024 + half * 512:1024 + (half + 1) * 512]))
        for tt in range(16):
            b, i = tt // 8, tt % 8
            pv, pg = pm[n % len(pm)], pm[(n + 1) % len(pm)]
            n += 2
            for (ps, w) in ((pv, wb[half][0]), (pg, wb[half][1])):
                for c in range(8):
                    P.I("pe", "matmul", out=ps[:], lhsT=V(xTr[tt], xT.t[:, c, tt * 128:(tt + 1) * 128]), rhs=w[:, c, :],
                        start=(c == 0), stop=(c == 7))
            s = sg[tt % 2]
            P.I("act", "activation", out=s[:], in_=pg[:], func=AF.Sigmoid)
            P.I("dve", "tensor_tensor", out=s[:], in0=s[:], in1=pv[:], op=ALU.mult)
            hv = V(C.h[b][i], C.hb[b].t[:, i, half * 512:(half + 1) * 512])
            P.I("pool", "tensor_tensor", out=hv, in0=hv, in1=s[:], op=ALU.add)


PARAMS = [("mix_norm", [4, 1024]), ("mlp_norm", [4, 1024]), ("mlp_w1", [4, 1024, 4096]), ("mlp_w2", [4, 4096, 1024]),
          ("ssm_log_dt", [2, 64]), ("ssm_a_re", [2, 64, 64]), ("ssm_a_im", [2, 64, 64]),
          ("ssm_b_re", [2, 64, 64, 16]), ("ssm_b_im", [2, 64, 64, 16]), ("ssm_c_re", [2, 64, 16, 64]),
          ("ssm_c_im", [2, 64, 16, 64]), ("ssm_d", [2, 1024]), ("ssm_w_glu", [2, 1024, 2048]), ("kv_norm", [1024]),
          ("w_kvf", [1024, 2064]), ("b_f", [16]), ("attn_wq", [2, 1024, 1024]), ("attn_wo", [2, 1024, 1024]),
          ("final_norm", [1024])]


def declare(P, names):
    D = Ctx()
    for (n, shp) in PARAMS:
        if n in names:
            setattr(D, n, P.dram(n, shp, F32, "ExternalInput"))
    return D


def kv_stage(P, C, D, xT, xTr, KTo, Vo, Flo, Fto):
    P.push()
    pst = [P.psum([128, 8, 128], BF16, "pst%d" % j) for j in range(2)]
    gain = load_bcast(P, D.kv_norm.t[:], D.kv_norm, 1024, "kg")
    rmsnorm_T(P, C, gain, xT, xTr, pst)
    P.pop()
    P.push()
    pm = [P.psum([128, 512], F32, "pm%d" % j) for j in range(4)]
    wsrc = D.w_kvf.t.rearrange("(c p) n -> p c n", p=128)
    wk = P.sbuf([128, 8, 1024], BF16, "wk")
    wv = P.sbuf([128, 8, 1024], BF16, "wv")
    wf = P.sbuf([128, 8, 16], BF16, "wf")
    P.dma("pool", wk[:], V(D.w_kvf, wsrc[:, :, 0:1024]))
    P.dma("pool", wv[:], V(D.w_kvf, wsrc[:, :, 1024:2048]))
    P.dma("pool", wf[:], V(D.w_kvf, wsrc[:, :, 2048:2064]))
    ktb = [P.sbuf([64, 2048], BF16, "ktb%d" % j) for j in range(2)]
    vb = [P.sbuf([128, 1024], BF16, "vb%d" % j) for j in range(2)]
    n = 0
    for h in range(16):
        kt = ktb[h % 2]
        for tg in range(4):
            ps = pm[n % 4]
            n += 1
            for c in range(8):
                P.I("pe", "matmul", out=ps[0:64, :], lhsT=wk[:, c, h * 64:(h + 1) * 64],
                    rhs=xT[:, c, tg * 512:(tg + 1) * 512], start=(c == 0), stop=(c == 7), xr=xTr[tg * 4:tg * 4 + 4])
            if n % 2:
                P.I("act", "activation", out=kt[:, tg * 512:(tg + 1) * 512], in_=ps[0:64, :], func=AF.Copy)
            else:
                P.I("dve", "tensor_copy", out=kt[:, tg * 512:(tg + 1) * 512], in_=ps[0:64, :])
        P.dma("sp", V(KTo, KTo.t[h]), kt[:])
    for tt in range(16):
        v_ = vb[tt % 2]
        for half in range(2):
            ps = pm[n % 4]
            n += 1
            for c in range(8):
                P.I("pe", "matmul", out=ps[:], lhsT=V(xTr[tt], xT.t[:, c, tt * 128:(tt + 1) * 128]),
                    rhs=wv[:, c, half * 512:(half + 1) * 512], start=(c == 0), stop=(c == 7))
            if n % 2:
                P.I("act", "activation", out=v_[:, half * 512:(half + 1) * 512], in_=ps[:], func=AF.Copy)
            else:
                P.I("dve", "tensor_copy", out=v_[:, half * 512:(half + 1) * 512], in_=ps[:])
        P.dma("sp", V(Vo, Vo.t[tt * 128:(tt + 1) * 128, :]), v_[:])
    LF = P.sbuf([16, 2048], F32, "LF")
    nbf = P.sbuf([16, 1], F32, "nbf")
    P.dma("sp", nbf[:], V(D.b_f, D.b_f.t.rearrange("(h o) -> h o", o=1)))
    P.I("dve", "tensor_scalar", out=nbf[:], in0=nbf[:], scalar1=-1.0, scalar2=None, op0=ALU.mult)
    et = [P.sbuf([16, 512], F32, "et%d" % j) for j in range(2)]
    for tg in range(4):
        ps = pm[n % 4]
        n += 1
        for c in range(8):
            P.I("pe", "matmul", out=ps[0:16, :], lhsT=wf[:, c, :], rhs=xT[:, c, tg * 512:(tg + 1) * 512],
                start=(c == 0), stop=(c == 7), xr=xTr[tg * 4:tg * 4 + 4])
        P.I("act", "activation", out=et[tg % 2][:], in_=ps[0:16, :], func=AF.Exp, bias=nbf[:, 0:1], scale=-1.0)
        P.I("act", "activation", out=LF[:, tg * 512:(tg + 1) * 512], in_=et[tg % 2][:], func=AF.Ln, bias=1.0, scale=1.0)
    ones = P.sbuf([16, 128], F32, "ones16")
    _memset(P, "pool", ones[:], 1.0)
    PI = P.sbuf([16, 2, 128], F32, "PI")
    EX = P.sbuf([16, 2, 128], F32, "EX")
    L4 = LF.t[:, :].rearrange("h (b i k) -> h b i k", b=2, i=8)
    for b in range(2):
        for i in range(1, 8):
            P.I("dve", "tensor_tensor", out=LF.v(L4[:, b, i, :]), in0=LF.v(L4[:, b, i, :]), in1=LF.v(L4[:, b, i - 1, :]),
                op=ALU.add)
        init = 0.0 if b == 0 else PI[:, 0, 127:128]
        P.I("dve", "tensor_tensor_scan", out=PI[:, b, :], data0=ones[:], data1=LF.v(L4[:, b, 7, :]), initial=init,
            op0=ALU.mult, op1=ALU.add)
        P.I("dve", "tensor_tensor", out=EX[:, b, :], in0=PI[:, b, :], in1=LF.v(L4[:, b, 7, :]), op=ALU.subtract)
        P.I("dve", "tensor_tensor", out=LF.v(L4[:, b, :, :]), in0=LF.v(L4[:, b, :, :]),
            in1=EX.v(EX.t[:, b, :].unsqueeze(1).to_broadcast([16, 8, 128])), op=ALU.add)
    P.I("dve", "tensor_scalar", out=LF[:], in0=LF[:], scalar1=-1.0, scalar2=None, op0=ALU.mult)
    P.dma("sp", Flo[:], LF[:])
    P.dma("sp", Fto[:], LF[:, 2047:2048], allow_slow_non_contiguous=True)
    P.pop()


def split3(P, src, dst_dram_views, tmp):
    hi, r1, mid, lo = tmp
    n = None
    P.I("dve", "tensor_copy", out=hi[:], in_=src)
    P.I("dve", "tensor_tensor", out=r1[:], in0=src, in1=hi[:], op=ALU.subtract)
    P.I("dve", "tensor_copy", out=mid[:], in_=r1[:])
    P.I("dve", "tensor_tensor", out=r1[:], in0=r1[:], in1=mid[:], op=ALU.subtract)
    P.I("dve", "tensor_copy", out=lo[:], in_=r1[:])
    for piece, dv in zip((hi, mid, lo), dst_dram_views):
        P.dma("sp", dv, piece[:])


def att_prep(P, C, A):
    P.push()
    ft = P.sbuf([16, 4], F32, "ft")
    sq = P.sbuf([16, 4], F32, "sq")
    off = P.sbuf([16, 4], F32, "off")
    oo = P.sbuf([16, 1], F32, "oo")
    P.dma("sp", ft[:], A.Ftg[:])
    P.dma("sp", sq[:], A.selq[:])
    _memset(P, "pool", off[:], 0.0)
    for j in range(1, 4):
        P.I("dve", "tensor_tensor", out=off[:, j:j + 1], in0=off[:, j - 1:j], in1=ft[:, j - 1:j], op=ALU.add)
    P.I("dve", "tensor_tensor", out=sq[:], in0=sq[:], in1=ft[:], op=ALU.mult)
    P.I("dve", "tensor_reduce", out=oo[:], in_=sq[:], axis=mybir.AxisListType.X, op=ALU.add)
    fl = [P.sbuf([16, 2048], F32, "fl%d" % j) for j in range(2)]
    tmp = (P.sbuf([16, 2048], BF16, "s_hi"), P.sbuf([16, 2048], F32, "s_r1"), P.sbuf([16, 2048], BF16, "s_mid"),
           P.sbuf([16, 2048], BF16, "s_lo"))
    for j in range(4):
        f = fl[j % 2]
        P.dma("act", f[:], V(A.Flg, A.Flg.t[j]))
        P.I("dve", "tensor_scalar", out=f[:], in0=f[:], scalar1=off[:, j:j + 1], scalar2=-1.0, op0=ALU.add, op1=ALU.mult)
        split3(P, f[:], [V(A.Hd, A.Hd.t[:, r, j * 2048:(j + 1) * 2048]) for r in range(3)], tmp)
    f = fl[0]
    P.dma("act", f[:], A.Fown[:])
    P.I("dve", "tensor_scalar", out=f[:], in0=f[:], scalar1=oo[:, 0:1], scalar2=None, op0=ALU.add)
    split3(P, f[:], [V(A.Gd, A.Gd.t[:, r, :]) for r in range(3)], tmp)
    P.pop()


def attention(P, C, j, D, A, xT, xTr):
    P.push()
    pss = [P.psum([128, 512], F32, "pss%d" % i) for i in range(4)]
    ppo = [P.psum([128, 512], F32, "ppo%d" % i) for i in range(2)]
    ps2 = [P.psum([128, 512], F32, "ps2%d" % i) for i in range(2)]
    psb = ps2[1]
    AB = P.sbuf([128, 32], F32, "AB")
    P.dma("sp", AB[:], A.AB[:])
    KA = [P.sbuf([128, 8192], BF16, "KA%d" % i) for i in range(2)]
    VA = [P.sbuf([128, 64, 65], BF16, "VA%d" % i) for i in range(2)]
    QA = [P.sbuf([128, 2048], BF16, "QA%d" % i) for i in range(2)]
    for i in range(2):
        _memset(P, "pool", KA[i][:], 0.0)
        _memset(P, "pool", QA[i][:], 0.0)
        _memset(P, "pool", KA[i][64:70, :], 1.0)
        _memset(P, "pool", QA[i][64:70, :], 1.0)
        _memset(P, "pool", VA[i][:, :, 64:65], 1.0)
    zt = P.sbuf([128, 512], BF16, "zt")
    _memset(P, "pool", zt[:], 0.0)
    TRI = P.sbuf([128, 8, 2, 512], BF16, "TRI")
    for ik in range(8):
        for a in range(2):
            P.I("pool", "affine_select", out=TRI.v(TRI.t[:, ik, a, :].rearrange("p (i k) -> p i k", i=4)),
                in_=zt.v(zt.t[:, :].rearrange("p (i k) -> p i k", i=4)), pattern=[[1, 4], [8, 128]], base=4 * a - ik,
                channel_multiplier=-8, compare_op=ALU.is_ge, fill=NEG)
    ones1 = P.sbuf([128, 64], F32, "ones1")
    _memset(P, "pool", ones1[:], 1.0)
    wq = [P.sbuf([128, 8, 64], BF16, "wq%d" % i) for i in range(2)]
    wo = [P.sbuf([128, 1024], BF16, "wo%d" % i) for i in range(2)]
    for i in range(2):
        _memset(P, "pool", wo[i][:], 0.0)
    tmpf = [P.sbuf([128, 512], F32, "tmpf%d" % i) for i in range(4)]
    pT = [P.sbuf([128, 512], BF16, "pT%d" % i) for i in range(8)]
    osb = [P.sbuf([128, 512], F32, "osb%d" % i) for i in range(2)]
    rc = [P.sbuf([128, 512], F32, "rc%d" % i) for i in range(2)]
    otn = [P.sbuf([128, 512], BF16, "otn%d" % i) for i in range(2)]
    for i in range(2):
        _memset(P, "pool", otn[i][:], 0.0)
    wqv = D.attn_wq.t[j].rearrange("(c p) n -> p c n", p=128)
    vgv = A.Vg.t.rearrange("(kt p) (h d) -> p kt h d", p=128, h=16)
    def prep(h):
        ka, va, qa = KA[h % 2], VA[h % 2], QA[h % 2]
        P.dma("sp", ka[0:64, :], V(A.KTg, A.KTg.t[h]))
        P.dma("sp", ka[67:70, :], V(A.Hd, A.Hd.t[h]))
        for q4 in range(4):
            P.dma("act", va[:, q4 * 16:(q4 + 1) * 16, 0:64], V(A.Vg, vgv[:, q4 * 16:(q4 + 1) * 16, h, :]))
        P.dma("sp", qa[64:67, :], V(A.Gd, A.Gd.t[h]))
        P.dma("pool", wq[h % 2][:], V(D.attn_wq, wqv[:, :, h * 64:(h + 1) * 64]))
        P.dma("pool", wo[h % 2][0:64, :], V(D.attn_wo, D.attn_wo.t[j, h * 64:(h + 1) * 64, :]))
        for tg in range(4):
            ps = ps2[tg % 2]
            for c in range(8):
                P.I("pe", "matmul", out=ps[0:64, :], lhsT=wq[h % 2][:, c, :], rhs=xT[:, c, tg * 512:(tg + 1) * 512],
                    start=(c == 0), stop=(c == 7), xr=xTr[tg * 4:tg * 4 + 4])
            P.I("act", "activation", out=qa[0:64, tg * 512:(tg + 1) * 512], in_=ps[0:64, :], func=AF.Copy, scale=0.125)

    def score(h, qg, kt, idx):
        ka, qa = KA[h % 2], QA[h % 2]
        b, a = qg // 2, qg % 2
        kb, ik = kt // 8, kt % 8
        ps, tf, pt_ = pss[idx % 4], tmpf[idx % 4], pT[idx % 8]
        P.I("pe", "matmul", out=ps[:], lhsT=ka[:, kt * 128:(kt + 1) * 128], rhs=qa[:, qg * 512:(qg + 1) * 512],
            start=True, stop=True)
        ci = kb * 2 + b
        if kb % 2 == b:
            P.I("dve", "scalar_tensor_tensor", out=tf[:], in0=TRI[:, ik, a, :], scalar=AB[:, ci:ci + 1], in1=ps[:],
                op0=ALU.mult, op1=ALU.add)
            P.I("act", "activation", out=pt_[:], in_=tf[:], func=AF.Exp, bias=AB[:, 16 + ci:17 + ci], scale=1.0)
        else:
            P.I("act", "activation", out=pt_[:], in_=ps[:], func=AF.Exp, bias=AB[:, 16 + ci:17 + ci], scale=1.0)

    def pv(h, qg, kt, idx):
        P.I("pe", "matmul", out=ppo[qg % 2][0:65, :], lhsT=VA[h % 2][:, kt, :], rhs=pT[idx % 8][:], start=(kt == 0),
            stop=(kt == 63))

    def ep1(h, qg):
        P.I("act", "activation", out=osb[qg % 2][0:65, :], in_=ppo[qg % 2][0:65, :], func=AF.Copy)
        P.I("dve", "reciprocal", out=rc[qg % 2][64:65, :], in_=osb[qg % 2][64:65, :])

    def ep2(h, qg):
        P.I("pe", "matmul", out=psb[0:64, :], lhsT=ones1[64:65, 0:64], rhs=rc[qg % 2][64:65, :], start=True, stop=True)
        P.I("dve", "tensor_tensor", out=otn[qg % 2][0:64, :], in0=osb[qg % 2][0:64, :], in1=psb[0:64, :], op=ALU.mult)

    def ep3(h, qg):
        on = otn[qg % 2]
        for t4 in range(4):
            tt = qg * 4 + t4
            bb, ii = tt // 8, tt % 8
            for half in range(2):
                p2 = ps2[(t4 * 2 + half) % 2]
                P.I("pe", "matmul", out=p2[:], lhsT=on[:, t4 * 128:(t4 + 1) * 128],
                    rhs=wo[h % 2][:, half * 512:(half + 1) * 512], start=True, stop=True)
                hv = V(C.h[bb][ii], C.hb[bb].t[:, ii, half * 512:(half + 1) * 512])
                P.I("dve", "tensor_tensor", out=hv, in0=hv, in1=p2[:], op=ALU.add)

    units = [(h, qg, kt) for h in range(16) for qg in range(4) for kt in range(64)]
    n = len(units)
    DEPTH = 6
    events = {}

    def at(i_, fn):
        events.setdefault(i_, []).append(fn)
    prep(0)
    for idx in range(n + DEPTH + 8):
        for fn in events.pop(idx, []):
            fn()
        if idx < n:
            h, qg, kt = units[idx]
            if qg == 0 and kt == 0 and h + 1 < 16:
                at(idx + 96, lambda h=h: prep(h + 1))
            score(h, qg, kt, idx)
        jx = idx - DEPTH
        if 0 <= jx < n:
            h, qg, kt = units[jx]
            pv(h, qg, kt, jx)
            if kt == 63:
                ep1(h, qg)
                at(idx + 2, lambda h=h, qg=qg: ep2(h, qg))
                at(idx + 4, lambda h=h, qg=qg: ep3(h, qg))
    assert not events
    P.pop()


def final_norm(P, C, D, yout):
    P.push()
    gain = load_bcast(P, D.final_norm.t[:], D.final_norm, 1024, "fg")
    ss = P.sbuf([128, 16], F32, "fss")
    rs = P.sbuf([128, 16], F32, "frs")
    junk = P.sbuf([128, 1024], F32, "fjunk")
    ob = [P.sbuf([128, 1024], F32, "fob%d" % j) for j in range(2)]
    for b in range(2):
        for i in range(8):
            jx = b * 8 + i
            P.I("act", "activation", out=junk[:], in_=V(C.h[b][i], C.hb[b].t[:, i, :]), func=AF.Square,
                accum_out=ss[:, jx:jx + 1])
    P.I("dve", "tensor_scalar", out=rs[:], in0=ss[:], scalar1=1.0 / 1024, scalar2=1e-6, op0=ALU.mult, op1=ALU.add)
    P.I("act", "activation", out=rs[:], in_=rs[:], func=AF.Sqrt)
    P.I("dve", "reciprocal", out=rs[:], in_=rs[:])
    yv = yout.t.rearrange("(b k i) d -> b k i d", b=2, k=128)
    for b in range(2):
        for i in range(8):
            jx = b * 8 + i
            o = ob[jx % 2]
            P.I("dve", "scalar_tensor_tensor", out=o[:], in0=V(C.h[b][i], C.hb[b].t[:, i, :]), scalar=rs[:, jx:jx + 1],
                in1=gain[:], op0=ALU.mult, op1=ALU.mult)
            P.dma("sp", V(yout, yv[b, :, i, :]), o[:])
    P.pop()


PARAMS = [("mix_norm", [4, 1024]), ("mlp_norm", [4, 1024]), ("mlp_w1", [4, 1024, 4096]), ("mlp_w2", [4, 4096, 1024]),
          ("ssm_log_dt", [2, 64]), ("ssm_a_re", [2, 64, 64]), ("ssm_a_im", [2, 64, 64]),
          ("ssm_b_re", [2, 64, 64, 16]), ("ssm_b_im", [2, 64, 64, 16]), ("ssm_c_re", [2, 64, 16, 64]),
          ("ssm_c_im", [2, 64, 16, 64]), ("ssm_d", [2, 1024]), ("ssm_w_glu", [2, 1024, 2048]), ("kv_norm", [1024]),
          ("w_kvf", [1024, 2064]), ("b_f", [16]), ("attn_wq", [2, 1024, 1024]), ("attn_wo", [2, 1024, 1024]),
          ("final_norm", [1024])]
S5_NAMES = ["mix_norm", "mlp_norm", "mlp_w1", "mlp_w2", "ssm_log_dt", "ssm_a_re", "ssm_a_im", "ssm_b_re",
            "ssm_b_im", "ssm_c_re", "ssm_c_im", "ssm_d", "ssm_w_glu"]


def declare(P, names):
    D = Ctx()
    for (n, shp) in PARAMS:
        if n in names:
            setattr(D, n, P.dram(n, shp, F32, "ExternalInput"))
    return D


def build_s5(stage):
    nc = bass.Bass("TRN2", target_bir_lowering=False)
    with ExitStack() as st:
        P = Prog(nc, st)
        C = Ctx()
        names = list(S5_NAMES) + (["kv_norm", "w_kvf", "b_f"] if stage == 3 else [])
        if stage == 1:
            names = [n for n in names if n not in ("mlp_norm", "mlp_w1", "mlp_w2", "ssm_w_glu")]
        D = declare(P, names)
        hin = P.dram("hin", [2048, 1024], F32, "ExternalInput")
        make_consts(P, C)
        load_h(P, C, hin)
        xT = P.sbuf([128, 8, 2048], BF16, "xT")
        xTr = [xT.sub("xTr%d" % j) for j in range(16)]
        if stage == 1:
            Fout = P.dram("Fout", [128, 64], F32, "ExternalOutput")
            lA = 0
        else:
            Fall = P.dram("Fall", [8, 128, 64], F32, "ExternalInput")
            selS = P.dram("selS", [128, 24], F32, "ExternalInput")
            hout = P.dram("hout", [2048, 1024], F32, "ExternalOutput")
            lB = stage - 2
            if stage == 2:
                Fout = P.dram("Fout", [128, 64], F32, "ExternalOutput")
                lA = 1
            else:
                KTo = P.dram("KTo", [16, 64, 2048], BF16, "ExternalOutput")
                Vo = P.dram("Vo", [2048, 1024], BF16, "ExternalOutput")
                Flo = P.dram("Flo", [16, 2048], F32, "ExternalOutput")
                Fto = P.dram("Fto", [16, 1], F32, "ExternalOutput")

        need = {1: [0], 2: [0, 1], 3: [1]}[stage]
        SP_ = {l: s5_params(P, C, l, D) for l in need}

        def stageA(l):
            P.push()
            pst = [P.psum([128, 8, 128], BF16, "pst%d" % j) for j in range(2)]
            gain = load_bcast(P, D.mix_norm.t[l, :], D.mix_norm, 1024, "g")
            rmsnorm_T(P, C, gain, xT, xTr, pst)
            P.pop()
            return SP_[l]

        if stage >= 2:
            S = stageA(lB)
            P.push()
            s5_core(P, C, lB, D, S, xT, xTr, "B", Fall=Fall, selS=selS)
            P.pop()
            P.push()
            pm = [P.psum([128, 512], F32, "pm%d" % j) for j in range(4)]
            glu(P, C, lB, D, xT, xTr, pm)
            P.pop()
            P.push()
            pst = [P.psum([128, 8, 128], BF16, "pst%d" % j) for j in range(2)]
            pm = [P.psum([128, 512], F32, "pm%d" % j) for j in range(4)]
            mlp(P, C, lB, D, xT, xTr, pst, pm)
            P.pop()
            store_h(P, C, hout)
        if stage <= 2:
            S = stageA(lA)
            P.push()
            s5_core(P, C, lA, D, S, xT, xTr, "A", Fout=Fout)
            P.pop()
        else:
            kv_stage(P, C, D, xT, xTr, KTo, Vo, Flo, Fto)
        P.finish()
    return nc, names


def build_att():
    nc = bass.Bass("TRN2", target_bir_lowering=False)
    with ExitStack() as st:
        P = Prog(nc, st)
        C = Ctx()
        names = ["mix_norm", "mlp_norm", "mlp_w1", "mlp_w2", "attn_wq", "attn_wo", "final_norm"]
        D = declare(P, names)
        A = Ctx()
        hin = P.dram("hin", [2048, 1024], F32, "ExternalInput")
        A.KTg = P.dram("KTg", [16, 64, 8192], BF16, "ExternalInput")
        A.Vg = P.dram("Vg", [8192, 1024], BF16, "ExternalInput")
        A.Flg = P.dram("Flg", [4, 16, 2048], F32, "ExternalInput")
        A.Ftg = P.dram("Ftg", [16, 4], F32, "ExternalInput")
        A.Fown = P.dram("Fown", [16, 2048], F32, "ExternalInput")
        A.selq = P.dram("selq", [16, 4], F32, "ExternalInput")
        A.AB = P.dram("ABsel", [128, 32], F32, "ExternalInput")
        A.Hd = P.dram("Hd", [16, 3, 8192], BF16, "Internal")
        A.Gd = P.dram("Gd", [16, 3, 2048], BF16, "Internal")
        yout = P.dram("yout", [2048, 1024], F32, "ExternalOutput")
        make_consts(P, C)
        load_h(P, C, hin)
        xT = P.sbuf([128, 8, 2048], BF16, "xT")
        xTr = [xT.sub("xTr%d" % j) for j in range(16)]
        att_prep(P, C, A)
        for j in range(2):
            l = 2 + j
            P.push()
            pst = [P.psum([128, 8, 128], BF16, "pst%d" % k) for k in range(2)]
            gain = load_bcast(P, D.mix_norm.t[l, :], D.mix_norm, 1024, "g")
            rmsnorm_T(P, C, gain, xT, xTr, pst)
            P.pop()
            attention(P, C, j, D, A, xT, xTr)
            P.push()
            pst = [P.psum([128, 8, 128], BF16, "pst%d" % k) for k in range(2)]
            pm = [P.psum([128, 512], F32, "pm%d" % k) for k in range(4)]
            mlp(P, C, l, D, xT, xTr, pst, pm)
            P.pop()
        final_norm(P, C, D, yout)
        P.finish()
    return nc, names


def build_fused(debug=False):
    nc = bass.Bass("TRN2", target_bir_lowering=False)
    with ExitStack() as st:
        P = Prog(nc, st)
        C = Ctx()
        names = [n for n, _ in PARAMS]
        D = declare(P, names)
        xfull = P.dram("xfull", [8192, 1024], F32, "ExternalInput")
        selseg = P.dram("selseg", [128, 4], F32, "ExternalInput")
        A = Ctx()
        A.selq = P.dram("selq", [16, 4], F32, "ExternalInput")
        A.AB = P.dram("ABsel", [128, 32], F32, "ExternalInput")
        A.KTg = P.dram("KTd", [16, 64, 8192], BF16, "Internal")
        A.Vg = P.dram("Vd", [8192, 1024], BF16, "Internal")
        dk = "ExternalOutput" if debug else "Internal"
        A.Flg = P.dram("Fld", [4, 16, 2048], F32, dk)
        A.Ftg = P.dram("Ftd", [16, 4], F32, dk)
        A.Fown = P.dram("Fownd", [16, 2048], F32, dk)
        A.Hd = P.dram("Hd", [16, 3, 8192], BF16, "Internal")
        A.Gd = P.dram("Gd", [16, 3, 2048], BF16, "Internal")
        H2d = P.dram("H2d", [4, 2048, 1024], F32, dk)
        TabW = [P.dram("TabW%d" % l, [8, 128, 2048], BF16, "Internal") for l in range(2)]
        TabC = [P.dram("TabC%d" % l, [8, 128, 2304], BF16, "Internal") for l in range(2)]
        TabK = [P.dram("TabK%d" % l, [8, 128, 256], BF16, "Internal") for l in range(2)]
        yout = P.dram("yout", [2048, 1024], F32, "ExternalOutput")
        make_consts(P, C)
        xT = P.sbuf([128, 8, 2048], BF16, "xT")
        xTr = [xT.sub("xTr%d" % j) for j in range(16)]
        C.hb = [P.sbuf([128, 8, 1024], F32, "h%d" % b) for b in range(2)]
        C.h = [[C.hb[b].sub("h%d_%d" % (b, i)) for i in range(8)] for b in range(2)]
        P.push()
        SPr_ = {l: s5_params(P, C, l, D) for l in range(2)}
        for l in range(2):
            s5_tables(P, C, SPr_[l], TabW[l], TabC[l], TabK[l])
        carry = [P.sbuf([128, 32, 2], F32, "carry%d" % l) for l in range(2)]
        for l in range(2):
            _memset(P, "pool", carry[l][:], 0.0)
        for seg in range(4):
            xs = Buf(xfull.t[seg * 2048:(seg + 1) * 2048, :], "xseg%d" % seg)
            load_h(P, C, xs)
            for l in range(2):
                P.push()
                pst = [P.psum([128, 8, 128], BF16, "pst%d" % j) for j in range(2)]
                gain = load_bcast(P, D.mix_norm.t[l, :], D.mix_norm, 1024, "g")
                rmsnorm_T(P, C, gain, xT, xTr, pst)
                P.pop()
                P.push()
                s5_core_pipe(P, C, l, D, SPr_[l], xT, carry[l], TabW[l], TabC[l], TabK[l])
                P.pop()
                P.push()
                pm = [P.psum([128, 512], F32, "pm%d" % j) for j in range(4)]
                glu(P, C, l, D, xT, xTr, pm)
                P.pop()
                P.push()
                pst = [P.psum([128, 8, 128], BF16, "pst%d" % j) for j in range(2)]
                pm = [P.psum([128, 512], F32, "pm%d" % j) for j in range(4)]
                mlp(P, C, l, D, xT, xTr, pst, pm)
                P.pop()
            kv_stage(P, C, D, xT, xTr,
                     Buf(A.KTg.t[:, :, seg * 2048:(seg + 1) * 2048], "ktseg"), Buf(A.Vg.t[seg * 2048:(seg + 1) * 2048, :], "vseg"),
                     Buf(A.Flg.t[seg], "flseg"), Buf(A.Ftg.t[:, seg:seg + 1], "ftseg"))
            store_h(P, C, Buf(H2d.t[seg], "h2seg"))
            P.barrier()
        P.pop()
        P.push()
        sel = P.sbuf([128, 4], F32, "sel")
        P.dma("sp", sel[:], selseg[:])
        tmp = [P.sbuf([128, 1024], F32, "seltmp%d" % j) for j in range(3)]
        n = 0
        for b in range(2):
            for i in range(8):
                hv = V(C.h[b][i], C.hb[b].t[:, i, :])
                for seg in range(4):
                    t = tmp[n % 3]
                    n += 1
                    sv = H2d.t[seg].rearrange("(b k i) d -> b k i d", b=2, k=128)[b, :, i, :]
                    P.dma(("sp", "act")[n % 2], t[:], V(H2d, sv))
                    if seg == 0:
                        P.I("dve", "tensor_scalar", out=hv, in0=t[:], scalar1=sel[:, 0:1], scalar2=None, op0=ALU.mult)
                    else:
                        P.I("dve", "scalar_tensor_tensor", out=hv, in0=t[:], scalar=sel[:, seg:seg + 1], in1=hv,
                            op0=ALU.mult, op1=ALU.add)
        facc = P.sbuf([16, 2048], F32, "facc")
        ftmp = [P.sbuf([16, 2048], F32, "ftmp%d" % j) for j in range(2)]
        for seg in range(4):
            t = ftmp[seg % 2]
            P.dma("sp", t[:], V(A.Flg, A.Flg.t[seg]))
            if seg == 0:
                P.I("dve", "tensor_scalar", out=facc[:], in0=t[:], scalar1=sel[0:16, 0:1], scalar2=None, op0=ALU.mult)
            else:
                P.I("dve", "scalar_tensor_tensor", out=facc[:], in0=t[:], scalar=sel[0:16, seg:seg + 1], in1=facc[:],
                    op0=ALU.mult, op1=ALU.add)
        P.dma("sp", A.Fown[:], facc[:])
        P.pop()
        att_prep(P, C, A)
        for j in range(2):
            l = 2 + j
            P.push()
            pst = [P.psum([128, 8, 128], BF16, "pst%d" % k) for k in range(2)]
            gain = load_bcast(P, D.mix_norm.t[l, :], D.mix_norm, 1024, "g")
            rmsnorm_T(P, C, gain, xT, xTr, pst)
            P.pop()
            attention(P, C, j, D, A, xT, xTr)
            P.push()
            pst = [P.psum([128, 8, 128], BF16, "pst%d" % k) for k in range(2)]
            pm = [P.psum([128, 512], F32, "pm%d" % k) for k in range(4)]
            mlp(P, C, l, D, xT, xTr, pst, pm)
            P.pop()
        final_norm(P, C, D, yout)
        P.finish()
    return nc, names


def sel_state(core):
    s = np.zeros((8, 3), np.float32)
    b, q = core // 4, core % 4
    for qq in range(q):
        s[4 * b + qq, q - 1 - qq] = 1.0
    return np.tile(s.reshape(1, 24), (128, 1))


def sel_att(core):
    q = core % 4
    selq = np.zeros((16, 4), np.float32)
    selq[:, :q] = 1.0
    ab = np.zeros((32,), np.float32)
    for kb in range(8):
        for qb in range(2):
            g = 2 * q + qb
            ab[kb * 2 + qb] = 1.0 if kb == g else 0.0
            ab[16 + kb * 2 + qb] = NEG if kb > g else 0.0
    return selq, np.tile(ab.reshape(1, 32), (128, 1))


_CACHE = {}


def get_prog(key, fn):
    if key not in _CACHE:
        _CACHE[key] = fn()
    return _CACHE[key]


def run(nc, in_maps):
    res = run_bass_kernel_spmd(nc, in_maps, core_ids=list(range(8)))
    return res.results


def kernel_unfused(**inputs):
    inp = {k: np.ascontiguousarray(np.asarray(v, dtype=np.float32)) for k, v in inputs.items()}
    x = inp["x"].reshape(8, 2048, 1024)
    nc1, n1 = get_prog("s1", lambda: build_s5(1))
    r1 = run(nc1, [dict({n: inp[n] for n in n1}, hin=x[c]) for c in range(8)])
    F0 = np.stack([r["Fout"] for r in r1])
    nc2, n2 = get_prog("s2", lambda: build_s5(2))
    r2 = run(nc2, [dict({n: inp[n] for n in n2}, hin=x[c], Fall=F0, selS=sel_state(c)) for c in range(8)])
    F1 = np.stack([r["Fout"] for r in r2])
    h1 = [r["hout"] for r in r2]
    nc3, n3 = get_prog("s3", lambda: build_s5(3))
    r3 = run(nc3, [dict({n: inp[n] for n in n3}, hin=h1[c], Fall=F1, selS=sel_state(c)) for c in range(8)])
    nc4, n4 = get_prog("s4", build_att2)
    r4 = run(nc4, [dict({n: inp[n] for n in n4}, **att2_inputs(c, r3)) for c in range(8)])
    y = np.zeros((2, 8, 1024, 1024), np.float32)
    for c in range(8):
        bt, q = c // 4, c % 4
        y[bt, q] = r4[c]["yout"][:1024]
        y[bt, 7 - q] = r4[c]["yout"][1024:]
    y = y.reshape(2, 8192, 1024)
    return y


NSLOT = 12


def att_prep2(P, C, A):
    P.push()
    ft = P.sbuf([16, 4], F32, "ft")
    off = P.sbuf([16, 4], F32, "off")
    sk = P.sbuf([16, NSLOT * 4], F32, "sk")
    sq = P.sbuf([16, 8], F32, "sq")
    offs = P.sbuf([16, NSLOT + 2], F32, "offs")
    P.dma("sp", ft[:], A.Ftg[:])
    P.dma("sp", sk[:], A.selk[:])
    P.dma("sp", sq[:], A.selq[:])
    _memset(P, "pool", off[:], 0.0)
    for j in range(1, 4):
        P.I("dve", "tensor_tensor", out=off[:, j:j + 1], in0=off[:, j - 1:j], in1=ft[:, j - 1:j], op=ALU.add)
    for s_ in range(NSLOT + 2):
        src = sk[:, s_ * 4:(s_ + 1) * 4] if s_ < NSLOT else sq[:, (s_ - NSLOT) * 4:(s_ - NSLOT + 1) * 4]
        P.I("dve", "tensor_tensor", out=src, in0=src, in1=ft[:], op=ALU.mult)
        P.I("dve", "tensor_reduce", out=offs[:, s_:s_ + 1], in_=src, axis=mybir.AxisListType.X, op=ALU.add)
    fl = [P.sbuf([16, 1024], F32, "fl%d" % j) for j in range(2)]
    tmp = (P.sbuf([16, 1024], BF16, "s_hi"), P.sbuf([16, 1024], F32, "s_r1"), P.sbuf([16, 1024], BF16, "s_mid"),
           P.sbuf([16, 1024], BF16, "s_lo"))
    for s_ in range(NSLOT):
        f = fl[s_ % 2]
        P.dma("act", f[:], V(A.Fkg, A.Fkg.t[s_]))
        P.I("dve", "tensor_scalar", out=f[:], in0=f[:], scalar1=offs[:, s_:s_ + 1], scalar2=-1.0, op0=ALU.add, op1=ALU.mult)
        split3(P, f[:], [V(A.Hd, A.Hd.t[:, r, s_ * 1024:(s_ + 1) * 1024]) for r in range(3)], tmp)
    for lb in range(2):
        f = fl[lb % 2]
        P.dma("act", f[:], V(A.Fq, A.Fq.t[:, lb * 1024:(lb + 1) * 1024]))
        P.I("dve", "tensor_scalar", out=f[:], in0=f[:], scalar1=offs[:, NSLOT + lb:NSLOT + lb + 1], scalar2=None,
            op0=ALU.add)
        split3(P, f[:], [V(A.Gd, A.Gd.t[:, r, lb * 1024:(lb + 1) * 1024]) for r in range(3)], tmp)
    P.pop()


def attention2(P, C, j, D, A, xT, xTr):
    P.push()
    NK = NSLOT * 1024
    pss = [P.psum([128, 2, 512], F32, "pss%d" % i) for i in range(2)]
    ppo = [P.psum([128, 512], F32, "ppo%d" % i) for i in range(2)]
    pmi = [P.psum([128, 512], F32, "pmi%d" % i) for i in range(2)]
    BM = P.sbuf([128, NSLOT], F32, "BM")
    P.dma("sp", BM[:], A.Bm[:])
    KA = [P.sbuf([70, 8192], BF16, "KA%d" % i) for i in range(2)]
    VA = [P.sbuf([128, 64, 65], BF16, "VA%d" % i) for i in range(2)]
    QA = [P.sbuf([70, 2048], BF16, "QA%d" % i) for i in range(2)]
    for i in range(2):
        _memset(P, "pool", KA[i][64:70, :], 1.0)
        _memset(P, "pool", QA[i][64:70, :], 1.0)
        _memset(P, "pool", VA[i][:, :, 64:65], 1.0)
    zt = P.sbuf([128, 512], BF16, "zt")
    _memset(P, "pool", zt[:], 0.0)
    TRI = P.sbuf([128, 2, 8, 512], BF16, "TRI")
    for ik in range(8):
        for a in range(2):
            P.I("pool", "affine_select", out=TRI.v(TRI.t[:, a, ik, :].rearrange("p (i k) -> p i k", i=4)),
                in_=zt.v(zt.t[:, :].rearrange("p (i k) -> p i k", i=4)), pattern=[[1, 4], [8, 128]], base=4 * a - ik,
                channel_multiplier=-8, compare_op=ALU.is_ge, fill=NEG)
    ones1 = P.sbuf([128, 64], F32, "ones1")
    _memset(P, "pool", ones1[:], 1.0)
    wq = [P.sbuf([128, 8, 64], BF16, "wq%d" % i) for i in range(2)]
    wo = [P.sbuf([64, 1024], BF16, "wo%d" % i) for i in range(2)]
    tmpf = [P.sbuf([128, 2, 512], F32, "tmpf0")] * 2
    NPT = 4
    pT = [P.sbuf([128, 2, 512], BF16, "pT%d" % i) for i in range(NPT)]
    osb = [P.sbuf([128, 512], F32, "osb%d" % i) for i in range(2)]
    rc = [P.sbuf([128, 512], F32, "rc0")] * 2
    otn = [P.sbuf([64, 512], BF16, "otn%d" % i) for i in range(2)]
    wqv = D.attn_wq.t[j].rearrange("(c p) n -> p c n", p=128)
    vgv = A.Vg.t.rearrange("(kt p) (h d) -> p kt h d", p=128, h=16)

    def prep(p):
        h, lb = p // 2, p % 2
        ka, va = KA[p % 2], VA[p % 2]
        s0, ns = (0, 4) if lb == 0 else (4, 8)
        P.dma("sp", ka[0:64, 0:ns * 1024], V(A.KTg, A.KTg.t[h, :, s0 * 1024:(s0 + ns) * 1024]))
        P.dma("sp", ka[67:70, 0:ns * 1024], V(A.Hd, A.Hd.t[h, :, s0 * 1024:(s0 + ns) * 1024]))
        for q4 in range(ns // 2):
            P.dma("act", va[:, q4 * 16:(q4 + 1) * 16, 0:64], V(A.Vg, vgv[:, s0 * 8 + q4 * 16:s0 * 8 + (q4 + 1) * 16, h, :]))
        if lb == 1:
            return
        qa = QA[h % 2]
        P.dma("sp", qa[64:67, :], V(A.Gd, A.Gd.t[h]))
        P.dma("pool", wq[h % 2][:], V(D.attn_wq, wqv[:, :, h * 64:(h + 1) * 64]))
        P.dma("pool", wo[h % 2][:], V(D.attn_wo, D.attn_wo.t[j, h * 64:(h + 1) * 64, :]))
        for tg in range(4):
            ps = pmi[tg % 2]
            for c in range(8):
                P.I("pe", "matmul", out=ps[0:64, :], lhsT=wq[h % 2][:, c, :], rhs=xT[:, c, tg * 512:(tg + 1) * 512],
                    start=(c == 0), stop=(c == 7), xr=xTr[tg * 4:tg * 4 + 4])
            P.I("act", "activation", out=qa[0:64, tg * 512:(tg + 1) * 512], in_=ps[0:64, :], func=AF.Copy, scale=0.125)

    units = []
    for h in range(16):
        for lb in range(2):
            slots = list(range(0, 4)) if lb == 0 else list(range(4, 12))
            for a in range(2):
                qg = lb * 2 + a
                lst = [(s_, ikp) for s_ in slots for ikp in range(4)]
                for n_, (s_, ikp) in enumerate(lst):
                    units.append((h, qg, s_, ikp, n_ == 0, n_ == len(lst) - 1))

    def score(u, idx):
        h, qg, s_, ikp, first, last = u
        lb = qg // 2
        ka, qa = KA[(h * 2 + lb) % 2], QA[h % 2]
        a = qg % 2
        ps, pt_ = pss[idx % 2], pT[idx % NPT]
        for e in range(2):
            kt = (s_ - 4 * lb) * 8 + ikp * 2 + e
            P.I("pe", "matmul", out=ps[:, e, :], lhsT=ka[0:70, kt * 128:(kt + 1) * 128],
                rhs=qa[0:70, qg * 512:(qg + 1) * 512], start=True, stop=True)
        if s_ in (0, 4):
            tf = tmpf[idx % 2]
            P.I("dve", "tensor_tensor", out=tf[:], in0=ps[:], in1=TRI[:, a, ikp * 2:ikp * 2 + 2, :], op=ALU.add)
            P.I("act", "activation", out=pt_[:], in_=tf[:], func=AF.Exp, bias=BM[:, s_:s_ + 1], scale=1.0)
        else:
            P.I("act", "activation", out=pt_[:], in_=ps[:], func=AF.Exp, bias=BM[:, s_:s_ + 1], scale=1.0)

    def pv(u, idx):
        h, qg, s_, ikp, first, last = u
        lb = qg // 2
        for e in range(2):
            kt = (s_ - 4 * lb) * 8 + ikp * 2 + e
            P.I("pe", "matmul", out=ppo[qg % 2][0:65, :], lhsT=VA[(h * 2 + lb) % 2][:, kt, :], rhs=pT[idx % NPT][:, e, :],
                start=(first and e == 0), stop=(last and e == 1))

    def ep1(h, qg):
        P.I("act", "activation", out=osb[qg % 2][0:65, :], in_=ppo[qg % 2][0:65, :], func=AF.Copy)
        P.I("dve", "reciprocal", out=rc[qg % 2][64:65, :], in_=osb[qg % 2][64:65, :])

    def ep2(h, qg):
        P.I("pe", "matmul", out=pmi[0][0:64, :], lhsT=ones1[64:65, 0:64], rhs=rc[qg % 2][64:65, :], start=True, stop=True)
        P.I("dve", "tensor_tensor", out=otn[qg % 2][:], in0=osb[qg % 2][0:64, :], in1=pmi[0][0:64, :], op=ALU.mult)

    def ep3(h, qg):
        on = otn[qg % 2]
        for t4 in range(4):
            tt = qg * 4 + t4
            bb, ii = tt // 8, tt % 8
            for half in range(2):
                p2 = pmi[(t4 * 2 + half) % 2]
                P.I("pe", "matmul", out=p2[:], lhsT=on[:, t4 * 128:(t4 + 1) * 128],
                    rhs=wo[h % 2][:, half * 512:(half + 1) * 512], start=True, stop=True)
                hv = V(C.h[bb][ii], C.hb[bb].t[:, ii, half * 512:(half + 1) * 512])
                P.I("dve", "tensor_tensor", out=hv, in0=hv, in1=p2[:], op=ALU.add)

    n = len(units)
    DEPTH = 3
    events = {}

    def at(i_, fn):
        events.setdefault(i_, []).append(fn)
    prep(0)
    prev_p = -1
    for idx in range(n + DEPTH + 8):
        for fn in events.pop(idx, []):
            fn()
        if idx < n:
            u = units[idx]
            p = u[0] * 2 + u[1] // 2
            if p != prev_p:
                prev_p = p
                if p + 1 < 32:
                    at(idx + 10, lambda p=p: prep(p + 1))
            score(u, idx)
        jx = idx - DEPTH
        if 0 <= jx < n:
            u = units[jx]
            pv(u, jx)
            if u[5]:
                ep1(u[0], u[1])
                at(idx + 2, lambda h=u[0], qg=u[1]: ep2(h, qg))
                at(idx + 4, lambda h=u[0], qg=u[1]: ep3(h, qg))
    assert not events
    P.pop()


def build_att2():
    nc = bass.Bass("TRN2", target_bir_lowering=False)
    with ExitStack() as st:
        P = Prog(nc, st)
        C = Ctx()
        names = ["mix_norm", "mlp_norm", "mlp_w1", "mlp_w2", "attn_wq", "attn_wo", "final_norm"]
        D = declare(P, names)
        A = Ctx()
        NK = NSLOT * 1024
        hin = P.dram("hin", [2048, 1024], F32, "ExternalInput")
        A.KTg = P.dram("KTg", [16, 64, NK], BF16, "ExternalInput")
        A.Vg = P.dram("Vg", [NK, 1024], BF16, "ExternalInput")
        A.Fkg = P.dram("Fkg", [NSLOT, 16, 1024], F32, "ExternalInput")
        A.Ftg = P.dram("Ftg", [16, 4], F32, "ExternalInput")
        A.Fq = P.dram("Fq", [16, 2048], F32, "ExternalInput")
        A.selk = P.dram("selk", [16, NSLOT * 4], F32, "ExternalInput")
        A.selq = P.dram("selq", [16, 8], F32, "ExternalInput")
        A.Bm = P.dram("Bm", [128, NSLOT], F32, "ExternalInput")
        A.Hd = P.dram("Hd", [16, 3, NK], BF16, "Internal")
        A.Gd = P.dram("Gd", [16, 3, 2048], BF16, "Internal")
        yout = P.dram("yout", [2048, 1024], F32, "ExternalOutput")
        make_consts(P, C)
        load_h(P, C, hin)
        xT = P.sbuf([128, 8, 2048], BF16, "xT")
        xTr = [xT.sub("xTr%d" % j) for j in range(16)]
        att_prep2(P, C, A)
        for j in range(2):
            l = 2 + j
            P.push()
            pst = [P.psum([128, 8, 128], BF16, "pst%d" % k) for k in range(2)]
            gain = load_bcast(P, D.mix_norm.t[l, :], D.mix_norm, 1024, "g")
            rmsnorm_T(P, C, gain, xT, xTr, pst)
            P.pop()
            attention2(P, C, j, D, A, xT, xTr)
            P.push()
            pst = [P.psum([128, 8, 128], BF16, "pst%d" % k) for k in range(2)]
            pm = [P.psum([128, 512], F32, "pm%d" % k) for k in range(4)]
            mlp(P, C, l, D, xT, xTr, pst, pm)
            P.pop()
        final_norm(P, C, D, yout)
        P.finish()
    return nc, names


def att2_inputs(c, r3):
    bt, q = c // 4, c % 4
    own = [q, 7 - q]

    def src(g):
        return r3[4 * bt + g // 2], slice((g % 2) * 1024, (g % 2 + 1) * 1024)
    slots = [own[0] - r for r in range(4)] + [own[1] - r for r in range(8)]
    kt, vg, fk = [], [], []
    selk = np.zeros((NSLOT, 4), np.float32)
    bm = np.zeros((NSLOT,), np.float32)
    for s_, g in enumerate(slots):
        if g < 0:
            kt.append(np.zeros((16, 64, 1024), ml_dtypes.bfloat16))
            vg.append(np.zeros((1024, 1024), ml_dtypes.bfloat16))
            fk.append(np.zeros((16, 1024), np.float32))
            bm[s_] = NEG
        else:
            r, sl = src(g)
            kt.append(r["KTo"][:, :, sl])
            vg.append(r["Vo"][sl, :])
            fk.append(r["Flo"][:, sl])
            selk[s_, :g // 2] = 1.0
    selq = np.zeros((2, 4), np.float32)
    hin, fq = [], []
    for lb, g in enumerate(own):
        r, sl = src(g)
        hin.append(r["hout"][sl, :])
        fq.append(r["Flo"][:, sl])
        selq[lb, :g // 2] = 1.0
    grp = [r3[4 * bt + k] for k in range(4)]
    return dict(hin=np.ascontiguousarray(np.concatenate(hin, 0)),
                KTg=np.ascontiguousarray(np.concatenate(kt, 2)), Vg=np.ascontiguousarray(np.concatenate(vg, 0)),
                Fkg=np.ascontiguousarray(np.stack(fk, 0)), Ftg=np.ascontiguousarray(np.concatenate([g["Fto"] for g in grp], 1)),
                Fq=np.ascontiguousarray(np.concatenate(fq, 1)), selk=np.tile(selk.reshape(1, -1), (16, 1)),
                selq=np.tile(selq.reshape(1, -1), (16, 1)), Bm=np.tile(bm.reshape(1, -1), (128, 1)))


def kernel(**inputs):
    inp = {k: np.ascontiguousarray(np.asarray(v, dtype=np.float32)) for k, v in inputs.items()}
    nc, names = get_prog("fused", build_fused)
    maps = []
    for c in range(8):
        bt, q = c // 4, c % 4
        selseg = np.zeros((128, 4), np.float32)
        selseg[:, q] = 1.0
        selq, ab = sel_att(c)
        maps.append(dict({n: inp[n] for n in names}, xfull=np.ascontiguousarray(inp["x"][bt]), selseg=selseg, selq=selq,
                         ABsel=ab))
    r = run(nc, maps)
    return np.stack([q["yout"] for q in r]).reshape(2, 8192, 1024).astype(np.float32)
```

```python
import math
from contextlib import ExitStack
import numpy as np
import ml_dtypes
import concourse.bass as bass
import concourse.mybir as mybir
from concourse.bass_utils import run_bass_kernel_spmd

F32 = mybir.dt.float32
BF16 = mybir.dt.bfloat16
I32 = mybir.dt.int32
ALU = mybir.AluOpType
AF = mybir.ActivationFunctionType

ENGS = ["pe", "act", "dve", "pool", "sp"]
NDMA_SEM = 16
NEG = -30000.0
TWO_PI_LO = 6.283185
POWS = [0, 1, 2, 3, 4, 5, 6, 7, 8, 16, 32, 64, 128, 256, 512, 1024, 2048, 4096]
PIDX = {n: j for j, n in enumerate(POWS)}
NPW = len(POWS)


class V:
    __slots__ = ("buf", "ap")

    def __init__(self, buf, ap):
        self.buf = buf
        self.ap = ap


class Buf:
    __slots__ = ("t", "name", "lastw", "readers")

    def __init__(self, t, name):
        self.t = t
        self.name = name
        self.lastw = None
        self.readers = []

    def __getitem__(self, idx):
        return V(self, self.t[idx])

    def v(self, ap):
        return V(self, ap)

    def sub(self, name=None):
        return Buf(self.t, name or self.name)


class Prog:
    def __init__(self, nc, stack):
        self.nc = nc
        self.stack = stack
        self.q = {e: [] for e in ENGS}
        self.cnt = {e: 0 for e in ENGS}
        self.csem = {e: [stack.enter_context(nc.semaphore("s_" + e))] for e in ENGS}
        self.dsem = {e: [stack.enter_context(nc.semaphore("d_%s%d" % (e, i))) for i in range(NDMA_SEM)]
                     for e in ENGS}
        self.dcnt = {e: 0 for e in ENGS}
        self.known = {e: {} for e in ENGS}
        self.nbuf = 0
        self.scopes = [stack]

    def push(self):
        st = ExitStack()
        self.scopes.append(st)
        return st

    def pop(self):
        self.barrier()
        st = self.scopes.pop()
        st.close()

    def sbuf(self, shape, dtype, name=None):
        self.nbuf += 1
        name = (name or "sb") + "_%d" % self.nbuf
        t = self.scopes[-1].enter_context(self.nc.sbuf_tensor(name, list(shape), dtype))
        return Buf(t, name)

    def psum(self, shape, dtype, name=None):
        self.nbuf += 1
        name = (name or "ps") + "_%d" % self.nbuf
        t = self.scopes[-1].enter_context(self.nc.psum_tensor(name, list(shape), dtype))
        return Buf(t, name)

    def dram(self, name, shape, dtype, kind):
        t = self.nc.dram_tensor(name, list(shape), dtype, kind=kind)
        return Buf(t.ap(), name)

    EPOCH = 24000

    def _ctok(self, eng):
        self.cnt[eng] += 1
        n = self.cnt[eng]
        ep = (n - 1) // self.EPOCH
        sems = self.csem[eng]
        while len(sems) <= ep:
            sems.append(self.stack.enter_context(self.nc.semaphore("s_%s_%d" % (eng, len(sems)))))
        return ((eng, "c"), sems[ep], (n - 1) % self.EPOCH + 1, n)

    def _need(self, eng, tok, waits):
        if tok is None:
            return
        key, sem, val, absn = tok
        k = self.known[eng]
        if k.get(key, 0) >= absn:
            return
        k[key] = absn
        waits.append((sem, val))

    def _deps(self, eng, reads, writes, pe_ok):
        waits = []

        def skip(tok):
            return pe_ok and tok[0] == ("pe", "c")
        for b in reads:
            if b.lastw is not None and not skip(b.lastw):
                self._need(eng, b.lastw, waits)
        for b in writes:
            if b.lastw is not None and not skip(b.lastw):
                self._need(eng, b.lastw, waits)
            for r in b.readers:
                if not skip(r):
                    self._need(eng, r, waits)
        return waits

    def _record(self, tok, reads, writes):
        for b in writes:
            b.lastw = tok
            b.readers = []
        for b in reads:
            if b in writes:
                continue
            if tok[0][1] == "c":
                b.readers = [r for r in b.readers if r[0] != tok[0]]
            b.readers.append(tok)

    def I(self, eng, meth, xr=(), xw=(), **kw):
        reads, writes, args = list(xr), list(xw), {}
        for k, v in kw.items():
            if isinstance(v, V):
                (writes if k in ("out", "accum_out") else reads).append(v.buf)
                args[k] = v.ap
            else:
                args[k] = v
        waits = self._deps(eng, reads, writes, eng == "pe")
        tok = self._ctok(eng)
        sem = tok[1]
        self._record(tok, reads, writes)

        def emit(e, meth=meth, args=args, waits=waits, sem=sem):
            for (s, v) in waits:
                e.wait_ge(s, v)
            getattr(e, meth)(**args).then_inc(sem, 1)
        self.q[eng].append(emit)

    def dma(self, eng, out, in_, **kw):
        reads, writes = [in_.buf], [out.buf]
        waits = self._deps(eng, reads, writes, False)
        i = self.dcnt[eng]
        self.dcnt[eng] += 1
        slot, rnd = i % NDMA_SEM, i // NDMA_SEM
        sem = self.dsem[eng][slot]
        key = (eng, "d", slot)
        if rnd > 0:
            self._need(eng, (key, sem, 16 * rnd, 16 * rnd), waits)
        tok = (key, sem, 16 * (rnd + 1), 16 * (rnd + 1))
        self._record(tok, reads, writes)

        def emit(e, waits=waits, sem=sem, o=out.ap, i_=in_.ap, kw=kw):
            for (s, v) in waits:
                e.wait_ge(s, v)
            e.dma_start(out=o, in_=i_, **kw).then_inc(sem, 16)
        self.q[eng].append(emit)

    def barrier(self):
        toks = []
        for e in ENGS:
            if self.cnt[e] > 0:
                n_ = self.cnt[e]
                ep_ = (n_ - 1) // self.EPOCH
                toks.append(((e, "c"), self.csem[e][ep_], (n_ - 1) % self.EPOCH + 1, n_))
            for slot in range(NDMA_SEM):
                n = (self.dcnt[e] - slot + NDMA_SEM - 1) // NDMA_SEM
                if n > 0:
                    toks.append(((e, "d", slot), self.dsem[e][slot], 16 * n, 16 * n))
        for e in ENGS:
            waits = []
            for t in toks:
                self._need(e, t, waits)

            def emit(en, waits=waits):
                for (s, v) in waits:
                    en.wait_ge(s, v)
            self.q[e].append(emit)

    def finish(self):
        self.barrier()
        nc = self.nc
        with nc.Block() as block:
            @block.tensor
            def _(e):
                for f in self.q["pe"]:
                    f(e)

            @block.scalar
            def _(e):
                for f in self.q["act"]:
                    f(e)

            @block.vector
            def _(e):
                for f in self.q["dve"]:
                    f(e)

            @block.gpsimd
            def _(e):
                for f in self.q["pool"]:
                    f(e)

            @block.sync
            def _(e):
                for f in self.q["sp"]:
                    f(e)


class Ctx:
    pass


def make_consts(P, C):
    identf = P.sbuf([128, 128], F32, "identf")
    C.ident = P.sbuf([128, 128], BF16, "ident")
    _memset(P, "pool", identf[:], 0.0)
    P.I("pool", "affine_select", out=identf[:], in_=identf[:], pattern=[[-1, 128]], base=0,
        channel_multiplier=1, compare_op=ALU.not_equal, fill=1.0)
    P.I("pool", "tensor_copy", out=C.ident[:], in_=identf[:])
    C.identf = identf


def _memset(P, eng, v, val):
    buf = v.buf
    waits = P._deps(eng, [], [buf], False)
    tok = P._ctok(eng)
    sem = tok[1]
    P._record(tok, [], [buf])

    def emit(e, ap=v.ap, val=val, waits=waits, sem=sem):
        for (s, x) in waits:
            e.wait_ge(s, x)
        e.memset(ap, val).then_inc(sem, 1)
    P.q[eng].append(emit)


def load_h(P, C, src):
    if not hasattr(C, "hb"):
        C.hb = [P.sbuf([128, 8, 1024], F32, "h%d" % b) for b in range(2)]
        C.h = [[C.hb[b].sub("h%d_%d" % (b, i)) for i in range(8)] for b in range(2)]
    sv = src.t.rearrange("(b k i) d -> b k i d", b=2, k=128)
    for b in range(2):
        for i in range(8):
            P.dma("sp" if i % 2 == 0 else "act", V(C.h[b][i], C.hb[b].t[:, i, :]), V(src, sv[b, :, i, :]))


def store_h(P, C, dst):
    dv = dst.t.rearrange("(b k i) d -> b k i d", b=2, k=128)
    for b in range(2):
        for i in range(8):
            P.dma("sp", V(dst, dv[b, :, i, :]), V(C.h[b][i], C.hb[b].t[:, i, :]))


def load_bcast(P, dram_vec_ap, dram_buf, n, name):
    t = P.sbuf([128, n], F32, name)
    P.dma("act", t[:], V(dram_buf, dram_vec_ap.partition_broadcast(128)))
    return t


def rmsnorm_T(P, C, gain, xT, xTr, pst, out_tm=None):
    ss = P.sbuf([128, 16], F32, "ss")
    rs = P.sbuf([128, 16], F32, "rs")
    junk = P.sbuf([128, 1024], F32, "junk")
    hn = [P.sbuf([128, 1024], BF16, "hn%d" % j) for j in range(2)]
    for b in range(2):
        for i in range(8):
            j = b * 8 + i
            hv = V(C.h[b][i], C.hb[b].t[:, i, :])
            P.I("act", "activation", out=junk[:], in_=hv, func=AF.Square, accum_out=ss[:, j:j + 1])
    P.I("dve", "tensor_scalar", out=rs[:], in0=ss[:], scalar1=1.0 / 1024, scalar2=1e-6, op0=ALU.mult, op1=ALU.add)
    P.I("act", "activation", out=rs[:], in_=rs[:], func=AF.Sqrt)
    P.I("dve", "reciprocal", out=rs[:], in_=rs[:])
    C.rs = rs
    for b in range(2):
        for i in range(8):
            j = b * 8 + i
            hv = V(C.h[b][i], C.hb[b].t[:, i, :])
            if out_tm is not None:
                P.I("dve", "scalar_tensor_tensor", out=out_tm[j][:], in0=hv, scalar=rs[:, j:j + 1], in1=gain[:],
                    op0=ALU.mult, op1=ALU.mult)
                continue
            hb = hn[j % 2]
            P.I("dve", "scalar_tensor_tensor", out=hb[:], in0=hv, scalar=rs[:, j:j + 1], in1=gain[:],
                op0=ALU.mult, op1=ALU.mult)
            pt = pst[j % 2]
            for c in range(8):
                P.I("pe", "transpose", out=pt[:, c, :], in_=hb[:, c * 128:(c + 1) * 128], identity=C.ident[:])
            P.I("act" if j % 2 else "dve", "activation" if j % 2 else "tensor_copy",
                out=V(xTr[j], xT.t[:, :, j * 128:(j + 1) * 128]), in_=pt[:], **({"func": AF.Copy} if j % 2 else {}))


def mlp(P, C, l, D, xT, xTr, pst, pm):
    gain = load_bcast(P, D.mlp_norm.t[l, :], D.mlp_norm, 1024, "mg")
    rmsnorm_T(P, C, gain, xT, xTr, pst)
    w1b = [P.sbuf([128, 8, 512], BF16, "w1b%d" % j) for j in range(2)]
    w2b = [P.sbuf([128, 4, 1024], BF16, "w2b%d" % j) for j in range(2)]
    hid = P.sbuf([128, 4, 2048], BF16, "hid")
    hidr = [[hid.sub() for tg in range(4)] for fc in range(4)]
    rl = [P.sbuf([128, 512], F32, "rl%d" % j) for j in range(2)]
    w1v = D.mlp_w1.t[l].rearrange("(c p) f -> p c f", p=128)
    w2v = D.mlp_w2.t[l].rearrange("(c p) n -> p c n", p=128)
    n = 0
    for fb in range(8):
        a, bb = w1b[fb % 2], w2b[fb % 2]
        P.dma("pool", a[:], V(D.mlp_w1, w1v[:, :, fb * 512:(fb + 1) * 512]))
        P.dma("pool", bb[:], V(D.mlp_w2, w2v[:, fb * 4:(fb + 1) * 4, :]))
        for fc in range(4):
            for tg in range(4):
                ps = pm[n % len(pm)]
                r = rl[n % 2]
                n += 1
                for c in range(8):
                    P.I("pe", "matmul", out=ps[:], lhsT=a[:, c, fc * 128:(fc + 1) * 128],
                        rhs=xT[:, c, tg * 512:(tg + 1) * 512], start=(c == 0), stop=(c == 7), xr=xTr[tg * 4:tg * 4 + 4])
                P.I("act", "activation", out=r[:], in_=ps[:], func=AF.Relu)
                P.I("pool", "tensor_tensor", out=V(hidr[fc][tg], hid.t[:, fc, tg * 512:(tg + 1) * 512]), in0=r[:], in1=r[:],
                    op=ALU.mult)
        for tt in range(16):
            b, i = tt // 8, tt % 8
            for half in range(2):
                ps = pm[n % len(pm)]
                n += 1
                for fc in range(4):
                    P.I("pe", "matmul", out=ps[:], lhsT=V(hidr[fc][tt // 4], hid.t[:, fc, tt * 128:(tt + 1) * 128]),
                        rhs=bb[:, fc, half * 512:(half + 1) * 512], start=(fc == 0), stop=(fc == 3))
                hv = V(C.h[b][i], C.hb[b].t[:, i, half * 512:(half + 1) * 512])
                P.I("dve", "tensor_tensor", out=hv, in0=hv, in1=ps[:], op=ALU.add)


def s5_params(P, C, l, D):
    S = Ctx()
    nc_slow = dict(allow_slow_non_contiguous=True)
    S.LR = P.sbuf([128, NPW, 32], F32, "LR")
    S.LI = P.sbuf([128, NPW, 32], F32, "LI")
    S.NLI = P.sbuf([128, NPW, 32], F32, "NLI")
    S.BR = P.sbuf([128, 32, 16], F32, "BR")
    S.BI = P.sbuf([128, 32, 16], F32, "BI")
    cr = P.sbuf([128, 32, 16], F32, "cr")
    ci = P.sbuf([128, 32, 16], F32, "ci")
    S.dcol = P.sbuf([128, 8], F32, "dcol")
    P.push()
    are = P.sbuf([128, 32], F32, "are")
    aim = P.sbuf([128, 32], F32, "aim")
    ldt = P.sbuf([128, 32], F32, "ldt")
    br = P.sbuf([128, 32, 16], F32, "br")
    bi = P.sbuf([128, 32, 16], F32, "bi")
    for g2 in range(2):
        ps_ = slice(g2 * 64, (g2 + 1) * 64)
        for (dst, src) in ((are, D.ssm_a_re), (aim, D.ssm_a_im)):
            P.dma("sp", dst[ps_, :], V(src, src.t[l].rearrange("(gp g2) p -> g2 p gp", g2=2)[g2]), **nc_slow)
        P.dma("sp", ldt[ps_, :], V(D.ssm_log_dt, D.ssm_log_dt.t[l].rearrange("(gp g2) -> g2 gp", g2=2)[g2]
                                    .partition_broadcast(64)), **nc_slow)
        for (dst, src) in ((br, D.ssm_b_re), (bi, D.ssm_b_im)):
            P.dma("act", dst[ps_, :, :], V(src, src.t[l].rearrange("(gp g2) p c -> g2 p gp c", g2=2)[g2]))
    P.dma("sp", S.dcol[:], V(D.ssm_d, D.ssm_d.t[l].rearrange("(c p) -> p c", p=128)), **nc_slow)
    pct = P.psum([128, 4, 128], F32, "pct")
    for ti, (dst, src) in enumerate(((cr, D.ssm_c_re), (ci, D.ssm_c_im))):
        ct2 = P.sbuf([128, 8, 2, 64], F32, "ct2_%d" % ti)
        sv = src.t[l].rearrange("(gb gl) c p -> (gl c) gb p", gl=8)
        for dup in range(2):
            P.dma("act" if dup else "sp", ct2[:, :, dup, :], V(src, sv))
        for half in range(2):
            for k4 in range(4):
                gb = half * 4 + k4
                P.I("pe", "transpose", out=pct[:, k4, :], in_=ct2.v(ct2.t[:, gb, :, :].rearrange("q d p -> q (d p)")),
                    identity=C.identf[:])
            for g2 in range(2):
                ps_ = slice(g2 * 64, (g2 + 1) * 64)
                srcv = pct.v(pct.t[ps_, :, :].rearrange("q k (gpl g c) -> q k gpl g c", g=2, c=16)[:, :, :, g2, :])
                dstv = dst.v(dst.t[ps_, half * 16:(half + 1) * 16, :].rearrange("q (k gpl) c -> q k gpl c", k=4))
                P.I("dve" if g2 else "act", "tensor_copy" if g2 else "activation", out=dstv, in_=srcv,
                    **({} if g2 else {"func": AF.Copy}))
    dt = P.sbuf([128, 32], F32, "dt")
    P.I("act", "activation", out=dt[:], in_=ldt[:], func=AF.Exp)
    xr = P.sbuf([128, 32], F32, "xr")
    xi = P.sbuf([128, 32], F32, "xi")
    P.I("dve", "tensor_tensor", out=xr[:], in0=are[:], in1=dt[:], op=ALU.mult)
    P.I("dve", "tensor_tensor", out=xi[:], in0=aim[:], in1=dt[:], op=ALU.mult)
    ncst = P.sbuf([128, NPW, 32], F32, "ncst")
    for j, n in enumerate(POWS):
        _memset(P, "pool", ncst[:, j, :], float(n))
    R = P.sbuf([128, NPW, 32], F32, "R")
    E = P.sbuf([128, NPW, 32], F32, "E")
    Ki = P.sbuf([128, NPW, 32], I32, "Ki")
    Kf = P.sbuf([128, NPW, 32], F32, "Kf")
    T1 = P.sbuf([128, NPW, 32], F32, "T1")
    T2 = P.sbuf([128, NPW, 32], F32, "T2")
    xib = xi.v(xi.t[:, :].unsqueeze(1).to_broadcast([128, NPW, 32]))
    xrb = xr.v(xr.t[:, :].unsqueeze(1).to_broadcast([128, NPW, 32]))
    P.I("dve", "tensor_tensor", out=R[:], in0=ncst[:], in1=xib, op=ALU.mult)
    P.I("dve", "tensor_scalar", out=R[:], in0=R[:], scalar1=1.0 / (2 * math.pi), scalar2=None, op0=ALU.mult)
    P.I("dve", "tensor_tensor", out=E[:], in0=ncst[:], in1=xrb, op=ALU.mult)
    P.I("dve", "tensor_copy", out=Ki[:], in_=R[:])
    P.I("dve", "tensor_copy", out=Kf[:], in_=Ki[:])
    P.I("dve", "tensor_tensor", out=R[:], in0=R[:], in1=Kf[:], op=ALU.subtract)
    P.I("dve", "scalar_tensor_tensor", out=T1[:], in0=R[:], scalar=0.5, in1=R[:], op0=ALU.is_gt, op1=ALU.subtract)
    P.I("dve", "scalar_tensor_tensor", out=T2[:], in0=T1[:], scalar=0.5, in1=T1[:], op0=ALU.is_gt, op1=ALU.subtract)
    SN = P.sbuf([128, NPW, 32], F32, "SN")
    CS = P.sbuf([128, NPW, 32], F32, "CS")
    MG = P.sbuf([128, NPW, 32], F32, "MG")
    P.I("act", "activation", out=SN[:], in_=T2[:], func=AF.Sin, scale=TWO_PI_LO)
    P.I("dve", "tensor_scalar", out=T1[:], in0=T2[:], scalar1=0.25, scalar2=None, op0=ALU.add)
    P.I("dve", "scalar_tensor_tensor", out=T2[:], in0=T1[:], scalar=0.5, in1=T1[:], op0=ALU.is_gt, op1=ALU.subtract)
    P.I("act", "activation", out=CS[:], in_=T2[:], func=AF.Sin, scale=-TWO_PI_LO)
    P.I("act", "activation", out=MG[:], in_=E[:], func=AF.Exp)
    P.I("dve", "tensor_tensor", out=S.LR[:], in0=MG[:], in1=CS[:], op=ALU.mult)
    P.I("dve", "tensor_tensor", out=S.LI[:], in0=MG[:], in1=SN[:], op=ALU.mult)
    P.I("dve", "tensor_scalar", out=S.NLI[:], in0=S.LI[:], scalar1=-1.0, scalar2=None, op0=ALU.mult)
    nr = P.sbuf([128, 32], F32, "nr")
    den = P.sbuf([128, 32], F32, "den")
    t1 = P.sbuf([128, 32], F32, "t1")
    t2 = P.sbuf([128, 32], F32, "t2")
    cfr = P.sbuf([128, 32], F32, "cfr")
    cfi = P.sbuf([128, 32], F32, "cfi")
    l1r, l1i = S.LR[:, PIDX[1], :], S.LI[:, PIDX[1], :]
    P.I("dve", "tensor_scalar", out=nr[:], in0=l1r, scalar1=-1.0, scalar2=None, op0=ALU.add)
    P.I("dve", "tensor_tensor", out=den[:], in0=are[:], in1=are[:], op=ALU.mult)
    P.I("dve", "tensor_tensor", out=t1[:], in0=aim[:], in1=aim[:], op=ALU.mult)
    P.I("dve", "tensor_tensor", out=den[:], in0=den[:], in1=t1[:], op=ALU.add)
    P.I("dve", "reciprocal", out=den[:], in_=den[:])
    P.I("dve", "tensor_tensor", out=t1[:], in0=nr[:], in1=are[:], op=ALU.mult)
    P.I("dve", "tensor_tensor", out=t2[:], in0=l1i, in1=aim[:], op=ALU.mult)
    P.I("dve", "tensor_tensor", out=t1[:], in0=t1[:], in1=t2[:], op=ALU.add)
    P.I("dve", "tensor_tensor", out=cfr[:], in0=t1[:], in1=den[:], op=ALU.mult)
    P.I("dve", "tensor_tensor", out=t1[:], in0=l1i, in1=are[:], op=ALU.mult)
    P.I("dve", "tensor_tensor", out=t2[:], in0=nr[:], in1=aim[:], op=ALU.mult)
    P.I("dve", "tensor_tensor", out=t1[:], in0=t1[:], in1=t2[:], op=ALU.subtract)
    P.I("dve", "tensor_tensor", out=cfi[:], in0=t1[:], in1=den[:], op=ALU.mult)
    u1 = P.sbuf([128, 32, 16], F32, "u1")
    u2 = P.sbuf([128, 32, 16], F32, "u2")
    cfrb = cfr.v(cfr.t[:, :].unsqueeze(2).to_broadcast([128, 32, 16]))
    cfib = cfi.v(cfi.t[:, :].unsqueeze(2).to_broadcast([128, 32, 16]))
    P.I("dve", "tensor_tensor", out=u1[:], in0=br[:], in1=cfrb, op=ALU.mult)
    P.I("dve", "tensor_tensor", out=u2[:], in0=bi[:], in1=cfib, op=ALU.mult)
    P.I("dve", "tensor_tensor", out=S.BR[:], in0=u1[:], in1=u2[:], op=ALU.subtract)
    P.I("dve", "tensor_tensor", out=u1[:], in0=bi[:], in1=cfrb, op=ALU.mult)
    P.I("dve", "tensor_tensor", out=u2[:], in0=br[:], in1=cfib, op=ALU.mult)
    P.I("dve", "tensor_tensor", out=S.BI[:], in0=u1[:], in1=u2[:], op=ALU.add)
    S.CR, S.CI = cr, ci
    P.pop()
    return S


def s5_core(P, C, l, D, S, xT, xTr, mode, Fout=None, Fall=None, selS=None, carry=None):
    pw = P.psum([128, 16, 128], BF16, "pw")
    pk = P.psum([128, 8, 32], F32, "pk")
    pz = [P.psum([128, 2, 256], F32, "pz%d" % j) for j in range(2 if mode == "A" else 1)]
    py = [P.psum([128, 8, 128], F32, "py%d" % j) for j in range(2)] if mode == "B" else None
    XB = P.sbuf([128, 8, 2, 4, 32], BF16, "XB")
    CB = P.sbuf([128, 9, 2, 4, 32], BF16, "CB")
    _memset(P, "pool", XB[:], 0.0)
    _memset(P, "pool", CB[:], 0.0)
    WT = P.sbuf([128, 16, 128], BF16, "WT")
    KT = P.sbuf([128, 8, 32], BF16, "KT")
    v1 = P.sbuf([128, 9, 4, 16], F32, "v1")
    v2 = P.sbuf([128, 9, 4, 16], F32, "v2")
    ZW = [[P.sbuf([128, 2, 256], F32, "zw%d_%d" % (m, j)) for j in range(2)] for m in range(4)]
    tt_ = [P.sbuf([128, 256], F32, "tt%d" % m) for m in range(4)]
    tu_ = [P.sbuf([128, 256], F32, "tu%d" % m) for m in range(4)]
    if mode == "A":
        Fsb = P.sbuf([128, 32, 2], F32, "Fsb")
    else:
        SP = P.sbuf([128, 4, 2, 256], BF16, "SP")
        SPr = [SP.sub() for m in range(4)]
        e1 = P.sbuf([128, 1024], F32, "e1")
        e2 = P.sbuf([128, 1024], F32, "e2")
        e3 = P.sbuf([128, 1024], F32, "e3")
        if carry is None:
            FA = P.sbuf([128, 8, 64], F32, "FA")
            P.dma("sp", FA[:], V(Fall, Fall.t.rearrange("c p f -> p c f")))
            SL = P.sbuf([128, 24], F32, "SL")
            P.dma("sp", SL[:], selS[:])
            acc = [P.sbuf([128, 32, 2], F32, "acc%d" % j) for j in range(3)]
            for mm in range(3):
                av = acc[mm].v(acc[mm].t[:].rearrange("p g r -> p (g r)"))
                for c in range(8):
                    if c == 0:
                        P.I("dve", "tensor_scalar", out=av, in0=FA[:, 0, :], scalar1=SL[:, mm:mm + 1], scalar2=None,
                            op0=ALU.mult)
                    else:
                        P.I("dve", "scalar_tensor_tensor", out=av, in0=FA[:, c, :],
                            scalar=SL[:, c * 3 + mm:c * 3 + mm + 1], in1=av, op0=ALU.mult, op1=ALU.add)
        SIN = P.sbuf([128, 32, 2], F32, "SIN")
        w1 = P.sbuf([128, 32], F32, "w1")
        w2 = P.sbuf([128, 32], F32, "w2")

        def cmul_acc(dst, src, pw_, first):
            lr, li = S.LR[:, PIDX[pw_], :], S.LI[:, PIDX[pw_], :]
            P.I("dve", "tensor_tensor", out=w1[:], in0=src[:, :, 0], in1=lr, op=ALU.mult)
            P.I("dve", "tensor_tensor", out=w2[:], in0=src[:, :, 1], in1=li, op=ALU.mult)
            P.I("dve", "tensor_tensor", out=w1[:], in0=w1[:], in1=w2[:], op=ALU.subtract)
            if first:
                P.I("dve", "tensor_copy", out=dst[:, :, 0], in_=w1[:])
            else:
                P.I("dve", "tensor_tensor", out=dst[:, :, 0], in0=dst[:, :, 0], in1=w1[:], op=ALU.add)
            P.I("dve", "tensor_tensor", out=w1[:], in0=src[:, :, 1], in1=lr, op=ALU.mult)
            P.I("dve", "tensor_tensor", out=w2[:], in0=src[:, :, 0], in1=li, op=ALU.mult)
            P.I("dve", "tensor_tensor", out=w1[:], in0=w1[:], in1=w2[:], op=ALU.add)
            if first:
                P.I("dve", "tensor_copy", out=dst[:, :, 1], in_=w1[:])
            else:
                P.I("dve", "tensor_tensor", out=dst[:, :, 1], in0=dst[:, :, 1], in1=w1[:], op=ALU.add)
        if carry is None:
            P.I("dve", "tensor_copy", out=SIN[:], in_=acc[0][:])
            cmul_acc(SIN, acc[1], 2048, False)
            cmul_acc(SIN, acc[2], 4096, False)
        else:
            P.I("dve", "tensor_copy", out=SIN[:], in_=carry[:])
        INJ = P.sbuf([128, 32, 2], F32, "INJ")
        cmul_acc(INJ, SIN, 8, True)

    for ch in range(8):
        gs = slice(ch * 4, ch * 4 + 4)
        xreg = [xTr[j] for j in range(16)]
        for half in range(2):
            pr = slice(half * 64, half * 64 + 64)
            hs = slice(half * 16, half * 16 + 16)
            for (tab, n0, nn, A_r, A_i, conjC) in ((XB, 0, 8, S.BR, S.BI, False), (CB, 0, 9, S.CR, S.CI, True)):
                ar = A_r.v(A_r.t[pr, gs, :].unsqueeze(1).to_broadcast([64, nn, 4, 16]))
                ai = A_i.v(A_i.t[pr, gs, :].unsqueeze(1).to_broadcast([64, nn, 4, 16]))
                lr = S.LR.v(S.LR.t[pr, n0:n0 + nn, gs].unsqueeze(3).to_broadcast([64, nn, 4, 16]))
                li = S.LI.v(S.LI.t[pr, n0:n0 + nn, gs].unsqueeze(3).to_broadcast([64, nn, 4, 16]))
                a1, a2 = v1[pr, 0:nn], v2[pr, 0:nn]
                P.I("dve", "tensor_tensor", out=a1, in0=ar, in1=lr, op=ALU.mult)
                P.I("pool", "tensor_tensor", out=a2, in0=ai, in1=li, op=ALU.mult)
                P.I("dve", "tensor_tensor", out=tab[pr, 0:nn, 0, :, hs], in0=a1, in1=a2, op=ALU.subtract)
                P.I("dve", "tensor_tensor", out=a1, in0=ar, in1=li, op=ALU.mult)
                P.I("pool", "tensor_tensor", out=a2, in0=ai, in1=lr, op=ALU.mult)
                if conjC:
                    P.I("dve", "tensor_tensor", out=a1, in0=a1, in1=a2, op=ALU.add)
                    P.I("dve", "tensor_scalar", out=tab[pr, 0:nn, 1, :, hs], in0=a1, scalar1=-1.0, scalar2=None,
                        op0=ALU.mult)
                else:
                    P.I("dve", "tensor_tensor", out=tab[pr, 0:nn, 1, :, hs], in0=a1, in1=a2, op=ALU.add)
        for n in range(8):
            for r in range(2):
                P.I("pe", "transpose", out=pw[:, n * 2 + r, :],
                    in_=XB.v(XB.t[:, n, r, :, :].rearrange("p m c -> p (m c)")), identity=C.ident[:])
        P.I("act", "activation", out=WT[:], in_=pw[:], func=AF.Copy)
        if mode == "B":
            for tau in range(8):
                for m in range(4):
                    for r in range(2):
                        P.I("pe", "matmul", out=pk[m * 32:(m + 1) * 32, tau, :], lhsT=XB[:, tau, r, m, :],
                            rhs=CB[:, 0, r, m, :], start=(r == 0), stop=(r == 1), tile_position=(0, m * 32))
            P.I("act", "activation", out=KT[:], in_=pk[:], func=AF.Copy)
        for m in range(4):
            pzz = pz[m % len(pz)]
            rs_ = slice(m * 32, m * 32 + 32)
            for r in range(2):
                for i in range(8):
                    rhs = xT.v(xT.t[rs_, ch, :].rearrange("p (b i k) -> p b i k", b=2, i=8)[:, :, i, :])
                    P.I("pe", "matmul", out=pzz[:, r, :], lhsT=WT[rs_, (7 - i) * 2 + r, :], rhs=rhs,
                        start=(i == 0), stop=(i == 7), tile_position=(m * 32, 0), xr=xreg)
            P.I("act", "activation", out=ZW[m][0][:], in_=pzz[:], func=AF.Copy)
            if mode == "B":
                gp = ch * 4 + m
                P.I("pool", "tensor_tensor", out=ZW[m][0][:, :, 0], in0=ZW[m][0][:, :, 0], in1=INJ[:, gp, :], op=ALU.add)
        if mode == "A":
            cur = [0, 0, 0, 0]
            ln = 256
            for s in range(8):
                pidx = PIDX[8 << s]
                hf = ln // 2
                for m in range(4):
                    gp = ch * 4 + m
                    a = ZW[m][cur[m]]
                    lr = S.LR[:, pidx, gp:gp + 1]
                    P.I("dve", "scalar_tensor_tensor", out=tt_[m][:, 0:hf], in0=a[:, 0, 0:ln:2], scalar=lr,
                        in1=a[:, 0, 1:ln:2], op0=ALU.mult, op1=ALU.add)
                    P.I("dve", "scalar_tensor_tensor", out=tu_[m][:, 0:hf], in0=a[:, 1, 0:ln:2], scalar=lr,
                        in1=a[:, 1, 1:ln:2], op0=ALU.mult, op1=ALU.add)
                for m in range(4):
                    gp = ch * 4 + m
                    a, b_ = ZW[m][cur[m]], ZW[m][1 - cur[m]]
                    li, nli = S.LI[:, pidx, gp:gp + 1], S.NLI[:, pidx, gp:gp + 1]
                    P.I("dve", "scalar_tensor_tensor", out=b_[:, 0, 0:hf], in0=a[:, 1, 0:ln:2], scalar=nli,
                        in1=tt_[m][:, 0:hf], op0=ALU.mult, op1=ALU.add)
                    P.I("dve", "scalar_tensor_tensor", out=b_[:, 1, 0:hf], in0=a[:, 0, 0:ln:2], scalar=li,
                        in1=tu_[m][:, 0:hf], op0=ALU.mult, op1=ALU.add)
                    cur[m] = 1 - cur[m]
                ln = hf
            for m in range(4):
                gp = ch * 4 + m
                P.I("pool", "tensor_copy", out=Fsb[:, gp, :], in_=ZW[m][cur[m]][:, :, 0])
            continue
        cur = [0, 0, 0, 0]
        for s in range(8):
            d = 1 << s
            pidx = PIDX[8 * d]
            for m in range(4):
                gp = ch * 4 + m
                a, b_ = ZW[m][cur[m]], ZW[m][1 - cur[m]]
                lr, li, nli = S.LR[:, pidx, gp:gp + 1], S.LI[:, pidx, gp:gp + 1], S.NLI[:, pidx, gp:gp + 1]
                P.I("pool", "tensor_copy", out=b_[:, :, 0:d], in_=a[:, :, 0:d])
                P.I("dve", "scalar_tensor_tensor", out=tt_[m][:, 0:256 - d], in0=a[:, 0, 0:256 - d], scalar=lr,
                    in1=a[:, 0, d:256], op0=ALU.mult, op1=ALU.add)
                P.I("dve", "scalar_tensor_tensor", out=tu_[m][:, 0:256 - d], in0=a[:, 1, 0:256 - d], scalar=lr,
                    in1=a[:, 1, d:256], op0=ALU.mult, op1=ALU.add)
            for m in range(4):
                gp = ch * 4 + m
                a, b_ = ZW[m][cur[m]], ZW[m][1 - cur[m]]
                li, nli = S.LI[:, pidx, gp:gp + 1], S.NLI[:, pidx, gp:gp + 1]
                P.I("dve", "scalar_tensor_tensor", out=b_[:, 0, d:256], in0=a[:, 1, 0:256 - d], scalar=nli,
                    in1=tt_[m][:, 0:256 - d], op0=ALU.mult, op1=ALU.add)
                P.I("dve", "scalar_tensor_tensor", out=b_[:, 1, d:256], in0=a[:, 0, 0:256 - d], scalar=li,
                    in1=tu_[m][:, 0:256 - d], op0=ALU.mult, op1=ALU.add)
                cur[m] = 1 - cur[m]
        if mode == "A":
            for m in range(4):
                gp = ch * 4 + m
                P.I("pool", "tensor_copy", out=Fsb[:, gp, :], in_=ZW[m][cur[m]][:, :, 255])
            continue
        for m in range(4):
            gp = ch * 4 + m
            fin = ZW[m][cur[m]]
            if carry is not None:
                P.I("pool", "tensor_copy", out=carry[:, gp, :], in_=fin[:, :, 255])
            P.I("act", "activation", out=V(SPr[m], SP.t[:, m, :, 1:256]), in_=fin[:, :, 0:255], func=AF.Copy)
            P.I("pool", "tensor_copy", out=V(SPr[m], SP.t[:, m, :, 0]), in_=SIN[:, gp, :])
        for b in range(2):
            pyy = py[b]
            for m in range(4):
                rs_ = slice(m * 32, m * 32 + 32)
                for j in range(8):
                    nmm = (j + 1) + 2
                    k_ = 0
                    for i in range(j + 1):
                        P.I("pe", "matmul", out=pyy[rs_, j, :], lhsT=KT[rs_, j - i, :],
                            rhs=xT[rs_, ch, b * 1024 + i * 128: b * 1024 + (i + 1) * 128],
                            start=(k_ == 0), stop=False, tile_position=(m * 32, m * 32), xr=xreg)
                        k_ += 1
                    for r in range(2):
                        P.I("pe", "matmul", out=pyy[rs_, j, :], lhsT=CB[:, j + 1, r, m, :],
                            rhs=V(SPr[m], SP.t[:, m, r, b * 128:(b + 1) * 128]),
                            start=False, stop=(r == 1), tile_position=(0, m * 32))
            uv = xT.v(xT.t[:, ch, b * 1024:(b + 1) * 1024])
            yv = pyy.v(pyy.t[:].rearrange("p j k -> p (j k)"))
            P.I("dve", "scalar_tensor_tensor", out=e1[:], in0=uv, scalar=S.dcol[:, ch:ch + 1], in1=yv,
                op0=ALU.mult, op1=ALU.add, xr=xreg)
            P.I("pool", "tensor_tensor", out=e2[:], in0=e1[:], in1=e1[:], op=ALU.mult)
            P.I("pool", "tensor_scalar", out=e2[:], in0=e2[:], scalar1=0.044715, scalar2=1.0, op0=ALU.mult, op1=ALU.add)
            P.I("pool", "tensor_tensor", out=e2[:], in0=e2[:], in1=e1[:], op=ALU.mult)
            P.I("act", "activation", out=e3[:], in_=e2[:], func=AF.Sigmoid, scale=1.5957691216)
            for i in range(8):
                P.I("dve", "tensor_tensor", out=V(xTr[b * 8 + i], xT.t[:, ch, b * 1024 + i * 128:b * 1024 + (i + 1) * 128]),
                    in0=e3[:, i * 128:(i + 1) * 128], in1=e1[:, i * 128:(i + 1) * 128], op=ALU.mult)
    if mode == "A":
        P.dma("sp", Fout[:], Fsb.v(Fsb.t[:].rearrange("p g r -> p (g r)")))


def s5_tables(P, C, S, TabW, TabC, TabK):
    P.push()
    pw = P.psum([128, 16, 128], BF16, "pw")
    pk = P.psum([128, 8, 32], F32, "pk")
    XB = P.sbuf([128, 8, 2, 4, 32], BF16, "XB")
    CBs = [P.sbuf([128, 9, 2, 4, 32], BF16, "CB%d" % j) for j in range(2)]
    WTs = [P.sbuf([128, 16, 128], BF16, "WT%d" % j) for j in range(2)]
    KTs = [P.sbuf([128, 8, 32], BF16, "KT%d" % j) for j in range(2)]
    _memset(P, "pool", XB[:], 0.0)
    for j in range(2):
        _memset(P, "pool", CBs[j][:], 0.0)
    v1 = P.sbuf([128, 9, 4, 16], F32, "v1")
    v2 = P.sbuf([128, 9, 4, 16], F32, "v2")
    for ch in range(8):
        gs = slice(ch * 4, ch * 4 + 4)
        CB, WT, KT = CBs[ch % 2], WTs[ch % 2], KTs[ch % 2]
        for half in range(2):
            pr = slice(half * 64, half * 64 + 64)
            hs = slice(half * 16, half * 16 + 16)
            for (tab, nn, A_r, A_i, conjC) in ((XB, 8, S.BR, S.BI, False), (CB, 9, S.CR, S.CI, True)):
                ar = A_r.v(A_r.t[pr, gs, :].unsqueeze(1).to_broadcast([64, nn, 4, 16]))
                ai = A_i.v(A_i.t[pr, gs, :].unsqueeze(1).to_broadcast([64, nn, 4, 16]))
                lr = S.LR.v(S.LR.t[pr, 0:nn, gs].unsqueeze(3).to_broadcast([64, nn, 4, 16]))
                li = S.LI.v(S.LI.t[pr, 0:nn, gs].unsqueeze(3).to_broadcast([64, nn, 4, 16]))
                a1, a2 = v1[pr, 0:nn], v2[pr, 0:nn]
                P.I("dve", "tensor_tensor", out=a1, in0=ar, in1=lr, op=ALU.mult)
                P.I("pool", "tensor_tensor", out=a2, in0=ai, in1=li, op=ALU.mult)
                P.I("dve", "tensor_tensor", out=tab[pr, 0:nn, 0, :, hs], in0=a1, in1=a2, op=ALU.subtract)
                P.I("dve", "tensor_tensor", out=a1, in0=ar, in1=li, op=ALU.mult)
                P.I("pool", "tensor_tensor", out=a2, in0=ai, in1=lr, op=ALU.mult)
                if conjC:
                    P.I("dve", "tensor_tensor", out=a1, in0=a1, in1=a2, op=ALU.add)
                    P.I("dve", "tensor_scalar", out=tab[pr, 0:nn, 1, :, hs], in0=a1, scalar1=-1.0, scalar2=None,
                        op0=ALU.mult)
                else:
                    P.I("dve", "tensor_tensor", out=tab[pr, 0:nn, 1, :, hs], in0=a1, in1=a2, op=ALU.add)
        for n in range(8):
            for r in range(2):
                P.I("pe", "transpose", out=pw[:, n * 2 + r, :],
                    in_=XB.v(XB.t[:, n, r, :, :].rearrange("p m c -> p (m c)")), identity=C.ident[:])
        P.I("act", "activation", out=WT[:], in_=pw[:], func=AF.Copy)
        for tau in range(8):
            for m in range(4):
                for r in range(2):
                    P.I("pe", "matmul", out=pk[m * 32:(m + 1) * 32, tau, :], lhsT=XB[:, tau, r, m, :],
                        rhs=CB[:, 0, r, m, :], start=(r == 0), stop=(r == 1), tile_position=(0, m * 32))
        P.I("act", "activation", out=KT[:], in_=pk[:], func=AF.Copy)
        P.dma("sp", V(TabW, TabW.t[ch]), WT[:])
        P.dma("act", V(TabC, TabC.t[ch]), CB.v(CB.t[:].rearrange("p n r m c -> p (n r m c)")))
        P.dma("sp", V(TabK, TabK.t[ch]), KT.v(KT.t[:].rearrange("p t c -> p (t c)")))
    P.pop()


def s5_core_pipe(P, C, l, D, S, xT, carry, TabW, TabC, TabK):
    pzs = [P.psum([128, 2, 256], F32, "pz%d" % j) for j in range(2)]
    py = P.psum([128, 8, 2, 128], F32, "py")
    CBs = [P.sbuf([128, 9, 2, 4, 32], BF16, "CB%d" % j) for j in range(2)]
    WTs = [P.sbuf([128, 16, 128], BF16, "WT%d" % j) for j in range(2)]
    KTs = [P.sbuf([128, 8, 32], BF16, "KT%d" % j) for j in range(2)]
    ZWs = [[[P.sbuf([128, 2, 256], F32, "zw%d_%d_%d" % (pp, m, j)) for j in range(2)] for m in range(4)] for pp in range(2)]
    tt_ = [P.sbuf([128, 256], F32, "tt%d" % m) for m in range(4)]
    tu_ = [P.sbuf([128, 256], F32, "tu%d" % m) for m in range(4)]
    SP = P.sbuf([128, 4, 2, 256], BF16, "SP")
    SPr = [SP.sub() for m in range(4)]
    e1s = [P.sbuf([128, 1024], F32, "e1_%d" % k) for k in range(2)]
    e2s = [P.sbuf([128, 1024], F32, "e2_0")] * 2
    e3s = [P.sbuf([128, 1024], BF16, "e3_%d" % k) for k in range(2)]
    SIN = P.sbuf([128, 32, 2], F32, "SIN")
    INJ = P.sbuf([128, 32, 2], F32, "INJ")
    w1 = P.sbuf([128, 32], F32, "w1")
    w2 = P.sbuf([128, 32], F32, "w2")
    xTc = [xT.sub("xTc%d" % ch) for ch in range(8)]
    P.I("dve", "tensor_copy", out=SIN[:], in_=carry[:])
    lr8, li8 = S.LR[:, PIDX[8], :], S.LI[:, PIDX[8], :]
    P.I("dve", "tensor_tensor", out=w1[:], in0=SIN[:, :, 0], in1=lr8, op=ALU.mult)
    P.I("dve", "tensor_tensor", out=w2[:], in0=SIN[:, :, 1], in1=li8, op=ALU.mult)
    P.I("dve", "tensor_tensor", out=INJ[:, :, 0], in0=w1[:], in1=w2[:], op=ALU.subtract)
    P.I("dve", "tensor_tensor", out=w1[:], in0=SIN[:, :, 1], in1=lr8, op=ALU.mult)
    P.I("dve", "tensor_tensor", out=w2[:], in0=SIN[:, :, 0], in1=li8, op=ALU.mult)
    P.I("dve", "tensor_tensor", out=INJ[:, :, 1], in0=w1[:], in1=w2[:], op=ALU.add)
    curs = {}

    def T(ch):
        CB, KT, ZW, WT = CBs[ch % 2], KTs[ch % 2], ZWs[ch % 2], WTs[ch % 2]
        P.dma("sp", WT[:], V(TabW, TabW.t[ch]))
        P.dma("act", CB.v(CB.t[:].rearrange("p n r m c -> p (n r m c)")), V(TabC, TabC.t[ch]))
        P.dma("sp", KT.v(KT.t[:].rearrange("p t c -> p (t c)")), V(TabK, TabK.t[ch]))
        for m in range(4):
            rs_ = slice(m * 32, m * 32 + 32)
            for r in range(2):
                for i in range(8):
                    rhs = xT.v(xT.t[rs_, ch, :].rearrange("p (b i k) -> p b i k", b=2, i=8)[:, :, i, :])
                    P.I("pe", "matmul", out=pzs[m % 2][:, r, :], lhsT=WT[rs_, (7 - i) * 2 + r, :], rhs=rhs,
                        start=(i == 0), stop=(i == 7), tile_position=(m * 32, 0), xr=[xTc[ch]])
            P.I("act", "activation", out=ZW[m][0][:], in_=pzs[m % 2][:], func=AF.Copy)

    def Sx(ch):
        ZW = ZWs[ch % 2]
        cur = [0, 0, 0, 0]
        for m in range(4):
            gp = ch * 4 + m
            P.I("pool", "tensor_tensor", out=ZW[m][0][:, :, 0], in0=ZW[m][0][:, :, 0], in1=INJ[:, gp, :], op=ALU.add)
        for s_ in range(8):
            d = 1 << s_
            pidx = PIDX[8 * d]
            for m in range(4):
                gp = ch * 4 + m
                a, b_ = ZW[m][cur[m]], ZW[m][1 - cur[m]]
                lr = S.LR[:, pidx, gp:gp + 1]
                P.I("pool", "tensor_copy", out=b_[:, :, 0:d], in_=a[:, :, 0:d])
                P.I("dve", "scalar_tensor_tensor", out=tt_[m][:, 0:256 - d], in0=a[:, 0, 0:256 - d], scalar=lr,
                    in1=a[:, 0, d:256], op0=ALU.mult, op1=ALU.add)
                P.I("dve", "scalar_tensor_tensor", out=tu_[m][:, 0:256 - d], in0=a[:, 1, 0:256 - d], scalar=lr,
                    in1=a[:, 1, d:256], op0=ALU.mult, op1=ALU.add)
            for m in range(4):
                gp = ch * 4 + m
                a, b_ = ZW[m][cur[m]], ZW[m][1 - cur[m]]
                li, nli = S.LI[:, pidx, gp:gp + 1], S.NLI[:, pidx, gp:gp + 1]
                P.I("dve", "scalar_tensor_tensor", out=b_[:, 0, d:256], in0=a[:, 1, 0:256 - d], scalar=nli,
                    in1=tt_[m][:, 0:256 - d], op0=ALU.mult, op1=ALU.add)
                P.I("dve", "scalar_tensor_tensor", out=b_[:, 1, d:256], in0=a[:, 0, 0:256 - d], scalar=li,
                    in1=tu_[m][:, 0:256 - d], op0=ALU.mult, op1=ALU.add)
                cur[m] = 1 - cur[m]
        for m in range(4):
            gp = ch * 4 + m
            fin = ZW[m][cur[m]]
            P.I("pool", "tensor_copy", out=carry[:, gp, :], in_=fin[:, :, 255])
            P.I("act", "activation", out=V(SPr[m], SP.t[:, m, :, 1:256]), in_=fin[:, :, 0:255], func=AF.Copy)
            P.I("pool", "tensor_copy", out=V(SPr[m], SP.t[:, m, :, 0]), in_=SIN[:, gp, :])

    def Y(ch):
        CB, KT = CBs[ch % 2], KTs[ch % 2]
        for m in range(4):
            rs_ = slice(m * 32, m * 32 + 32)
            for j in range(8):
                for i in range(j + 1):
                    rhs = xT.v(xT.t[rs_, ch, :].rearrange("p (b i k) -> p b i k", b=2, i=8)[:, :, i, :])
                    P.I("pe", "matmul", out=py[rs_, j, :, :], lhsT=KT[rs_, j - i, :], rhs=rhs,
                        start=(i == 0), stop=False, tile_position=(m * 32, m * 32), xr=[xTc[ch]])
                for r in range(2):
                    P.I("pe", "matmul", out=py[rs_, j, :, :], lhsT=CB[:, j + 1, r, m, :],
                        rhs=V(SPr[m], SP.t[:, m, r, :].rearrange("p (b k) -> p b k", b=2)),
                        start=False, stop=(r == 1), tile_position=(0, m * 32))

    def E(ch):
        for b in range(2):
            uv = xT.v(xT.t[:, ch, b * 1024:(b + 1) * 1024].rearrange("p (j k) -> p j k", j=8))
            e1v = e1s[b].v(e1s[b].t[:, :].rearrange("p (j k) -> p j k", j=8))
            P.I("dve", "scalar_tensor_tensor", out=e1v, in0=uv, scalar=S.dcol[:, ch:ch + 1], in1=py[:, :, b, :],
                op0=ALU.mult, op1=ALU.add, xr=[xTc[ch]])
        for b in range(2):
            e1, e2, e3 = e1s[b], e2s[b], e3s[b]
            P.I("pool", "tensor_tensor", out=e2[:], in0=e1[:], in1=e1[:], op=ALU.mult)
            P.I("pool", "tensor_scalar", out=e2[:], in0=e2[:], scalar1=0.044715, scalar2=1.0, op0=ALU.mult, op1=ALU.add)
            P.I("pool", "tensor_tensor", out=e2[:], in0=e2[:], in1=e1[:], op=ALU.mult)
            P.I("act", "activation", out=e3[:], in_=e2[:], func=AF.Sigmoid, scale=1.5957691216)
        for b in range(2):
            P.I("dve", "tensor_tensor", out=V(xTc[ch], xT.t[:, ch, b * 1024:(b + 1) * 1024]), in0=e3s[b][:], in1=e1s[b][:],
                op=ALU.mult)

    T(0)
    for ch in range(8):
        if ch + 1 < 8:
            T(ch + 1)
        Sx(ch)
        if ch >= 1:
            E(ch - 1)
        Y(ch)
    E(7)


def glu(P, C, l, D, xT, xTr, pm):
    wv = D.ssm_w_glu.t[l].rearrange("(c p) n -> p c n", p=128)
    wb = [[P.sbuf([128, 8, 512], BF16, "wg%d%d" % (a, j)) for j in range(2)] for a in range(2)]
    sg = [P.sbuf([128, 512], F32, "sg%d" % j) for j in range(2)]
    n = 0
    for half in range(2):
        P.dma("pool", wb[half][0][:], V(D.ssm_w_glu, wv[:, :, half * 512:(half + 1) * 512]))
        P.dma("pool", wb[half][1][:], V(D.ssm_w_glu, wv[:, :, 1024 + half * 512:1024 + (half + 1) * 512]))
        for tt in range(16):
            b, i = tt // 8, tt % 8
            pv, pg = pm[n % len(pm)], pm[(n + 1) % len(pm)]
            n += 2
            for (ps, w) in ((pv, wb[half][0]), (pg, wb[half][1])):
                for c in range(8):
                    P.I("pe", "matmul", out=ps[:], lhsT=V(xTr[tt], xT.t[:, c, tt * 128:(tt + 1) * 128]), rhs=w[:, c, :],
                        start=(c == 0), stop=(c == 7))
            s = sg[tt % 2]
            P.I("act", "activation", out=s[:], in_=pg[:], func=AF.Sigmoid)
            P.I("dve", "tensor_tensor", out=s[:], in0=s[:], in1=pv[:], op=ALU.mult)
            hv = V(C.h[b][i], C.hb[b].t[:, i, half * 512:(half + 1) * 512])
            P.I("pool", "tensor_tensor", out=hv, in0=hv, in1=s[:], op=ALU.add)


PARAMS = [("mix_norm", [4, 1024]), ("mlp_norm", [4, 1024]), ("mlp_w1", [4, 1024, 4096]), ("mlp_w2", [4, 4096, 1024]),
          ("ssm_log_dt", [2, 64]), ("ssm_a_re", [2, 64, 64]), ("ssm_a_im", [2, 64, 64]),
          ("ssm_b_re", [2, 64, 64, 16]), ("ssm_b_im", [2, 64, 64, 16]), ("ssm_c_re", [2, 64, 16, 64]),
          ("ssm_c_im", [2, 64, 16, 64]), ("ssm_d", [2, 1024]), ("ssm_w_glu", [2, 1024, 2048]), ("kv_norm", [1024]),
          ("w_kvf", [1024, 2064]), ("b_f", [16]), ("attn_wq", [2, 1024, 1024]), ("attn_wo", [2, 1024, 1024]),
          ("final_norm", [1024])]


def declare(P, names):
    D = Ctx()
    for (n, shp) in PARAMS:
        if n in names:
            setattr(D, n, P.dram(n, shp, F32, "ExternalInput"))
    return D


def kv_stage(P, C, D, xT, xTr, KTo, Vo, Flo, Fto):
    P.push()
    pst = [P.psum([128, 8, 128], BF16, "pst%d" % j) for j in range(2)]
    gain = load_bcast(P, D.kv_norm.t[:], D.kv_norm, 1024, "kg")
    rmsnorm_T(P, C, gain, xT, xTr, pst)
    P.pop()
    P.push()
    pm = [P.psum([128, 512], F32, "pm%d" % j) for j in range(4)]
    wsrc = D.w_kvf.t.rearrange("(c p) n -> p c n", p=128)
    wk = P.sbuf([128, 8, 1024], BF16, "wk")
    wv = P.sbuf([128, 8, 1024], BF16, "wv")
    wf = P.sbuf([128, 8, 16], BF16, "wf")
    P.dma("pool", wk[:], V(D.w_kvf, wsrc[:, :, 0:1024]))
    P.dma("pool", wv[:], V(D.w_kvf, wsrc[:, :, 1024:2048]))
    P.dma("pool", wf[:], V(D.w_kvf, wsrc[:, :, 2048:2064]))
    ktb = [P.sbuf([64, 2048], BF16, "ktb%d" % j) for j in range(2)]
    vb = [P.sbuf([128, 1024], BF16, "vb%d" % j) for j in range(2)]
    n = 0
    for h in range(16):
        kt = ktb[h % 2]
        for tg in range(4):
            ps = pm[n % 4]
            n += 1
            for c in range(8):
                P.I("pe", "matmul", out=ps[0:64, :], lhsT=wk[:, c, h * 64:(h + 1) * 64],
                    rhs=xT[:, c, tg * 512:(tg + 1) * 512], start=(c == 0), stop=(c == 7), xr=xTr[tg * 4:tg * 4 + 4])
            if n % 2:
                P.I("act", "activation", out=kt[:, tg * 512:(tg + 1) * 512], in_=ps[0:64, :], func=AF.Copy)
            else:
                P.I("dve", "tensor_copy", out=kt[:, tg * 512:(tg + 1) * 512], in_=ps[0:64, :])
        P.dma("sp", V(KTo, KTo.t[h]), kt[:])
    for tt in range(16):
        v_ = vb[tt % 2]
        for half in range(2):
            ps = pm[n % 4]
            n += 1
            for c in range(8):
                P.I("pe", "matmul", out=ps[:], lhsT=V(xTr[tt], xT.t[:, c, tt * 128:(tt + 1) * 128]),
                    rhs=wv[:, c, half * 512:(half + 1) * 512], start=(c == 0), stop=(c == 7))
            if n % 2:
                P.I("act", "activation", out=v_[:, half * 512:(half + 1) * 512], in_=ps[:], func=AF.Copy)
            else:
                P.I("dve", "tensor_copy", out=v_[:, half * 512:(half + 1) * 512], in_=ps[:])
        P.dma("sp", V(Vo, Vo.t[tt * 128:(tt + 1) * 128, :]), v_[:])
    LF = P.sbuf([16, 2048], F32, "LF")
    nbf = P.sbuf([16, 1], F32, "nbf")
    P.dma("sp", nbf[:], V(D.b_f, D.b_f.t.rearrange("(h o) -> h o", o=1)))
    P.I("dve", "tensor_scalar", out=nbf[:], in0=nbf[:], scalar1=-1.0, scalar2=None, op0=ALU.mult)
    et = [P.sbuf([16, 512], F32, "et%d" % j) for j in range(2)]
    for tg in range(4):
        ps = pm[n % 4]
        n += 1
        for c in range(8):
            P.I("pe", "matmul", out=ps[0:16, :], lhsT=wf[:, c, :], rhs=xT[:, c, tg * 512:(tg + 1) * 512],
                start=(c == 0), stop=(c == 7), xr=xTr[tg * 4:tg * 4 + 4])
        P.I("act", "activation", out=et[tg % 2][:], in_=ps[0:16, :], func=AF.Exp, bias=nbf[:, 0:1], scale=-1.0)
        P.I("act", "activation", out=LF[:, tg * 512:(tg + 1) * 512], in_=et[tg % 2][:], func=AF.Ln, bias=1.0, scale=1.0)
    ones = P.sbuf([16, 128], F32, "ones16")
    _memset(P, "pool", ones[:], 1.0)
    PI = P.sbuf([16, 2, 128], F32, "PI")
    EX = P.sbuf([16, 2, 128], F32, "EX")
    L4 = LF.t[:, :].rearrange("h (b i k) -> h b i k", b=2, i=8)
    for b in range(2):
        for i in range(1, 8):
            P.I("dve", "tensor_tensor", out=LF.v(L4[:, b, i, :]), in0=LF.v(L4[:, b, i, :]), in1=LF.v(L4[:, b, i - 1, :]),
                op=ALU.add)
        init = 0.0 if b == 0 else PI[:, 0, 127:128]
        P.I("dve", "tensor_tensor_scan", out=PI[:, b, :], data0=ones[:], data1=LF.v(L4[:, b, 7, :]), initial=init,
            op0=ALU.mult, op1=ALU.add)
        P.I("dve", "tensor_tensor", out=EX[:, b, :], in0=PI[:, b, :], in1=LF.v(L4[:, b, 7, :]), op=ALU.subtract)
        P.I("dve", "tensor_tensor", out=LF.v(L4[:, b, :, :]), in0=LF.v(L4[:, b, :, :]),
            in1=EX.v(EX.t[:, b, :].unsqueeze(1).to_broadcast([16, 8, 128])), op=ALU.add)
    P.I("dve", "tensor_scalar", out=LF[:], in0=LF[:], scalar1=-1.0, scalar2=None, op0=ALU.mult)
    P.dma("sp", Flo[:], LF[:])
    P.dma("sp", Fto[:], LF[:, 2047:2048], allow_slow_non_contiguous=True)
    P.pop()


def split3(P, src, dst_dram_views, tmp):
    hi, r1, mid, lo = tmp
    n = None
    P.I("dve", "tensor_copy", out=hi[:], in_=src)
    P.I("dve", "tensor_tensor", out=r1[:], in0=src, in1=hi[:], op=ALU.subtract)
    P.I("dve", "tensor_copy", out=mid[:], in_=r1[:])
    P.I("dve", "tensor_tensor", out=r1[:], in0=r1[:], in1=mid[:], op=ALU.subtract)
    P.I("dve", "tensor_copy", out=lo[:], in_=r1[:])
    for piece, dv in zip((hi, mid, lo), dst_dram_views):
        P.dma("sp", dv, piece[:])


def att_prep(P, C, A):
    P.push()
    ft = P.sbuf([16, 4], F32, "ft")
    sq = P.sbuf([16, 4], F32, "sq")
    off = P.sbuf([16, 4], F32, "off")
    oo = P.sbuf([16, 1], F32, "oo")
    P.dma("sp", ft[:], A.Ftg[:])
    P.dma("sp", sq[:], A.selq[:])
    _memset(P, "pool", off[:], 0.0)
    for j in range(1, 4):
        P.I("dve", "tensor_tensor", out=off[:, j:j + 1], in0=off[:, j - 1:j], in1=ft[:, j - 1:j], op=ALU.add)
    P.I("dve", "tensor_tensor", out=sq[:], in0=sq[:], in1=ft[:], op=ALU.mult)
    P.I("dve", "tensor_reduce", out=oo[:], in_=sq[:], axis=mybir.AxisListType.X, op=ALU.add)
    fl = [P.sbuf([16, 2048], F32, "fl%d" % j) for j in range(2)]
    tmp = (P.sbuf([16, 2048], BF16, "s_hi"), P.sbuf([16, 2048], F32, "s_r1"), P.sbuf([16, 2048], BF16, "s_mid"),
           P.sbuf([16, 2048], BF16, "s_lo"))
    for j in range(4):
        f = fl[j % 2]
        P.dma("act", f[:], V(A.Flg, A.Flg.t[j]))
        P.I("dve", "tensor_scalar", out=f[:], in0=f[:], scalar1=off[:, j:j + 1], scalar2=-1.0, op0=ALU.add, op1=ALU.mult)
        split3(P, f[:], [V(A.Hd, A.Hd.t[:, r, j * 2048:(j + 1) * 2048]) for r in range(3)], tmp)
    f = fl[0]
    P.dma("act", f[:], A.Fown[:])
    P.I("dve", "tensor_scalar", out=f[:], in0=f[:], scalar1=oo[:, 0:1], scalar2=None, op0=ALU.add)
    split3(P, f[:], [V(A.Gd, A.Gd.t[:, r, :]) for r in range(3)], tmp)
    P.pop()


def attention(P, C, j, D, A, xT, xTr):
    P.push()
    pss = [P.psum([128, 512], F32, "pss%d" % i) for i in range(4)]
    ppo = [P.psum([128, 512], F32, "ppo%d" % i) for i in range(2)]
    ps2 = [P.psum([128, 512], F32, "ps2%d" % i) for i in range(2)]
    psb = ps2[1]
    AB = P.sbuf([128, 32], F32, "AB")
    P.dma("sp", AB[:], A.AB[:])
    KA = [P.sbuf([128, 8192], BF16, "KA%d" % i) for i in range(2)]
    VA = [P.sbuf([128, 64, 65], BF16, "VA%d" % i) for i in range(2)]
    QA = [P.sbuf([128, 2048], BF16, "QA%d" % i) for i in range(2)]
    for i in range(2):
        _memset(P, "pool", KA[i][:], 0.0)
        _memset(P, "pool", QA[i][:], 0.0)
        _memset(P, "pool", KA[i][64:70, :], 1.0)
        _memset(P, "pool", QA[i][64:70, :], 1.0)
        _memset(P, "pool", VA[i][:, :, 64:65], 1.0)
    zt = P.sbuf([128, 512], BF16, "zt")
    _memset(P, "pool", zt[:], 0.0)
    TRI = P.sbuf([128, 8, 2, 512], BF16, "TRI")
    for ik in range(8):
        for a in range(2):
            P.I("pool", "affine_select", out=TRI.v(TRI.t[:, ik, a, :].rearrange("p (i k) -> p i k", i=4)),
                in_=zt.v(zt.t[:, :].rearrange("p (i k) -> p i k", i=4)), pattern=[[1, 4], [8, 128]], base=4 * a - ik,
                channel_multiplier=-8, compare_op=ALU.is_ge, fill=NEG)
    ones1 = P.sbuf([128, 64], F32, "ones1")
    _memset(P, "pool", ones1[:], 1.0)
    wq = [P.sbuf([128, 8, 64], BF16, "wq%d" % i) for i in range(2)]
    wo = [P.sbuf([128, 1024], BF16, "wo%d" % i) for i in range(2)]
    for i in range(2):
        _memset(P, "pool", wo[i][:], 0.0)
    tmpf = [P.sbuf([128, 512], F32, "tmpf%d" % i) for i in range(4)]
    pT = [P.sbuf([128, 512], BF16, "pT%d" % i) for i in range(8)]
    osb = [P.sbuf([128, 512], F32, "osb%d" % i) for i in range(2)]
    rc = [P.sbuf([128, 512], F32, "rc%d" % i) for i in range(2)]
    otn = [P.sbuf([128, 512], BF16, "otn%d" % i) for i in range(2)]
    for i in range(2):
        _memset(P, "pool", otn[i][:], 0.0)
    wqv = D.attn_wq.t[j].rearrange("(c p) n -> p c n", p=128)
    vgv = A.Vg.t.rearrange("(kt p) (h d) -> p kt h d", p=128, h=16)
    def prep(h):
        ka, va, qa = KA[h % 2], VA[h % 2], QA[h % 2]
        P.dma("sp", ka[0:64, :], V(A.KTg, A.KTg.t[h]))
        P.dma("sp", ka[67:70, :], V(A.Hd, A.Hd.t[h]))
        for q4 in range(4):
            P.dma("act", va[:, q4 * 16:(q4 + 1) * 16, 0:64], V(A.Vg, vgv[:, q4 * 16:(q4 + 1) * 16, h, :]))
        P.dma("sp", qa[64:67, :], V(A.Gd, A.Gd.t[h]))
        P.dma("pool", wq[h % 2][:], V(D.attn_wq, wqv[:, :, h * 64:(h + 1) * 64]))
        P.dma("pool", wo[h % 2][0:64, :], V(D.attn_wo, D.attn_wo.t[j, h * 64:(h + 1) * 64, :]))
        for tg in range(4):
            ps = ps2[tg % 2]
            for c in range(8):
                P.I("pe", "matmul", out=ps[0:64, :], lhsT=wq[h % 2][:, c, :], rhs=xT[:, c, tg * 512:(tg + 1) * 512],
                    start=(c == 0), stop=(c == 7), xr=xTr[tg * 4:tg * 4 + 4])
            P.I("act", "activation", out=qa[0:64, tg * 512:(tg + 1) * 512], in_=ps[0:64, :], func=AF.Copy, scale=0.125)

    def score(h, qg, kt, idx):
        ka, qa = KA[h % 2], QA[h % 2]
        b, a = qg // 2, qg % 2
        kb, ik = kt // 8, kt % 8
        ps, tf, pt_ = pss[idx % 4], tmpf[idx % 4], pT[idx % 8]
        P.I("pe", "matmul", out=ps[:], lhsT=ka[:, kt * 128:(kt + 1) * 128], rhs=qa[:, qg * 512:(qg + 1) * 512],
            start=True, stop=True)
        ci = kb * 2 + b
        if kb % 2 == b:
            P.I("dve", "scalar_tensor_tensor", out=tf[:], in0=TRI[:, ik, a, :], scalar=AB[:, ci:ci + 1], in1=ps[:],
                op0=ALU.mult, op1=ALU.add)
            P.I("act", "activation", out=pt_[:], in_=tf[:], func=AF.Exp, bias=AB[:, 16 + ci:17 + ci], scale=1.0)
        else:
            P.I("act", "activation", out=pt_[:], in_=ps[:], func=AF.Exp, bias=AB[:, 16 + ci:17 + ci], scale=1.0)

    def pv(h, qg, kt, idx):
        P.I("pe", "matmul", out=ppo[qg % 2][0:65, :], lhsT=VA[h % 2][:, kt, :], rhs=pT[idx % 8][:], start=(kt == 0),
            stop=(kt == 63))

    def ep1(h, qg):
        P.I("act", "activation", out=osb[qg % 2][0:65, :], in_=ppo[qg % 2][0:65, :], func=AF.Copy)
        P.I("dve", "reciprocal", out=rc[qg % 2][64:65, :], in_=osb[qg % 2][64:65, :])

    def ep2(h, qg):
        P.I("pe", "matmul", out=psb[0:64, :], lhsT=ones1[64:65, 0:64], rhs=rc[qg % 2][64:65, :], start=True, stop=True)
        P.I("dve", "tensor_tensor", out=otn[qg % 2][0:64, :], in0=osb[qg % 2][0:64, :], in1=psb[0:64, :], op=ALU.mult)

    def ep3(h, qg):
        on = otn[qg % 2]
        for t4 in range(4):
            tt = qg * 4 + t4
            bb, ii = tt // 8, tt % 8
            for half in range(2):
                p2 = ps2[(t4 * 2 + half) % 2]
                P.I("pe", "matmul", out=p2[:], lhsT=on[:, t4 * 128:(t4 + 1) * 128],
                    rhs=wo[h % 2][:, half * 512:(half + 1) * 512], start=True, stop=True)
                hv = V(C.h[bb][ii], C.hb[bb].t[:, ii, half * 512:(half + 1) * 512])
                P.I("dve", "tensor_tensor", out=hv, in0=hv, in1=p2[:], op=ALU.add)

    units = [(h, qg, kt) for h in range(16) for qg in range(4) for kt in range(64)]
    n = len(units)
    DEPTH = 6
    events = {}

    def at(i_, fn):
        events.setdefault(i_, []).append(fn)
    prep(0)
    for idx in range(n + DEPTH + 8):
        for fn in events.pop(idx, []):
            fn()
        if idx < n:
            h, qg, kt = units[idx]
            if qg == 0 and kt == 0 and h + 1 < 16:
                at(idx + 96, lambda h=h: prep(h + 1))
            score(h, qg, kt, idx)
        jx = idx - DEPTH
        if 0 <= jx < n:
            h, qg, kt = units[jx]
            pv(h, qg, kt, jx)
            if kt == 63:
                ep1(h, qg)
                at(idx + 2, lambda h=h, qg=qg: ep2(h, qg))
                at(idx + 4, lambda h=h, qg=qg: ep3(h, qg))
    assert not events
    P.pop()


def final_norm(P, C, D, yout):
    P.push()
    gain = load_bcast(P, D.final_norm.t[:], D.final_norm, 1024, "fg")
    ss = P.sbuf([128, 16], F32, "fss")
    rs = P.sbuf([128, 16], F32, "frs")
    junk = P.sbuf([128, 1024], F32, "fjunk")
    ob = [P.sbuf([128, 1024], F32, "fob%d" % j) for j in range(2)]
    for b in range(2):
        for i in range(8):
            jx = b * 8 + i
            P.I("act", "activation", out=junk[:], in_=V(C.h[b][i], C.hb[b].t[:, i, :]), func=AF.Square,
                accum_out=ss[:, jx:jx + 1])
    P.I("dve", "tensor_scalar", out=rs[:], in0=ss[:], scalar1=1.0 / 1024, scalar2=1e-6, op0=ALU.mult, op1=ALU.add)
    P.I("act", "activation", out=rs[:], in_=rs[:], func=AF.Sqrt)
    P.I("dve", "reciprocal", out=rs[:], in_=rs[:])
    yv = yout.t.rearrange("(b k i) d -> b k i d", b=2, k=128)
    for b in range(2):
        for i in range(8):
            jx = b * 8 + i
            o = ob[jx % 2]
            P.I("dve", "scalar_tensor_tensor", out=o[:], in0=V(C.h[b][i], C.hb[b].t[:, i, :]), scalar=rs[:, jx:jx + 1],
                in1=gain[:], op0=ALU.mult, op1=ALU.mult)
            P.dma("sp", V(yout, yv[b, :, i, :]), o[:])
    P.pop()


PARAMS = [("mix_norm", [4, 1024]), ("mlp_norm", [4, 1024]), ("mlp_w1", [4, 1024, 4096]), ("mlp_w2", [4, 4096, 1024]),
          ("ssm_log_dt", [2, 64]), ("ssm_a_re", [2, 64, 64]), ("ssm_a_im", [2, 64, 64]),
          ("ssm_b_re", [2, 64, 64, 16]), ("ssm_b_im", [2, 64, 64, 16]), ("ssm_c_re", [2, 64, 16, 64]),
          ("ssm_c_im", [2, 64, 16, 64]), ("ssm_d", [2, 1024]), ("ssm_w_glu", [2, 1024, 2048]), ("kv_norm", [1024]),
          ("w_kvf", [1024, 2064]), ("b_f", [16]), ("attn_wq", [2, 1024, 1024]), ("attn_wo", [2, 1024, 1024]),
          ("final_norm", [1024])]
S5_NAMES = ["mix_norm", "mlp_norm", "mlp_w1", "mlp_w2", "ssm_log_dt", "ssm_a_re", "ssm_a_im", "ssm_b_re",
            "ssm_b_im", "ssm_c_re", "ssm_c_im", "ssm_d", "ssm_w_glu"]


def declare(P, names):
    D = Ctx()
    for (n, shp) in PARAMS:
        if n in names:
            setattr(D, n, P.dram(n, shp, F32, "ExternalInput"))
    return D


def build_s5(stage):
    nc = bass.Bass("TRN2", target_bir_lowering=False)
    with ExitStack() as st:
        P = Prog(nc, st)
        C = Ctx()
        names = list(S5_NAMES) + (["kv_norm", "w_kvf", "b_f"] if stage == 3 else [])
        if stage == 1:
            names = [n for n in names if n not in ("mlp_norm", "mlp_w1", "mlp_w2", "ssm_w_glu")]
        D = declare(P, names)
        hin = P.dram("hin", [2048, 1024], F32, "ExternalInput")
        make_consts(P, C)
        load_h(P, C, hin)
        xT = P.sbuf([128, 8, 2048], BF16, "xT")
        xTr = [xT.sub("xTr%d" % j) for j in range(16)]
        if stage == 1:
            Fout = P.dram("Fout", [128, 64], F32, "ExternalOutput")
            lA = 0
        else:
            Fall = P.dram("Fall", [8, 128, 64], F32, "ExternalInput")
            selS = P.dram("selS", [128, 24], F32, "ExternalInput")
            hout = P.dram("hout", [2048, 1024], F32, "ExternalOutput")
            lB = stage - 2
            if stage == 2:
                Fout = P.dram("Fout", [128, 64], F32, "ExternalOutput")
                lA = 1
            else:
                KTo = P.dram("KTo", [16, 64, 2048], BF16, "ExternalOutput")
                Vo = P.dram("Vo", [2048, 1024], BF16, "ExternalOutput")
                Flo = P.dram("Flo", [16, 2048], F32, "ExternalOutput")
                Fto = P.dram("Fto", [16, 1], F32, "ExternalOutput")

        need = {1: [0], 2: [0, 1], 3: [1]}[stage]
        SP_ = {l: s5_params(P, C, l, D) for l in need}

        def stageA(l):
            P.push()
            pst = [P.psum([128, 8, 128], BF16, "pst%d" % j) for j in range(2)]
            gain = load_bcast(P, D.mix_norm.t[l, :], D.mix_norm, 1024, "g")
            rmsnorm_T(P, C, gain, xT, xTr, pst)
            P.pop()
            return SP_[l]

        if stage >= 2:
            S = stageA(lB)
            P.push()
            s5_core(P, C, lB, D, S, xT, xTr, "B", Fall=Fall, selS=selS)
            P.pop()
            P.push()
            pm = [P.psum([128, 512], F32, "pm%d" % j) for j in range(4)]
            glu(P, C, lB, D, xT, xTr, pm)
            P.pop()
            P.push()
            pst = [P.psum([128, 8, 128], BF16, "pst%d" % j) for j in range(2)]
            pm = [P.psum([128, 512], F32, "pm%d" % j) for j in range(4)]
            mlp(P, C, lB, D, xT, xTr, pst, pm)
            P.pop()
            store_h(P, C, hout)
        if stage <= 2:
            S = stageA(lA)
            P.push()
            s5_core(P, C, lA, D, S, xT, xTr, "A", Fout=Fout)
            P.pop()
        else:
            kv_stage(P, C, D, xT, xTr, KTo, Vo, Flo, Fto)
        P.finish()
    return nc, names


def build_att():
    nc = bass.Bass("TRN2", target_bir_lowering=False)
    with ExitStack() as st:
        P = Prog(nc, st)
        C = Ctx()
        names = ["mix_norm", "mlp_norm", "mlp_w1", "mlp_w2", "attn_wq", "attn_wo", "final_norm"]
        D = declare(P, names)
        A = Ctx()
        hin = P.dram("hin", [2048, 1024], F32, "ExternalInput")
        A.KTg = P.dram("KTg", [16, 64, 8192], BF16, "ExternalInput")
        A.Vg = P.dram("Vg", [8192, 1024], BF16, "ExternalInput")
        A.Flg = P.dram("Flg", [4, 16, 2048], F32, "ExternalInput")
        A.Ftg = P.dram("Ftg", [16, 4], F32, "ExternalInput")
        A.Fown = P.dram("Fown", [16, 2048], F32, "ExternalInput")
        A.selq = P.dram("selq", [16, 4], F32, "ExternalInput")
        A.AB = P.dram("ABsel", [128, 32], F32, "ExternalInput")
        A.Hd = P.dram("Hd", [16, 3, 8192], BF16, "Internal")
        A.Gd = P.dram("Gd", [16, 3, 2048], BF16, "Internal")
        yout = P.dram("yout", [2048, 1024], F32, "ExternalOutput")
        make_consts(P, C)
        load_h(P, C, hin)
        xT = P.sbuf([128, 8, 2048], BF16, "xT")
        xTr = [xT.sub("xTr%d" % j) for j in range(16)]
        att_prep(P, C, A)
        for j in range(2):
            l = 2 + j
            P.push()
            pst = [P.psum([128, 8, 128], BF16, "pst%d" % k) for k in range(2)]
            gain = load_bcast(P, D.mix_norm.t[l, :], D.mix_norm, 1024, "g")
            rmsnorm_T(P, C, gain, xT, xTr, pst)
            P.pop()
            attention(P, C, j, D, A, xT, xTr)
            P.push()
            pst = [P.psum([128, 8, 128], BF16, "pst%d" % k) for k in range(2)]
            pm = [P.psum([128, 512], F32, "pm%d" % k) for k in range(4)]
            mlp(P, C, l, D, xT, xTr, pst, pm)
            P.pop()
        final_norm(P, C, D, yout)
        P.finish()
    return nc, names


def build_fused(debug=False):
    nc = bass.Bass("TRN2", target_bir_lowering=False)
    with ExitStack() as st:
        P = Prog(nc, st)
        C = Ctx()
        names = [n for n, _ in PARAMS]
        D = declare(P, names)
        xfull = P.dram("xfull", [8192, 1024], F32, "ExternalInput")
        selseg = P.dram("selseg", [128, 4], F32, "ExternalInput")
        A = Ctx()
        A.selq = P.dram("selq", [16, 4], F32, "ExternalInput")
        A.AB = P.dram("ABsel", [128, 32], F32, "ExternalInput")
        A.KTg = P.dram("KTd", [16, 64, 8192], BF16, "Internal")
        A.Vg = P.dram("Vd", [8192, 1024], BF16, "Internal")
        dk = "ExternalOutput" if debug else "Internal"
        A.Flg = P.dram("Fld", [4, 16, 2048], F32, dk)
        A.Ftg = P.dram("Ftd", [16, 4], F32, dk)
        A.Fown = P.dram("Fownd", [16, 2048], F32, dk)
        A.Hd = P.dram("Hd", [16, 3, 8192], BF16, "Internal")
        A.Gd = P.dram("Gd", [16, 3, 2048], BF16, "Internal")
        H2d = P.dram("H2d", [4, 2048, 1024], F32, dk)
        TabW = [P.dram("TabW%d" % l, [8, 128, 2048], BF16, "Internal") for l in range(2)]
        TabC = [P.dram("TabC%d" % l, [8, 128, 2304], BF16, "Internal") for l in range(2)]
        TabK = [P.dram("TabK%d" % l, [8, 128, 256], BF16, "Internal") for l in range(2)]
        yout = P.dram("yout", [2048, 1024], F32, "ExternalOutput")
        make_consts(P, C)
        xT = P.sbuf([128, 8, 2048], BF16, "xT")
        xTr = [xT.sub("xTr%d" % j) for j in range(16)]
        C.hb = [P.sbuf([128, 8, 1024], F32, "h%d" % b) for b in range(2)]
        C.h = [[C.hb[b].sub("h%d_%d" % (b, i)) for i in range(8)] for b in range(2)]
        P.push()
        SPr_ = {l: s5_params(P, C, l, D) for l in range(2)}
        for l in range(2):
            s5_tables(P, C, SPr_[l], TabW[l], TabC[l], TabK[l])
        carry = [P.sbuf([128, 32, 2], F32, "carry%d" % l) for l in range(2)]
        for l in range(2):
            _memset(P, "pool", carry[l][:], 0.0)
        for seg in range(4):
            xs = Buf(xfull.t[seg * 2048:(seg + 1) * 2048, :], "xseg%d" % seg)
            load_h(P, C, xs)
            for l in range(2):
                P.push()
                pst = [P.psum([128, 8, 128], BF16, "pst%d" % j) for j in range(2)]
                gain = load_bcast(P, D.mix_norm.t[l, :], D.mix_norm, 1024, "g")
                rmsnorm_T(P, C, gain, xT, xTr, pst)
                P.pop()
                P.push()
                s5_core_pipe(P, C, l, D, SPr_[l], xT, carry[l], TabW[l], TabC[l], TabK[l])
                P.pop()
                P.push()
                pm = [P.psum([128, 512], F32, "pm%d" % j) for j in range(4)]
                glu(P, C, l, D, xT, xTr, pm)
                P.pop()
                P.push()
                pst = [P.psum([128, 8, 128], BF16, "pst%d" % j) for j in range(2)]
                pm = [P.psum([128, 512], F32, "pm%d" % j) for j in range(4)]
                mlp(P, C, l, D, xT, xTr, pst, pm)
                P.pop()
            kv_stage(P, C, D, xT, xTr,
                     Buf(A.KTg.t[:, :, seg * 2048:(seg + 1) * 2048], "ktseg"), Buf(A.Vg.t[seg * 2048:(seg + 1) * 2048, :], "vseg"),
                     Buf(A.Flg.t[seg], "flseg"), Buf(A.Ftg.t[:, seg:seg + 1], "ftseg"))
            store_h(P, C, Buf(H2d.t[seg], "h2seg"))
            P.barrier()
        P.pop()
        P.push()
        sel = P.sbuf([128, 4], F32, "sel")
        P.dma("sp", sel[:], selseg[:])
        tmp = [P.sbuf([128, 1024], F32, "seltmp%d" % j) for j in range(3)]
        n = 0
        for b in range(2):
            for i in range(8):
                hv = V(C.h[b][i], C.hb[b].t[:, i, :])
                for seg in range(4):
                    t = tmp[n % 3]
                    n += 1
                    sv = H2d.t[seg].rearrange("(b k i) d -> b k i d", b=2, k=128)[b, :, i, :]
                    P.dma(("sp", "act")[n % 2], t[:], V(H2d, sv))
                    if seg == 0:
                        P.I("dve", "tensor_scalar", out=hv, in0=t[:], scalar1=sel[:, 0:1], scalar2=None, op0=ALU.mult)
                    else:
                        P.I("dve", "scalar_tensor_tensor", out=hv, in0=t[:], scalar=sel[:, seg:seg + 1], in1=hv,
                            op0=ALU.mult, op1=ALU.add)
        facc = P.sbuf([16, 2048], F32, "facc")
        ftmp = [P.sbuf([16, 2048], F32, "ftmp%d" % j) for j in range(2)]
        for seg in range(4):
            t = ftmp[seg % 2]
            P.dma("sp", t[:], V(A.Flg, A.Flg.t[seg]))
            if seg == 0:
                P.I("dve", "tensor_scalar", out=facc[:], in0=t[:], scalar1=sel[0:16, 0:1], scalar2=None, op0=ALU.mult)
            else:
                P.I("dve", "scalar_tensor_tensor", out=facc[:], in0=t[:], scalar=sel[0:16, seg:seg + 1], in1=facc[:],
                    op0=ALU.mult, op1=ALU.add)
        P.dma("sp", A.Fown[:], facc[:])
        P.pop()
        att_prep(P, C, A)
        for j in range(2):
            l = 2 + j
            P.push()
            pst = [P.psum([128, 8, 128], BF16, "pst%d" % k) for k in range(2)]
            gain = load_bcast(P, D.mix_norm.t[l, :], D.mix_norm, 1024, "g")
            rmsnorm_T(P, C, gain, xT, xTr, pst)
            P.pop()
            attention(P, C, j, D, A, xT, xTr)
            P.push()
            pst = [P.psum([128, 8, 128], BF16, "pst%d" % k) for k in range(2)]
            pm = [P.psum([128, 512], F32, "pm%d" % k) for k in range(4)]
            mlp(P, C, l, D, xT, xTr, pst, pm)
            P.pop()
        final_norm(P, C, D, yout)
        P.finish()
    return nc, names


def sel_state(core):
    s = np.zeros((8, 3), np.float32)
    b, q = core // 4, core % 4
    for qq in range(q):
        s[4 * b + qq, q - 1 - qq] = 1.0
    return np.tile(s.reshape(1, 24), (128, 1))


def sel_att(core):
    q = core % 4
    selq = np.zeros((16, 4), np.float32)
    selq[:, :q] = 1.0
    ab = np.zeros((32,), np.float32)
    for kb in range(8):
        for qb in range(2):
            g = 2 * q + qb
            ab[kb * 2 + qb] = 1.0 if kb == g else 0.0
            ab[16 + kb * 2 + qb] = NEG if kb > g else 0.0
    return selq, np.tile(ab.reshape(1, 32), (128, 1))


_CACHE = {}


def get_prog(key, fn):
    if key not in _CACHE:
        _CACHE[key] = fn()
    return _CACHE[key]


def run(nc, in_maps):
    res = run_bass_kernel_spmd(nc, in_maps, core_ids=list(range(8)))
    return res.results


def kernel_unfused(**inputs):
    inp = {k: np.ascontiguousarray(np.asarray(v, dtype=np.float32)) for k, v in inputs.items()}
    x = inp["x"].reshape(8, 2048, 1024)
    nc1, n1 = get_prog("s1", lambda: build_s5(1))
    r1 = run(nc1, [dict({n: inp[n] for n in n1}, hin=x[c]) for c in range(8)])
    F0 = np.stack([r["Fout"] for r in r1])
    nc2, n2 = get_prog("s2", lambda: build_s5(2))
    r2 = run(nc2, [dict({n: inp[n] for n in n2}, hin=x[c], Fall=F0, selS=sel_state(c)) for c in range(8)])
    F1 = np.stack([r["Fout"] for r in r2])
    h1 = [r["hout"] for r in r2]
    nc3, n3 = get_prog("s3", lambda: build_s5(3))
    r3 = run(nc3, [dict({n: inp[n] for n in n3}, hin=h1[c], Fall=F1, selS=sel_state(c)) for c in range(8)])
    nc4, n4 = get_prog("s4", build_att2)
    r4 = run(nc4, [dict({n: inp[n] for n in n4}, **att2_inputs(c, r3)) for c in range(8)])
    y = np.zeros((2, 8, 1024, 1024), np.float32)
    for c in range(8):
        bt, q = c // 4, c % 4
        y[bt, q] = r4[c]["yout"][:1024]
        y[bt, 7 - q] = r4[c]["yout"][1024:]
    y = y.reshape(2, 8192, 1024)
    return y


NSLOT = 12


def att_prep2(P, C, A):
    P.push()
    ft = P.sbuf([16, 4], F32, "ft")
    off = P.sbuf([16, 4], F32, "off")
    sk = P.sbuf([16, NSLOT * 4], F32, "sk")
    sq = P.sbuf([16, 8], F32, "sq")
    offs = P.sbuf([16, NSLOT + 2], F32, "offs")
    P.dma("sp", ft[:], A.Ftg[:])
    P.dma("sp", sk[:], A.selk[:])
    P.dma("sp", sq[:], A.selq[:])
    _memset(P, "pool", off[:], 0.0)
    for j in range(1, 4):
        P.I("dve", "tensor_tensor", out=off[:, j:j + 1], in0=off[:, j - 1:j], in1=ft[:, j - 1:j], op=ALU.add)
    for s_ in range(NSLOT + 2):
        src = sk[:, s_ * 4:(s_ + 1) * 4] if s_ < NSLOT else sq[:, (s_ - NSLOT) * 4:(s_ - NSLOT + 1) * 4]
        P.I("dve", "tensor_tensor", out=src, in0=src, in1=ft[:], op=ALU.mult)
        P.I("dve", "tensor_reduce", out=offs[:, s_:s_ + 1], in_=src, axis=mybir.AxisListType.X, op=ALU.add)
    fl = [P.sbuf([16, 1024], F32, "fl%d" % j) for j in range(2)]
    tmp = (P.sbuf([16, 1024], BF16, "s_hi"), P.sbuf([16, 1024], F32, "s_r1"), P.sbuf([16, 1024], BF16, "s_mid"),
           P.sbuf([16, 1024], BF16, "s_lo"))
    for s_ in range(NSLOT):
        f = fl[s_ % 2]
        P.dma("act", f[:], V(A.Fkg, A.Fkg.t[s_]))
        P.I("dve", "tensor_scalar", out=f[:], in0=f[:], scalar1=offs[:, s_:s_ + 1], scalar2=-1.0, op0=ALU.add, op1=ALU.mult)
        split3(P, f[:], [V(A.Hd, A.Hd.t[:, r, s_ * 1024:(s_ + 1) * 1024]) for r in range(3)], tmp)
    for lb in range(2):
        f = fl[lb % 2]
        P.dma("act", f[:], V(A.Fq, A.Fq.t[:, lb * 1024:(lb + 1) * 1024]))
        P.I("dve", "tensor_scalar", out=f[:], in0=f[:], scalar1=offs[:, NSLOT + lb:NSLOT + lb + 1], scalar2=None,
            op0=ALU.add)
        split3(P, f[:], [V(A.Gd, A.Gd.t[:, r, lb * 1024:(lb + 1) * 1024]) for r in range(3)], tmp)
    P.pop()


def attention2(P, C, j, D, A, xT, xTr):
    P.push()
    NK = NSLOT * 1024
    pss = [P.psum([128, 2, 512], F32, "pss%d" % i) for i in range(2)]
    ppo = [P.psum([128, 512], F32, "ppo%d" % i) for i in range(2)]
    pmi = [P.psum([128, 512], F32, "pmi%d" % i) for i in range(2)]
    BM = P.sbuf([128, NSLOT], F32, "BM")
    P.dma("sp", BM[:], A.Bm[:])
    KA = [P.sbuf([70, 8192], BF16, "KA%d" % i) for i in range(2)]
    VA = [P.sbuf([128, 64, 65], BF16, "VA%d" % i) for i in range(2)]
    QA = [P.sbuf([70, 2048], BF16, "QA%d" % i) for i in range(2)]
    for i in range(2):
        _memset(P, "pool", KA[i][64:70, :], 1.0)
        _memset(P, "pool", QA[i][64:70, :], 1.0)
        _memset(P, "pool", VA[i][:, :, 64:65], 1.0)
    zt = P.sbuf([128, 512], BF16, "zt")
    _memset(P, "pool", zt[:], 0.0)
    TRI = P.sbuf([128, 2, 8, 512], BF16, "TRI")
    for ik in range(8):
        for a in range(2):
            P.I("pool", "affine_select", out=TRI.v(TRI.t[:, a, ik, :].rearrange("p (i k) -> p i k", i=4)),
                in_=zt.v(zt.t[:, :].rearrange("p (i k) -> p i k", i=4)), pattern=[[1, 4], [8, 128]], base=4 * a - ik,
                channel_multiplier=-8, compare_op=ALU.is_ge, fill=NEG)
    ones1 = P.sbuf([128, 64], F32, "ones1")
    _memset(P, "pool", ones1[:], 1.0)
    wq = [P.sbuf([128, 8, 64], BF16, "wq%d" % i) for i in range(2)]
    wo = [P.sbuf([64, 1024], BF16, "wo%d" % i) for i in range(2)]
    tmpf = [P.sbuf([128, 2, 512], F32, "tmpf0")] * 2
    NPT = 4
    pT = [P.sbuf([128, 2, 512], BF16, "pT%d" % i) for i in range(NPT)]
    osb = [P.sbuf([128, 512], F32, "osb%d" % i) for i in range(2)]
    rc = [P.sbuf([128, 512], F32, "rc0")] * 2
    otn = [P.sbuf([64, 512], BF16, "otn%d" % i) for i in range(2)]
    wqv = D.attn_wq.t[j].rearrange("(c p) n -> p c n", p=128)
    vgv = A.Vg.t.rearrange("(kt p) (h d) -> p kt h d", p=128, h=16)

    def prep(p):
        h, lb = p // 2, p % 2
        ka, va = KA[p % 2], VA[p % 2]
        s0, ns = (0, 4) if lb == 0 else (4, 8)
        P.dma("sp", ka[0:64, 0:ns * 1024], V(A.KTg, A.KTg.t[h, :, s0 * 1024:(s0 + ns) * 1024]))
        P.dma("sp", ka[67:70, 0:ns * 1024], V(A.Hd, A.Hd.t[h, :, s0 * 1024:(s0 + ns) * 1024]))
        for q4 in range(ns // 2):
            P.dma("act", va[:, q4 * 16:(q4 + 1) * 16, 0:64], V(A.Vg, vgv[:, s0 * 8 + q4 * 16:s0 * 8 + (q4 + 1) * 16, h, :]))
        if lb == 1:
            return
        qa = QA[h % 2]
        P.dma("sp", qa[64:67, :], V(A.Gd, A.Gd.t[h]))
        P.dma("pool", wq[h % 2][:], V(D.attn_wq, wqv[:, :, h * 64:(h + 1) * 64]))
        P.dma("pool", wo[h % 2][:], V(D.attn_wo, D.attn_wo.t[j, h * 64:(h + 1) * 64, :]))
        for tg in range(4):
            ps = pmi[tg % 2]
            for c in range(8):
                P.I("pe", "matmul", out=ps[0:64, :], lhsT=wq[h % 2][:, c, :], rhs=xT[:, c, tg * 512:(tg + 1) * 512],
                    start=(c == 0), stop=(c == 7), xr=xTr[tg * 4:tg * 4 + 4])
            P.I("act", "activation", out=qa[0:64, tg * 512:(tg + 1) * 512], in_=ps[0:64, :], func=AF.Copy, scale=0.125)

    units = []
    for h in range(16):
        for lb in range(2):
            slots = list(range(0, 4)) if lb == 0 else list(range(4, 12))
            for a in range(2):
                qg = lb * 2 + a
                lst = [(s_, ikp) for s_ in slots for ikp in range(4)]
                for n_, (s_, ikp) in enumerate(lst):
                    units.append((h, qg, s_, ikp, n_ == 0, n_ == len(lst) - 1))

    def score(u, idx):
        h, qg, s_, ikp, first, last = u
        lb = qg // 2
        ka, qa = KA[(h * 2 + lb) % 2], QA[h % 2]
        a = qg % 2
        ps, pt_ = pss[idx % 2], pT[idx % NPT]
        for e in range(2):
            kt = (s_ - 4 * lb) * 8 + ikp * 2 + e
            P.I("pe", "matmul", out=ps[:, e, :], lhsT=ka[0:70, kt * 128:(kt + 1) * 128],
                rhs=qa[0:70, qg * 512:(qg + 1) * 512], start=True, stop=True)
        if s_ in (0, 4):
            tf = tmpf[idx % 2]
            P.I("dve", "tensor_tensor", out=tf[:], in0=ps[:], in1=TRI[:, a, ikp * 2:ikp * 2 + 2, :], op=ALU.add)
            P.I("act", "activation", out=pt_[:], in_=tf[:], func=AF.Exp, bias=BM[:, s_:s_ + 1], scale=1.0)
        else:
            P.I("act", "activation", out=pt_[:], in_=ps[:], func=AF.Exp, bias=BM[:, s_:s_ + 1], scale=1.0)

    def pv(u, idx):
        h, qg, s_, ikp, first, last = u
        lb = qg // 2
        for e in range(2):
            kt = (s_ - 4 * lb) * 8 + ikp * 2 + e
            P.I("pe", "matmul", out=ppo[qg % 2][0:65, :], lhsT=VA[(h * 2 + lb) % 2][:, kt, :], rhs=pT[idx % NPT][:, e, :],
                start=(first and e == 0), stop=(last and e == 1))

    def ep1(h, qg):
        P.I("act", "activation", out=osb[qg % 2][0:65, :], in_=ppo[qg % 2][0:65, :], func=AF.Copy)
        P.I("dve", "reciprocal", out=rc[qg % 2][64:65, :], in_=osb[qg % 2][64:65, :])

    def ep2(h, qg):
        P.I("pe", "matmul", out=pmi[0][0:64, :], lhsT=ones1[64:65, 0:64], rhs=rc[qg % 2][64:65, :], start=True, stop=True)
        P.I("dve", "tensor_tensor", out=otn[qg % 2][:], in0=osb[qg % 2][0:64, :], in1=pmi[0][0:64, :], op=ALU.mult)

    def ep3(h, qg):
        on = otn[qg % 2]
        for t4 in range(4):
            tt = qg * 4 + t4
            bb, ii = tt // 8, tt % 8
            for half in range(2):
                p2 = pmi[(t4 * 2 + half) % 2]
                P.I("pe", "matmul", out=p2[:], lhsT=on[:, t4 * 128:(t4 + 1) * 128],
                    rhs=wo[h % 2][:, half * 512:(half + 1) * 512], start=True, stop=True)
                hv = V(C.h[bb][ii], C.hb[bb].t[:, ii, half * 512:(half + 1) * 512])
                P.I("dve", "tensor_tensor", out=hv, in0=hv, in1=p2[:], op=ALU.add)

    n = len(units)
    DEPTH = 3
    events = {}

    def at(i_, fn):
        events.setdefault(i_, []).append(fn)
    prep(0)
    prev_p = -1
    for idx in range(n + DEPTH + 8):
        for fn in events.pop(idx, []):
            fn()
        if idx < n:
            u = units[idx]
            p = u[0] * 2 + u[1] // 2
            if p != prev_p:
                prev_p = p
                if p + 1 < 32:
                    at(idx + 10, lambda p=p: prep(p + 1))
            score(u, idx)
        jx = idx - DEPTH
        if 0 <= jx < n:
            u = units[jx]
            pv(u, jx)
            if u[5]:
                ep1(u[0], u[1])
                at(idx + 2, lambda h=u[0], qg=u[1]: ep2(h, qg))
                at(idx + 4, lambda h=u[0], qg=u[1]: ep3(h, qg))
    assert not events
    P.pop()


def build_att2():
    nc = bass.Bass("TRN2", target_bir_lowering=False)
    with ExitStack() as st:
        P = Prog(nc, st)
        C = Ctx()
        names = ["mix_norm", "mlp_norm", "mlp_w1", "mlp_w2", "attn_wq", "attn_wo", "final_norm"]
        D = declare(P, names)
        A = Ctx()
        NK = NSLOT * 1024
        hin = P.dram("hin", [2048, 1024], F32, "ExternalInput")
        A.KTg = P.dram("KTg", [16, 64, NK], BF16, "ExternalInput")
        A.Vg = P.dram("Vg", [NK, 1024], BF16, "ExternalInput")
        A.Fkg = P.dram("Fkg", [NSLOT, 16, 1024], F32, "ExternalInput")
        A.Ftg = P.dram("Ftg", [16, 4], F32, "ExternalInput")
        A.Fq = P.dram("Fq", [16, 2048], F32, "ExternalInput")
        A.selk = P.dram("selk", [16, NSLOT * 4], F32, "ExternalInput")
        A.selq = P.dram("selq", [16, 8], F32, "ExternalInput")
        A.Bm = P.dram("Bm", [128, NSLOT], F32, "ExternalInput")
        A.Hd = P.dram("Hd", [16, 3, NK], BF16, "Internal")
        A.Gd = P.dram("Gd", [16, 3, 2048], BF16, "Internal")
        yout = P.dram("yout", [2048, 1024], F32, "ExternalOutput")
        make_consts(P, C)
        load_h(P, C, hin)
        xT = P.sbuf([128, 8, 2048], BF16, "xT")
        xTr = [xT.sub("xTr%d" % j) for j in range(16)]
        att_prep2(P, C, A)
        for j in range(2):
            l = 2 + j
            P.push()
            pst = [P.psum([128, 8, 128], BF16, "pst%d" % k) for k in range(2)]
            gain = load_bcast(P, D.mix_norm.t[l, :], D.mix_norm, 1024, "g")
            rmsnorm_T(P, C, gain, xT, xTr, pst)
            P.pop()
            attention2(P, C, j, D, A, xT, xTr)
            P.push()
            pst = [P.psum([128, 8, 128], BF16, "pst%d" % k) for k in range(2)]
            pm = [P.psum([128, 512], F32, "pm%d" % k) for k in range(4)]
            mlp(P, C, l, D, xT, xTr, pst, pm)
            P.pop()
        final_norm(P, C, D, yout)
        P.finish()
    return nc, names


def att2_inputs(c, r3):
    bt, q = c // 4, c % 4
    own = [q, 7 - q]

    def src(g):
        return r3[4 * bt + g // 2], slice((g % 2) * 1024, (g % 2 + 1) * 1024)
    slots = [own[0] - r for r in range(4)] + [own[1] - r for r in range(8)]
    kt, vg, fk = [], [], []
    selk = np.zeros((NSLOT, 4), np.float32)
    bm = np.zeros((NSLOT,), np.float32)
    for s_, g in enumerate(slots):
        if g < 0:
            kt.append(np.zeros((16, 64, 1024), ml_dtypes.bfloat16))
            vg.append(np.zeros((1024, 1024), ml_dtypes.bfloat16))
            fk.append(np.zeros((16, 1024), np.float32))
            bm[s_] = NEG
        else:
            r, sl = src(g)
            kt.append(r["KTo"][:, :, sl])
            vg.append(r["Vo"][sl, :])
            fk.append(r["Flo"][:, sl])
            selk[s_, :g // 2] = 1.0
    selq = np.zeros((2, 4), np.float32)
    hin, fq = [], []
    for lb, g in enumerate(own):
        r, sl = src(g)
        hin.append(r["hout"][sl, :])
        fq.append(r["Flo"][:, sl])
        selq[lb, :g // 2] = 1.0
    grp = [r3[4 * bt + k] for k in range(4)]
    return dict(hin=np.ascontiguousarray(np.concatenate(hin, 0)),
                KTg=np.ascontiguousarray(np.concatenate(kt, 2)), Vg=np.ascontiguousarray(np.concatenate(vg, 0)),
                Fkg=np.ascontiguousarray(np.stack(fk, 0)), Ftg=np.ascontiguousarray(np.concatenate([g["Fto"] for g in grp], 1)),
                Fq=np.ascontiguousarray(np.concatenate(fq, 1)), selk=np.tile(selk.reshape(1, -1), (16, 1)),
                selq=np.tile(selq.reshape(1, -1), (16, 1)), Bm=np.tile(bm.reshape(1, -1), (128, 1)))


def kernel(**inputs):
    inp = {k: np.ascontiguousarray(np.asarray(v, dtype=np.float32)) for k, v in inputs.items()}
    nc, names = get_prog("fused", build_fused)
    maps = []
    for c in range(8):
        bt, q = c // 4, c % 4
        selseg = np.zeros((128, 4), np.float32)
        selseg[:, q] = 1.0
        selq, ab = sel_att(c)
        maps.append(dict({n: inp[n] for n in names}, xfull=np.ascontiguousarray(inp["x"][bt]), selseg=selseg, selq=selq,
                         ABsel=ab))
    r = run(nc, maps)
    return np.stack([q["yout"] for q in r]).reshape(2, 8192, 1024).astype(np.float32)
```

```python
import math
from contextlib import ExitStack
import numpy as np
import ml_dtypes
import concourse.bass as bass
import concourse.mybir as mybir
from concourse.bass_utils import run_bass_kernel_spmd

F32 = mybir.dt.float32
BF16 = mybir.dt.bfloat16
I32 = mybir.dt.int32
ALU = mybir.AluOpType
AF = mybir.ActivationFunctionType

ENGS = ["pe", "act", "dve", "pool", "sp"]
NDMA_SEM = 16
NEG = -30000.0
TWO_PI_LO = 6.283185
POWS = [0, 1, 2, 3, 4, 5, 6, 7, 8, 16, 32, 64, 128, 256, 512, 1024, 2048, 4096]
PIDX = {n: j for j, n in enumerate(POWS)}
NPW = len(POWS)


class V:
    __slots__ = ("buf", "ap")

    def __init__(self, buf, ap):
        self.buf = buf
        self.ap = ap


class Buf:
    __slots__ = ("t", "name", "lastw", "readers")

    def __init__(self, t, name):
        self.t = t
        self.name = name
        self.lastw = None
        self.readers = []

    def __getitem__(self, idx):
        return V(self, self.t[idx])

    def v(self, ap):
        return V(self, ap)

    def sub(self, name=None):
        return Buf(self.t, name or self.name)


class Prog:
    def __init__(self, nc, stack):
        self.nc = nc
        self.stack = stack
        self.q = {e: [] for e in ENGS}
        self.cnt = {e: 0 for e in ENGS}
        self.csem = {e: [stack.enter_context(nc.semaphore("s_" + e))] for e in ENGS}
        self.dsem = {e: [stack.enter_context(nc.semaphore("d_%s%d" % (e, i))) for i in range(NDMA_SEM)]
                     for e in ENGS}
        self.dcnt = {e: 0 for e in ENGS}
        self.known = {e: {} for e in ENGS}
        self.nbuf = 0
        self.scopes = [stack]

    def push(self):
        st = ExitStack()
        self.scopes.append(st)
        return st

    def pop(self):
        self.barrier()
        st = self.scopes.pop()
        st.close()

    def sbuf(self, shape, dtype, name=None):
        self.nbuf += 1
        name = (name or "sb") + "_%d" % self.nbuf
        t = self.scopes[-1].enter_context(self.nc.sbuf_tensor(name, list(shape), dtype))
        return Buf(t, name)

    def psum(self, shape, dtype, name=None):
        self.nbuf += 1
        name = (name or "ps") + "_%d" % self.nbuf
        t = self.scopes[-1].enter_context(self.nc.psum_tensor(name, list(shape), dtype))
        return Buf(t, name)

    def dram(self, name, shape, dtype, kind):
        t = self.nc.dram_tensor(name, list(shape), dtype, kind=kind)
        return Buf(t.ap(), name)

    EPOCH = 24000

    def _ctok(self, eng):
        self.cnt[eng] += 1
        n = self.cnt[eng]
        ep = (n - 1) // self.EPOCH
        sems = self.csem[eng]
        while len(sems) <= ep:
            sems.append(self.stack.enter_context(self.nc.semaphore("s_%s_%d" % (eng, len(sems)))))
        return ((eng, "c"), sems[ep], (n - 1) % self.EPOCH + 1, n)

    def _need(self, eng, tok, waits):
        if tok is None:
            return
        key, sem, val, absn = tok
        k = self.known[eng]
        if k.get(key, 0) >= absn:
            return
        k[key] = absn
        waits.append((sem, val))

    def _deps(self, eng, reads, writes, pe_ok):
        waits = []

        def skip(tok):
            return pe_ok and tok[0] == ("pe", "c")
        for b in reads:
            if b.lastw is not None and not skip(b.lastw):
                self._need(eng, b.lastw, waits)
        for b in writes:
            if b.lastw is not None and not skip(b.lastw):
                self._need(eng, b.lastw, waits)
            for r in b.readers:
                if not skip(r):
                    self._need(eng, r, waits)
        return waits

    def _record(self, tok, reads, writes):
        for b in writes:
            b.lastw = tok
            b.readers = []
        for b in reads:
            if b in writes:
                continue
            if tok[0][1] == "c":
                b.readers = [r for r in b.readers if r[0] != tok[0]]
            b.readers.append(tok)

    def I(self, eng, meth, xr=(), xw=(), **kw):
        reads, writes, args = list(xr), list(xw), {}
        for k, v in kw.items():
            if isinstance(v, V):
                (writes if k in ("out", "accum_out") else reads).append(v.buf)
                args[k] = v.ap
            else:
                args[k] = v
        waits = self._deps(eng, reads, writes, eng == "pe")
        tok = self._ctok(eng)
        sem = tok[1]
        self._record(tok, reads, writes)

        def emit(e, meth=meth, args=args, waits=waits, sem=sem):
            for (s, v) in waits:
                e.wait_ge(s, v)
            getattr(e, meth)(**args).then_inc(sem, 1)
        self.q[eng].append(emit)

    def dma(self, eng, out, in_, **kw):
        reads, writes = [in_.buf], [out.buf]
        waits = self._deps(eng, reads, writes, False)
        i = self.dcnt[eng]
        self.dcnt[eng] += 1
        slot, rnd = i % NDMA_SEM, i // NDMA_SEM
        sem = self.dsem[eng][slot]
        key = (eng, "d", slot)
        if rnd > 0:
            self._need(eng, (key, sem, 16 * rnd, 16 * rnd), waits)
        tok = (key, sem, 16 * (rnd + 1), 16 * (rnd + 1))
        self._record(tok, reads, writes)

        def emit(e, waits=waits, sem=sem, o=out.ap, i_=in_.ap, kw=kw):
            for (s, v) in waits:
                e.wait_ge(s, v)
            e.dma_start(out=o, in_=i_, **kw).then_inc(sem, 16)
        self.q[eng].append(emit)

    def barrier(self):
        toks = []
        for e in ENGS:
            if self.cnt[e] > 0:
                n_ = self.cnt[e]
                ep_ = (n_ - 1) // self.EPOCH
                toks.append(((e, "c"), self.csem[e][ep_], (n_ - 1) % self.EPOCH + 1, n_))
            for slot in range(NDMA_SEM):
                n = (self.dcnt[e] - slot + NDMA_SEM - 1) // NDMA_SEM
                if n > 0:
                    toks.append(((e, "d", slot), self.dsem[e][slot], 16 * n, 16 * n))
        for e in ENGS:
            waits = []
            for t in toks:
                self._need(e, t, waits)

            def emit(en, waits=waits):
                for (s, v) in waits:
                    en.wait_ge(s, v)
            self.q[e].append(emit)

    def finish(self):
        self.barrier()
        nc = self.nc
        with nc.Block() as block:
            @block.tensor
            def _(e):
                for f in self.q["pe"]:
                    f(e)

            @block.scalar
            def _(e):
                for f in self.q["act"]:
                    f(e)

            @block.vector
            def _(e):
                for f in self.q["dve"]:
                    f(e)

            @block.gpsimd
            def _(e):
                for f in self.q["pool"]:
                    f(e)

            @block.sync
            def _(e):
                for f in self.q["sp"]:
                    f(e)


class Ctx:
    pass


def make_consts(P, C):
    identf = P.sbuf([128, 128], F32, "identf")
    C.ident = P.sbuf([128, 128], BF16, "ident")
    _memset(P, "pool", identf[:], 0.0)
    P.I("pool", "affine_select", out=identf[:], in_=identf[:], pattern=[[-1, 128]], base=0,
        channel_multiplier=1, compare_op=ALU.not_equal, fill=1.0)
    P.I("pool", "tensor_copy", out=C.ident[:], in_=identf[:])
    C.identf = identf


def _memset(P, eng, v, val):
    buf = v.buf
    waits = P._deps(eng, [], [buf], False)
    tok = P._ctok(eng)
    sem = tok[1]
    P._record(tok, [], [buf])

    def emit(e, ap=v.ap, val=val, waits=waits, sem=sem):
        for (s, x) in waits:
            e.wait_ge(s, x)
        e.memset(ap, val).then_inc(sem, 1)
    P.q[eng].append(emit)


def load_h(P, C, src):
    if not hasattr(C, "hb"):
        C.hb = [P.sbuf([128, 8, 1024], F32, "h%d" % b) for b in range(2)]
        C.h = [[C.hb[b].sub("h%d_%d" % (b, i)) for i in range(8)] for b in range(2)]
    sv = src.t.rearrange("(b k i) d -> b k i d", b=2, k=128)
    for b in range(2):
        for i in range(8):
            P.dma("sp" if i % 2 == 0 else "act", V(C.h[b][i], C.hb[b].t[:, i, :]), V(src, sv[b, :, i, :]))


def store_h(P, C, dst):
    dv = dst.t.rearrange("(b k i) d -> b k i d", b=2, k=128)
    for b in range(2):
        for i in range(8):
            P.dma("sp", V(dst, dv[b, :, i, :]), V(C.h[b][i], C.hb[b].t[:, i, :]))


def load_bcast(P, dram_vec_ap, dram_buf, n, name):
    t = P.sbuf([128, n], F32, name)
    P.dma("act", t[:], V(dram_buf, dram_vec_ap.partition_broadcast(128)))
    return t


def rmsnorm_T(P, C, gain, xT, xTr, pst, out_tm=None):
    ss = P.sbuf([128, 16], F32, "ss")
    rs = P.sbuf([128, 16], F32, "rs")
    junk = P.sbuf([128, 1024], F32, "junk")
    hn = [P.sbuf([128, 1024], BF16, "hn%d" % j) for j in range(2)]
    for b in range(2):
        for i in range(8):
            j = b * 8 + i
            hv = V(C.h[b][i], C.hb[b].t[:, i, :])
            P.I("act", "activation", out=junk[:], in_=hv, func=AF.Square, accum_out=ss[:, j:j + 1])
    P.I("dve", "tensor_scalar", out=rs[:], in0=ss[:], scalar1=1.0 / 1024, scalar2=1e-6, op0=ALU.mult, op1=ALU.add)
    P.I("act", "activation", out=rs[:], in_=rs[:], func=AF.Sqrt)
    P.I("dve", "reciprocal", out=rs[:], in_=rs[:])
    C.rs = rs
    for b in range(2):
        for i in range(8):
            j = b * 8 + i
            hv = V(C.h[b][i], C.hb[b].t[:, i, :])
            if out_tm is not None:
                P.I("dve", "scalar_tensor_tensor", out=out_tm[j][:], in0=hv, scalar=rs[:, j:j + 1], in1=gain[:],
                    op0=ALU.mult, op1=ALU.mult)
                continue
            hb = hn[j % 2]
            P.I("dve", "scalar_tensor_tensor", out=hb[:], in0=hv, scalar=rs[:, j:j + 1], in1=gain[:],
                op0=ALU.mult, op1=ALU.mult)
            pt = pst[j % 2]
            for c in range(8):
                P.I("pe", "transpose", out=pt[:, c, :], in_=hb[:, c * 128:(c + 1) * 128], identity=C.ident[:])
            P.I("act" if j % 2 else "dve", "activation" if j % 2 else "tensor_copy",
                out=V(xTr[j], xT.t[:, :, j * 128:(j + 1) * 128]), in_=pt[:], **({"func": AF.Copy} if j % 2 else {}))


def mlp(P, C, l, D, xT, xTr, pst, pm):
    gain = load_bcast(P, D.mlp_norm.t[l, :], D.mlp_norm, 1024, "mg")
    rmsnorm_T(P, C, gain, xT, xTr, pst)
    w1b = [P.sbuf([128, 8, 512], BF16, "w1b%d" % j) for j in range(2)]
    w2b = [P.sbuf([128, 4, 1024], BF16, "w2b%d" % j) for j in range(2)]
    hid = P.sbuf([128, 4, 2048], BF16, "hid")
    hidr = [[hid.sub() for tg in range(4)] for fc in range(4)]
    rl = [P.sbuf([128, 512], F32, "rl%d" % j) for j in range(2)]
    w1v = D.mlp_w1.t[l].rearrange("(c p) f -> p c f", p=128)
    w2v = D.mlp_w2.t[l].rearrange("(c p) n -> p c n", p=128)
    n = 0
    for fb in range(8):
        a, bb = w1b[fb % 2], w2b[fb % 2]
        P.dma("pool", a[:], V(D.mlp_w1, w1v[:, :, fb * 512:(fb + 1) * 512]))
        P.dma("pool", bb[:], V(D.mlp_w2, w2v[:, fb * 4:(fb + 1) * 4, :]))
        for fc in range(4):
            for tg in range(4):
                ps = pm[n % len(pm)]
                r = rl[n % 2]
                n += 1
                for c in range(8):
                    P.I("pe", "matmul", out=ps[:], lhsT=a[:, c, fc * 128:(fc + 1) * 128],
                        rhs=xT[:, c, tg * 512:(tg + 1) * 512], start=(c == 0), stop=(c == 7), xr=xTr[tg * 4:tg * 4 + 4])
                P.I("act", "activation", out=r[:], in_=ps[:], func=AF.Relu)
                P.I("pool", "tensor_tensor", out=V(hidr[fc][tg], hid.t[:, fc, tg * 512:(tg + 1) * 512]), in0=r[:], in1=r[:],
                    op=ALU.mult)
        for tt in range(16):
            b, i = tt // 8, tt % 8
            for half in range(2):
                ps = pm[n % len(pm)]
                n += 1
                for fc in range(4):
                    P.I("pe", "matmul", out=ps[:], lhsT=V(hidr[fc][tt // 4], hid.t[:, fc, tt * 128:(tt + 1) * 128]),
                        rhs=bb[:, fc, half * 512:(half + 1) * 512], start=(fc == 0), stop=(fc == 3))
                hv = V(C.h[b][i], C.hb[b].t[:, i, half * 512:(half + 1) * 512])
                P.I("dve", "tensor_tensor", out=hv, in0=hv, in1=ps[:], op=ALU.add)


def s5_params(P, C, l, D):
    S = Ctx()
    nc_slow = dict(allow_slow_non_contiguous=True)
    S.LR = P.sbuf([128, NPW, 32], F32, "LR")
    S.LI = P.sbuf([128, NPW, 32], F32, "LI")
    S.NLI = P.sbuf([128, NPW, 32], F32, "NLI")
    S.BR = P.sbuf([128, 32, 16], F32, "BR")
    S.BI = P.sbuf([128, 32, 16], F32, "BI")
    cr = P.sbuf([128, 32, 16], F32, "cr")
    ci = P.sbuf([128, 32, 16], F32, "ci")
    S.dcol = P.sbuf([128, 8], F32, "dcol")
    P.push()
    are = P.sbuf([128, 32], F32, "are")
    aim = P.sbuf([128, 32], F32, "aim")
    ldt = P.sbuf([128, 32], F32, "ldt")
    br = P.sbuf([128, 32, 16], F32, "br")
    bi = P.sbuf([128, 32, 16], F32, "bi")
    for g2 in range(2):
        ps_ = slice(g2 * 64, (g2 + 1) * 64)
        for (dst, src) in ((are, D.ssm_a_re), (aim, D.ssm_a_im)):
            P.dma("sp", dst[ps_, :], V(src, src.t[l].rearrange("(gp g2) p -> g2 p gp", g2=2)[g2]), **nc_slow)
        P.dma("sp", ldt[ps_, :], V(D.ssm_log_dt, D.ssm_log_dt.t[l].rearrange("(gp g2) -> g2 gp", g2=2)[g2]
                                    .partition_broadcast(64)), **nc_slow)
        for (dst, src) in ((br, D.ssm_b_re), (bi, D.ssm_b_im)):
            P.dma("act", dst[ps_, :, :], V(src, src.t[l].rearrange("(gp g2) p c -> g2 p gp c", g2=2)[g2]))
    P.dma("sp", S.dcol[:], V(D.ssm_d, D.ssm_d.t[l].rearrange("(c p) -> p c", p=128)), **nc_slow)
    pct = P.psum([128, 4, 128], F32, "pct")
    for ti, (dst, src) in enumerate(((cr, D.ssm_c_re), (ci, D.ssm_c_im))):
        ct2 = P.sbuf([128, 8, 2, 64], F32, "ct2_%d" % ti)
        sv = src.t[l].rearrange("(gb gl) c p -> (gl c) gb p", gl=8)
        for dup in range(2):
            P.dma("act" if dup else "sp", ct2[:, :, dup, :], V(src, sv))
        for half in range(2):
            for k4 in range(4):
                gb = half * 4 + k4
                P.I("pe", "transpose", out=pct[:, k4, :], in_=ct2.v(ct2.t[:, gb, :, :].rearrange("q d p -> q (d p)")),
                    identity=C.identf[:])
            for g2 in range(2):
                ps_ = slice(g2 * 64, (g2 + 1) * 64)
                srcv = pct.v(pct.t[ps_, :, :].rearrange("q k (gpl g c) -> q k gpl g c", g=2, c=16)[:, :, :, g2, :])
                dstv = dst.v(dst.t[ps_, half * 16:(half + 1) * 16, :].rearrange("q (k gpl) c -> q k gpl c", k=4))
                P.I("dve" if g2 else "act", "tensor_copy" if g2 else "activation", out=dstv, in_=srcv,
                    **({} if g2 else {"func": AF.Copy}))
    dt = P.sbuf([128, 32], F32, "dt")
    P.I("act", "activation", out=dt[:], in_=ldt[:], func=AF.Exp)
    xr = P.sbuf([128, 32], F32, "xr")
    xi = P.sbuf([128, 32], F32, "xi")
    P.I("dve", "tensor_tensor", out=xr[:], in0=are[:], in1=dt[:], op=ALU.mult)
    P.I("dve", "tensor_tensor", out=xi[:], in0=aim[:], in1=dt[:], op=ALU.mult)
    ncst = P.sbuf([128, NPW, 32], F32, "ncst")
    for j, n in enumerate(POWS):
        _memset(P, "pool", ncst[:, j, :], float(n))
    R = P.sbuf([128, NPW, 32], F32, "R")
    E = P.sbuf([128, NPW, 32], F32, "E")
    Ki = P.sbuf([128, NPW, 32], I32, "Ki")
    Kf = P.sbuf([128, NPW, 32], F32, "Kf")
    T1 = P.sbuf([128, NPW, 32], F32, "T1")
    T2 = P.sbuf([128, NPW, 32], F32, "T2")
    xib = xi.v(xi.t[:, :].unsqueeze(1).to_broadcast([128, NPW, 32]))
    xrb = xr.v(xr.t[:, :].unsqueeze(1).to_broadcast([128, NPW, 32]))
    P.I("dve", "tensor_tensor", out=R[:], in0=ncst[:], in1=xib, op=ALU.mult)
    P.I("dve", "tensor_scalar", out=R[:], in0=R[:], scalar1=1.0 / (2 * math.pi), scalar2=None, op0=ALU.mult)
    P.I("dve", "tensor_tensor", out=E[:], in0=ncst[:], in1=xrb, op=ALU.mult)
    P.I("dve", "tensor_copy", out=Ki[:], in_=R[:])
    P.I("dve", "tensor_copy", out=Kf[:], in_=Ki[:])
    P.I("dve", "tensor_tensor", out=R[:], in0=R[:], in1=Kf[:], op=ALU.subtract)
    P.I("dve", "scalar_tensor_tensor", out=T1[:], in0=R[:], scalar=0.5, in1=R[:], op0=ALU.is_gt, op1=ALU.subtract)
    P.I("dve", "scalar_tensor_tensor", out=T2[:], in0=T1[:], scalar=0.5, in1=T1[:], op0=ALU.is_gt, op1=ALU.subtract)
    SN = P.sbuf([128, NPW, 32], F32, "SN")
    CS = P.sbuf([128, NPW, 32], F32, "CS")
    MG = P.sbuf([128, NPW, 32], F32, "MG")
    P.I("act", "activation", out=SN[:], in_=T2[:], func=AF.Sin, scale=TWO_PI_LO)
    P.I("dve", "tensor_scalar", out=T1[:], in0=T2[:], scalar1=0.25, scalar2=None, op0=ALU.add)
    P.I("dve", "scalar_tensor_tensor", out=T2[:], in0=T1[:], scalar=0.5, in1=T1[:], op0=ALU.is_gt, op1=ALU.subtract)
    P.I("act", "activation", out=CS[:], in_=T2[:], func=AF.Sin, scale=-TWO_PI_LO)
    P.I("act", "activation", out=MG[:], in_=E[:], func=AF.Exp)
    P.I("dve", "tensor_tensor", out=S.LR[:], in0=MG[:], in1=CS[:], op=ALU.mult)
    P.I("dve", "tensor_tensor", out=S.LI[:], in0=MG[:], in1=SN[:], op=ALU.mult)
    P.I("dve", "tensor_scalar", out=S.NLI[:], in0=S.LI[:], scalar1=-1.0, scalar2=None, op0=ALU.mult)
    nr = P.sbuf([128, 32], F32, "nr")
    den = P.sbuf([128, 32], F32, "den")
    t1 = P.sbuf([128, 32], F32, "t1")
    t2 = P.sbuf([128, 32], F32, "t2")
    cfr = P.sbuf([128, 32], F32, "cfr")
    cfi = P.sbuf([128, 32], F32, "cfi")
    l1r, l1i = S.LR[:, PIDX[1], :], S.LI[:, PIDX[1], :]
    P.I("dve", "tensor_scalar", out=nr[:], in0=l1r, scalar1=-1.0, scalar2=None, op0=ALU.add)
    P.I("dve", "tensor_tensor", out=den[:], in0=are[:], in1=are[:], op=ALU.mult)
    P.I("dve", "tensor_tensor", out=t1[:], in0=aim[:], in1=aim[:], op=ALU.mult)
    P.I("dve", "tensor_tensor", out=den[:], in0=den[:], in1=t1[:], op=ALU.add)
    P.I("dve", "reciprocal", out=den[:], in_=den[:])
    P.I("dve", "tensor_tensor", out=t1[:], in0=nr[:], in1=are[:], op=ALU.mult)
    P.I("dve", "tensor_tensor", out=t2[:], in0=l1i, in1=aim[:], op=ALU.mult)
    P.I("dve", "tensor_tensor", out=t1[:], in0=t1[:], in1=t2[:], op=ALU.add)
    P.I("dve", "tensor_tensor", out=cfr[:], in0=t1[:], in1=den[:], op=ALU.mult)
    P.I("dve", "tensor_tensor", out=t1[:], in0=l1i, in1=are[:], op=ALU.mult)
    P.I("dve", "tensor_tensor", out=t2[:], in0=nr[:], in1=aim[:], op=ALU.mult)
    P.I("dve", "tensor_tensor", out=t1[:], in0=t1[:], in1=t2[:], op=ALU.subtract)
    P.I("dve", "tensor_tensor", out=cfi[:], in0=t1[:], in1=den[:], op=ALU.mult)
    u1 = P.sbuf([128, 32, 16], F32, "u1")
    u2 = P.sbuf([128, 32, 16], F32, "u2")
    cfrb = cfr.v(cfr.t[:, :].unsqueeze(2).to_broadcast([128, 32, 16]))
    cfib = cfi.v(cfi.t[:, :].unsqueeze(2).to_broadcast([128, 32, 16]))
    P.I("dve", "tensor_tensor", out=u1[:], in0=br[:], in1=cfrb, op=ALU.mult)
    P.I("dve", "tensor_tensor", out=u2[:], in0=bi[:], in1=cfib, op=ALU.mult)
    P.I("dve", "tensor_tensor", out=S.BR[:], in0=u1[:], in1=u2[:], op=ALU.subtract)
    P.I("dve", "tensor_tensor", out=u1[:], in0=bi[:], in1=cfrb, op=ALU.mult)
    P.I("dve", "tensor_tensor", out=u2[:], in0=br[:], in1=cfib, op=ALU.mult)
    P.I("dve", "tensor_tensor", out=S.BI[:], in0=u1[:], in1=u2[:], op=ALU.add)
    S.CR, S.CI = cr, ci
    P.pop()
    return S


def s5_core(P, C, l, D, S, xT, xTr, mode, Fout=None, Fall=None, selS=None, carry=None):
    pw = P.psum([128, 16, 128], BF16, "pw")
    pk = P.psum([128, 8, 32], F32, "pk")
    pz = [P.psum([128, 2, 256], F32, "pz%d" % j) for j in range(2 if mode == "A" else 1)]
    py = [P.psum([128, 8, 128], F32, "py%d" % j) for j in range(2)] if mode == "B" else None
    XB = P.sbuf([128, 8, 2, 4, 32], BF16, "XB")
    CB = P.sbuf([128, 9, 2, 4, 32], BF16, "CB")
    _memset(P, "pool", XB[:], 0.0)
    _memset(P, "pool", CB[:], 0.0)
    WT = P.sbuf([128, 16, 128], BF16, "WT")
    KT = P.sbuf([128, 8, 32], BF16, "KT")
    v1 = P.sbuf([128, 9, 4, 16], F32, "v1")
    v2 = P.sbuf([128, 9, 4, 16], F32, "v2")
    ZW = [[P.sbuf([128, 2, 256], F32, "zw%d_%d" % (m, j)) for j in range(2)] for m in range(4)]
    tt_ = [P.sbuf([128, 256], F32, "tt%d" % m) for m in range(4)]
    tu_ = [P.sbuf([128, 256], F32, "tu%d" % m) for m in range(4)]
    if mode == "A":
        Fsb = P.sbuf([128, 32, 2], F32, "Fsb")
    else:
        SP = P.sbuf([128, 4, 2, 256], BF16, "SP")
        SPr = [SP.sub() for m in range(4)]
        e1 = P.sbuf([128, 1024], F32, "e1")
        e2 = P.sbuf([128, 1024], F32, "e2")
        e3 = P.sbuf([128, 1024], F32, "e3")
        if carry is None:
            FA = P.sbuf([128, 8, 64], F32, "FA")
            P.dma("sp", FA[:], V(Fall, Fall.t.rearrange("c p f -> p c f")))
            SL = P.sbuf([128, 24], F32, "SL")
            P.dma("sp", SL[:], selS[:])
            acc = [P.sbuf([128, 32, 2], F32, "acc%d" % j) for j in range(3)]
            for mm in range(3):
                av = acc[mm].v(acc[mm].t[:].rearrange("p g r -> p (g r)"))
                for c in range(8):
                    if c == 0:
                        P.I("dve", "tensor_scalar", out=av, in0=FA[:, 0, :], scalar1=SL[:, mm:mm + 1], scalar2=None,
                            op0=ALU.mult)
                    else:
                        P.I("dve", "scalar_tensor_tensor", out=av, in0=FA[:, c, :],
                            scalar=SL[:, c * 3 + mm:c * 3 + mm + 1], in1=av, op0=ALU.mult, op1=ALU.add)
        SIN = P.sbuf([128, 32, 2], F32, "SIN")
        w1 = P.sbuf([128, 32], F32, "w1")
        w2 = P.sbuf([128, 32], F32, "w2")

        def cmul_acc(dst, src, pw_, first):
            lr, li = S.LR[:, PIDX[pw_], :], S.LI[:, PIDX[pw_], :]
            P.I("dve", "tensor_tensor", out=w1[:], in0=src[:, :, 0], in1=lr, op=ALU.mult)
            P.I("dve", "tensor_tensor", out=w2[:], in0=src[:, :, 1], in1=li, op=ALU.mult)
            P.I("dve", "tensor_tensor", out=w1[:], in0=w1[:], in1=w2[:], op=ALU.subtract)
            if first:
                P.I("dve", "tensor_copy", out=dst[:, :, 0], in_=w1[:])
            else:
                P.I("dve", "tensor_tensor", out=dst[:, :, 0], in0=dst[:, :, 0], in1=w1[:], op=ALU.add)
            P.I("dve", "tensor_tensor", out=w1[:], in0=src[:, :, 1], in1=lr, op=ALU.mult)
            P.I("dve", "tensor_tensor", out=w2[:], in0=src[:, :, 0], in1=li, op=ALU.mult)
            P.I("dve", "tensor_tensor", out=w1[:], in0=w1[:], in1=w2[:], op=ALU.add)
            if first:
                P.I("dve", "tensor_copy", out=dst[:, :, 1], in_=w1[:])
            else:
                P.I("dve", "tensor_tensor", out=dst[:, :, 1], in0=dst[:, :, 1], in1=w1[:], op=ALU.add)
        if carry is None:
            P.I("dve", "tensor_copy", out=SIN[:], in_=acc[0][:])
            cmul_acc(SIN, acc[1], 2048, False)
            cmul_acc(SIN, acc[2], 4096, False)
        else:
            P.I("dve", "tensor_copy", out=SIN[:], in_=carry[:])
        INJ = P.sbuf([128, 32, 2], F32, "INJ")
        cmul_acc(INJ, SIN, 8, True)

    for ch in range(8):
        gs = slice(ch * 4, ch * 4 + 4)
        xreg = [xTr[j] for j in range(16)]
        for half in range(2):
            pr = slice(half * 64, half * 64 + 64)
            hs = slice(half * 16, half * 16 + 16)
            for (tab, n0, nn, A_r, A_i, conjC) in ((XB, 0, 8, S.BR, S.BI, False), (CB, 0, 9, S.CR, S.CI, True)):
                ar = A_r.v(A_r.t[pr, gs, :].unsqueeze(1).to_broadcast([64, nn, 4, 16]))
                ai = A_i.v(A_i.t[pr, gs, :].unsqueeze(1).to_broadcast([64, nn, 4, 16]))
                lr = S.LR.v(S.LR.t[pr, n0:n0 + nn, gs].unsqueeze(3).to_broadcast([64, nn, 4, 16]))
                li = S.LI.v(S.LI.t[pr, n0:n0 + nn, gs].unsqueeze(3).to_broadcast([64, nn, 4, 16]))
                a1, a2 = v1[pr, 0:nn], v2[pr, 0:nn]
                P.I("dve", "tensor_tensor", out=a1, in0=ar, in1=lr, op=ALU.mult)
                P.I("pool", "tensor_tensor", out=a2, in0=ai, in1=li, op=ALU.mult)
                P.I("dve", "tensor_tensor", out=tab[pr, 0:nn, 0, :, hs], in0=a1, in1=a2, op=ALU.subtract)
                P.I("dve", "tensor_tensor", out=a1, in0=ar, in1=li, op=ALU.mult)
                P.I("pool", "tensor_tensor", out=a2, in0=ai, in1=lr, op=ALU.mult)
                if conjC:
                    P.I("dve", "tensor_tensor", out=a1, in0=a1, in1=a2, op=ALU.add)
                    P.I("dve", "tensor_scalar", out=tab[pr, 0:nn, 1, :, hs], in0=a1, scalar1=-1.0, scalar2=None,
                        op0=ALU.mult)
                else:
                    P.I("dve", "tensor_tensor", out=tab[pr, 0:nn, 1, :, hs], in0=a1, in1=a2, op=ALU.add)
        for n in range(8):
            for r in range(2):
                P.I("pe", "transpose", out=pw[:, n * 2 + r, :],
                    in_=XB.v(XB.t[:, n, r, :, :].rearrange("p m c -> p (m c)")), identity=C.ident[:])
        P.I("act", "activation", out=WT[:], in_=pw[:], func=AF.Copy)
        if mode == "B":
            for tau in range(8):
                for m in range(4):
                    for r in range(2):
                        P.I("pe", "matmul", out=pk[m * 32:(m + 1) * 32, tau, :], lhsT=XB[:, tau, r, m, :],
                            rhs=CB[:, 0, r, m, :], start=(r == 0), stop=(r == 1), tile_position=(0, m * 32))
            P.I("act", "activation", out=KT[:], in_=pk[:], func=AF.Copy)
        for m in range(4):
            pzz = pz[m % len(pz)]
            rs_ = slice(m * 32, m * 32 + 32)
            for r in range(2):
                for i in range(8):
                    rhs = xT.v(xT.t[rs_, ch, :].rearrange("p (b i k) -> p b i k", b=2, i=8)[:, :, i, :])
                    P.I("pe", "matmul", out=pzz[:, r, :], lhsT=WT[rs_, (7 - i) * 2 + r, :], rhs=rhs,
                        start=(i == 0), stop=(i == 7), tile_position=(m * 32, 0), xr=xreg)
            P.I("act", "activation", out=ZW[m][0][:], in_=pzz[:], func=AF.Copy)
            if mode == "B":
                gp = ch * 4 + m
                P.I("pool", "tensor_tensor", out=ZW[m][0][:, :, 0], in0=ZW[m][0][:, :, 0], in1=INJ[:, gp, :], op=ALU.add)
        if mode == "A":
            cur = [0, 0, 0, 0]
            ln = 256
            for s in range(8):
                pidx = PIDX[8 << s]
                hf = ln // 2
                for m in range(4):
                    gp = ch * 4 + m
                    a = ZW[m][cur[m]]
                    lr = S.LR[:, pidx, gp:gp + 1]
                    P.I("dve", "scalar_tensor_tensor", out=tt_[m][:, 0:hf], in0=a[:, 0, 0:ln:2], scalar=lr,
                        in1=a[:, 0, 1:ln:2], op0=ALU.mult, op1=ALU.add)
                    P.I("dve", "scalar_tensor_tensor", out=tu_[m][:, 0:hf], in0=a[:, 1, 0:ln:2], scalar=lr,
                        in1=a[:, 1, 1:ln:2], op0=ALU.mult, op1=ALU.add)
                for m in range(4):
                    gp = ch * 4 + m
                    a, b_ = ZW[m][cur[m]], ZW[m][1 - cur[m]]
                    li, nli = S.LI[:, pidx, gp:gp + 1], S.NLI[:, pidx, gp:gp + 1]
                    P.I("dve", "scalar_tensor_tensor", out=b_[:, 0, 0:hf], in0=a[:, 1, 0:ln:2], scalar=nli,
                        in1=tt_[m][:, 0:hf], op0=ALU.mult, op1=ALU.add)
                    P.I("dve", "scalar_tensor_tensor", out=b_[:, 1, 0:hf], in0=a[:, 0, 0:ln:2], scalar=li,
                        in1=tu_[m][:, 0:hf], op0=ALU.mult, op1=ALU.add)
                    cur[m] = 1 - cur[m]
                ln = hf
            for m in range(4):
                gp = ch * 4 + m
                P.I("pool", "tensor_copy", out=Fsb[:, gp, :], in_=ZW[m][cur[m]][:, :, 0])
            continue
        cur = [0, 0, 0, 0]
        for s in range(8):
            d = 1 << s
            pidx = PIDX[8 * d]
            for m in range(4):
                gp = ch * 4 + m
                a, b_ = ZW[m][cur[m]], ZW[m][1 - cur[m]]
                lr, li, nli = S.LR[:, pidx, gp:gp + 1], S.LI[:, pidx, gp:gp + 1], S.NLI[:, pidx, gp:gp + 1]
                P.I("pool", "tensor_copy", out=b_[:, :, 0:d], in_=a[:, :, 0:d])
                P.I("dve", "scalar_tensor_tensor", out=tt_[m][:, 0:256 - d], in0=a[:, 0, 0:256 - d], scalar=lr,
                    in1=a[:, 0, d:256], op0=ALU.mult, op1=ALU.add)
                P.I("dve", "scalar_tensor_tensor", out=tu_[m][:, 0:256 - d], in0=a[:, 1, 0:256 - d], scalar=lr,
                    in1=a[:, 1, d:256], op0=ALU.mult, op1=ALU.add)
            for m in range(4):
                gp = ch * 4 + m
                a, b_ = ZW[m][cur[m]], ZW[m][1 - cur[m]]
                li, nli = S.LI[:, pidx, gp:gp + 1], S.NLI[:, pidx, gp:gp + 1]
                P.I("dve", "scalar_tensor_tensor", out=b_[:, 0, d:256], in0=a[:, 1, 0:256 - d], scalar=nli,
                    in1=tt_[m][:, 0:256 - d], op0=ALU.mult, op1=ALU.add)
                P.I("dve", "scalar_tensor_tensor", out=b_[:, 1, d:256], in0=a[:, 0, 0:256 - d], scalar=li,
                    in1=tu_[m][:, 0:256 - d], op0=ALU.mult, op1=ALU.add)
                cur[m] = 1 - cur[m]
        if mode == "A":
            for m in range(4):
                gp = ch * 4 + m
                P.I("pool", "tensor_copy", out=Fsb[:, gp, :], in_=ZW[m][cur[m]][:, :, 255])
            continue
        for m in range(4):
            gp = ch * 4 + m
            fin = ZW[m][cur[m]]
            if carry is not None:
                P.I("pool", "tensor_copy", out=carry[:, gp, :], in_=fin[:, :, 255])
            P.I("act", "activation", out=V(SPr[m], SP.t[:, m, :, 1:256]), in_=fin[:, :, 0:255], func=AF.Copy)
            P.I("pool", "tensor_copy", out=V(SPr[m], SP.t[:, m, :, 0]), in_=SIN[:, gp, :])
        for b in range(2):
            pyy = py[b]
            for m in range(4):
                rs_ = slice(m * 32, m * 32 + 32)
                for j in range(8):
                    nmm = (j + 1) + 2
                    k_ = 0
                    for i in range(j + 1):
                        P.I("pe", "matmul", out=pyy[rs_, j, :], lhsT=KT[rs_, j - i, :],
                            rhs=xT[rs_, ch, b * 1024 + i * 128: b * 1024 + (i + 1) * 128],
                            start=(k_ == 0), stop=False, tile_position=(m * 32, m * 32), xr=xreg)
                        k_ += 1
                    for r in range(2):
                        P.I("pe", "matmul", out=pyy[rs_, j, :], lhsT=CB[:, j + 1, r, m, :],
                            rhs=V(SPr[m], SP.t[:, m, r, b * 128:(b + 1) * 128]),
                            start=False, stop=(r == 1), tile_position=(0, m * 32))
            uv = xT.v(xT.t[:, ch, b * 1024:(b + 1) * 1024])
            yv = pyy.v(pyy.t[:].rearrange("p j k -> p (j k)"))
            P.I("dve", "scalar_tensor_tensor", out=e1[:], in0=uv, scalar=S.dcol[:, ch:ch + 1], in1=yv,
                op0=ALU.mult, op1=ALU.add, xr=xreg)
            P.I("pool", "tensor_tensor", out=e2[:], in0=e1[:], in1=e1[:], op=ALU.mult)
            P.I("pool", "tensor_scalar", out=e2[:], in0=e2[:], scalar1=0.044715, scalar2=1.0, op0=ALU.mult, op1=ALU.add)
            P.I("pool", "tensor_tensor", out=e2[:], in0=e2[:], in1=e1[:], op=ALU.mult)
            P.I("act", "activation", out=e3[:], in_=e2[:], func=AF.Sigmoid, scale=1.5957691216)
            for i in range(8):
                P.I("dve", "tensor_tensor", out=V(xTr[b * 8 + i], xT.t[:, ch, b * 1024 + i * 128:b * 1024 + (i + 1) * 128]),
                    in0=e3[:, i * 128:(i + 1) * 128], in1=e1[:, i * 128:(i + 1) * 128], op=ALU.mult)
    if mode == "A":
        P.dma("sp", Fout[:], Fsb.v(Fsb.t[:].rearrange("p g r -> p (g r)")))


def s5_tables(P, C, S, TabW, TabC, TabK):
    P.push()
    pw = P.psum([128, 16, 128], BF16, "pw")
    pk = P.psum([128, 8, 32], F32, "pk")
    XB = P.sbuf([128, 8, 2, 4, 32], BF16, "XB")
    CBs = [P.sbuf([128, 9, 2, 4, 32], BF16, "CB%d" % j) for j in range(2)]
    WTs = [P.sbuf([128, 16, 128], BF16, "WT%d" % j) for j in range(2)]
    KTs = [P.sbuf([128, 8, 32], BF16, "KT%d" % j) for j in range(2)]
    _memset(P, "pool", XB[:], 0.0)
    for j in range(2):
        _memset(P, "pool", CBs[j][:], 0.0)
    v1 = P.sbuf([128, 9, 4, 16], F32, "v1")
    v2 = P.sbuf([128, 9, 4, 16], F32, "v2")
    for ch in range(8):
        gs = slice(ch * 4, ch * 4 + 4)
        CB, WT, KT = CBs[ch % 2], WTs[ch % 2], KTs[ch % 2]
        for half in range(2):
            pr = slice(half * 64, half * 64 + 64)
            hs = slice(half * 16, half * 16 + 16)
            for (tab, nn, A_r, A_i, conjC) in ((XB, 8, S.BR, S.BI, False), (CB, 9, S.CR, S.CI, True)):
                ar = A_r.v(A_r.t[pr, gs, :].unsqueeze(1).to_broadcast([64, nn, 4, 16]))
                ai = A_i.v(A_i.t[pr, gs, :].unsqueeze(1).to_broadcast([64, nn, 4, 16]))
                lr = S.LR.v(S.LR.t[pr, 0:nn, gs].unsqueeze(3).to_broadcast([64, nn, 4, 16]))
                li = S.LI.v(S.LI.t[pr, 0:nn, gs].unsqueeze(3).to_broadcast([64, nn, 4, 16]))
                a1, a2 = v1[pr, 0:nn], v2[pr, 0:nn]
                P.I("dve", "tensor_tensor", out=a1, in0=ar, in1=lr, op=ALU.mult)
                P.I("pool", "tensor_tensor", out=a2, in0=ai, in1=li, op=ALU.mult)
                P.I("dve", "tensor_tensor", out=tab[pr, 0:nn, 0, :, hs], in0=a1, in1=a2, op=ALU.subtract)
                P.I("dve", "tensor_tensor", out=a1, in0=ar, in1=li, op=ALU.mult)
                P.I("pool", "tensor_tensor", out=a2, in0=ai, in1=lr, op=ALU.mult)
                if conjC:
                    P.I("dve", "tensor_tensor", out=a1, in0=a1, in1=a2, op=ALU.add)
                    P.I("dve", "tensor_scalar", out=tab[pr, 0:nn, 1, :, hs], in0=a1, scalar1=-1.0, scalar2=None,
                        op0=ALU.mult)
                else:
                    P.I("dve", "tensor_tensor", out=tab[pr, 0:nn, 1, :, hs], in0=a1, in1=a2, op=ALU.add)
        for n in range(8):
            for r in range(2):
                P.I("pe", "transpose", out=pw[:, n * 2 + r, :],
                    in_=XB.v(XB.t[:, n, r, :, :].rearrange("p m c -> p (m c)")), identity=C.ident[:])
        P.I("act", "activation", out=WT[:], in_=pw[:], func=AF.Copy)
        for tau in range(8):
            for m in range(4):
                for r in range(2):
                    P.I("pe", "matmul", out=pk[m * 32:(m + 1) * 32, tau, :], lhsT=XB[:, tau, r, m, :],
                        rhs=CB[:, 0, r, m, :], start=(r == 0), stop=(r == 1), tile_position=(0, m * 32))
        P.I("act", "activation", out=KT[:], in_=pk[:], func=AF.Copy)
        P.dma("sp", V(TabW, TabW.t[ch]), WT[:])
        P.dma("act", V(TabC, TabC.t[ch]), CB.v(CB.t[:].rearrange("p n r m c -> p (n r m c)")))
        P.dma("sp", V(TabK, TabK.t[ch]), KT.v(KT.t[:].rearrange("p t c -> p (t c)")))
    P.pop()


def s5_core_pipe(P, C, l, D, S, xT, carry, TabW, TabC, TabK):
    pzs = [P.psum([128, 2, 256], F32, "pz%d" % j) for j in range(2)]
    py = P.psum([128, 8, 2, 128], F32, "py")
    CBs = [P.sbuf([128, 9, 2, 4, 32], BF16, "CB%d" % j) for j in range(2)]
    WTs = [P.sbuf([128, 16, 128], BF16, "WT%d" % j) for j in range(2)]
    KTs = [P.sbuf([128, 8, 32], BF16, "KT%d" % j) for j in range(2)]
    ZWs = [[[P.sbuf([128, 2, 256], F32, "zw%d_%d_%d" % (pp, m, j)) for j in range(2)] for m in range(4)] for pp in range(2)]
    tt_ = [P.sbuf([128, 256], F32, "tt%d" % m) for m in range(4)]
    tu_ = [P.sbuf([128, 256], F32, "tu%d" % m) for m in range(4)]
    SP = P.sbuf([128, 4, 2, 256], BF16, "SP")
    SPr = [SP.sub() for m in range(4)]
    e1s = [P.sbuf([128, 1024], F32, "e1_%d" % k) for k in range(2)]
    e2s = [P.sbuf([128, 1024], F32, "e2_0")] * 2
    e3s = [P.sbuf([128, 1024], BF16, "e3_%d" % k) for k in range(2)]
    SIN = P.sbuf([128, 32, 2], F32, "SIN")
    INJ = P.sbuf([128, 32, 2], F32, "INJ")
    w1 = P.sbuf([128, 32], F32, "w1")
    w2 = P.sbuf([128, 32], F32, "w2")
    xTc = [xT.sub("xTc%d" % ch) for ch in range(8)]
    P.I("dve", "tensor_copy", out=SIN[:], in_=carry[:])
    lr8, li8 = S.LR[:, PIDX[8], :], S.LI[:, PIDX[8], :]
    P.I("dve", "tensor_tensor", out=w1[:], in0=SIN[:, :, 0], in1=lr8, op=ALU.mult)
    P.I("dve", "tensor_tensor", out=w2[:], in0=SIN[:, :, 1], in1=li8, op=ALU.mult)
    P.I("dve", "tensor_tensor", out=INJ[:, :, 0], in0=w1[:], in1=w2[:], op=ALU.subtract)
    P.I("dve", "tensor_tensor", out=w1[:], in0=SIN[:, :, 1], in1=lr8, op=ALU.mult)
    P.I("dve", "tensor_tensor", out=w2[:], in0=SIN[:, :, 0], in1=li8, op=ALU.mult)
    P.I("dve", "tensor_tensor", out=INJ[:, :, 1], in0=w1[:], in1=w2[:], op=ALU.add)
    curs = {}

    def T(ch):
        CB, KT, ZW, WT = CBs[ch % 2], KTs[ch % 2], ZWs[ch % 2], WTs[ch % 2]
        P.dma("sp", WT[:], V(TabW, TabW.t[ch]))
        P.dma("act", CB.v(CB.t[:].rearrange("p n r m c -> p (n r m c)")), V(TabC, TabC.t[ch]))
        P.dma("sp", KT.v(KT.t[:].rearrange("p t c -> p (t c)")), V(TabK, TabK.t[ch]))
        for m in range(4):
            rs_ = slice(m * 32, m * 32 + 32)
            for r in range(2):
                for i in range(8):
                    rhs = xT.v(xT.t[rs_, ch, :].rearrange("p (b i k) -> p b i k", b=2, i=8)[:, :, i, :])
                    P.I("pe", "matmul", out=pzs[m % 2][:, r, :], lhsT=WT[rs_, (7 - i) * 2 + r, :], rhs=rhs,
                        start=(i == 0), stop=(i == 7), tile_position=(m * 32, 0), xr=[xTc[ch]])
            P.I("act", "activation", out=ZW[m][0][:], in_=pzs[m % 2][:], func=AF.Copy)

    def Sx(ch):
        ZW = ZWs[ch % 2]
        cur = [0, 0, 0, 0]
        for m in range(4):
            gp = ch * 4 + m
            P.I("pool", "tensor_tensor", out=ZW[m][0][:, :, 0], in0=ZW[m][0][:, :, 0], in1=INJ[:, gp, :], op=ALU.add)
        for s_ in range(8):
            d = 1 << s_
            pidx = PIDX[8 * d]
            for m in range(4):
                gp = ch * 4 + m
                a, b_ = ZW[m][cur[m]], ZW[m][1 - cur[m]]
                lr = S.LR[:, pidx, gp:gp + 1]
                P.I("pool", "tensor_copy", out=b_[:, :, 0:d], in_=a[:, :, 0:d])
                P.I("dve", "scalar_tensor_tensor", out=tt_[m][:, 0:256 - d], in0=a[:, 0, 0:256 - d], scalar=lr,
                    in1=a[:, 0, d:256], op0=ALU.mult, op1=ALU.add)
                P.I("dve", "scalar_tensor_tensor", out=tu_[m][:, 0:256 - d], in0=a[:, 1, 0:256 - d], scalar=lr,
                    in1=a[:, 1, d:256], op0=ALU.mult, op1=ALU.add)
            for m in range(4):
                gp = ch * 4 + m
                a, b_ = ZW[m][cur[m]], ZW[m][1 - cur[m]]
                li, nli = S.LI[:, pidx, gp:gp + 1], S.NLI[:, pidx, gp:gp + 1]
                P.I("dve", "scalar_tensor_tensor", out=b_[:, 0, d:256], in0=a[:, 1, 0:256 - d], scalar=nli,
                    in1=tt_[m][:, 0:256 - d], op0=ALU.mult, op1=ALU.add)
                P.I("dve", "scalar_tensor_tensor", out=b_[:, 1, d:256], in0=a[:, 0, 0:256 - d], scalar=li,
                    in1=tu_[m][:, 0:256 - d], op0=ALU.mult, op1=ALU.add)
                cur[m] = 1 - cur[m]
        for m in range(4):
            gp = ch * 4 + m
            fin = ZW[m][cur[m]]
            P.I("pool", "tensor_copy", out=carry[:, gp, :], in_=fin[:, :, 255])
            P.I("act", "activation", out=V(SPr[m], SP.t[:, m, :, 1:256]), in_=fin[:, :, 0:255], func=AF.Copy)
            P.I("pool", "tensor_copy", out=V(SPr[m], SP.t[:, m, :, 0]), in_=SIN[:, gp, :])

    def Y(ch):
        CB, KT = CBs[ch % 2], KTs[ch % 2]
        for m in range(4):
            rs_ = slice(m * 32, m * 32 + 32)
            for j in range(8):
                for i in range(j + 1):
                    rhs = xT.v(xT.t[rs_, ch, :].rearrange("p (b i k) -> p b i k", b=2, i=8)[:, :, i, :])
                    P.I("pe", "matmul", out=py[rs_, j, :, :], lhsT=KT[rs_, j - i, :], rhs=rhs,
                        start=(i == 0), stop=False, tile_position=(m * 32, m * 32), xr=[xTc[ch]])
                for r in range(2):
                    P.I("pe", "matmul", out=py[rs_, j, :, :], lhsT=CB[:, j + 1, r, m, :],
                        rhs=V(SPr[m], SP.t[:, m, r, :].rearrange("p (b k) -> p b k", b=2)),
                        start=False, stop=(r == 1), tile_position=(0, m * 32))

    def E(ch):
        for b in range(2):
            uv = xT.v(xT.t[:, ch, b * 1024:(b + 1) * 1024].rearrange("p (j k) -> p j k", j=8))
            e1v = e1s[b].v(e1s[b].t[:, :].rearrange("p (j k) -> p j k", j=8))
            P.I("dve", "scalar_tensor_tensor", out=e1v, in0=uv, scalar=S.dcol[:, ch:ch + 1], in1=py[:, :, b, :],
                op0=ALU.mult, op1=ALU.add, xr=[xTc[ch]])
        for b in range(2):
            e1, e2, e3 = e1s[b], e2s[b], e3s[b]
            P.I("pool", "tensor_tensor", out=e2[:], in0=e1[:], in1=e1[:], op=ALU.mult)
            P.I("pool", "tensor_scalar", out=e2[:], in0=e2[:], scalar1=0.044715, scalar2=1.0, op0=ALU.mult, op1=ALU.add)
            P.I("pool", "tensor_tensor", out=e2[:], in0=e2[:], in1=e1[:], op=ALU.mult)
            P.I("act", "activation", out=e3[:], in_=e2[:], func=AF.Sigmoid, scale=1.5957691216)
        for b in range(2):
            P.I("dve", "tensor_tensor", out=V(xTc[ch], xT.t[:, ch, b * 1024:(b + 1) * 1024]), in0=e3s[b][:], in1=e1s[b][:],
                op=ALU.mult)

    T(0)
    for ch in range(8):
        if ch + 1 < 8:
            T(ch + 1)
        Sx(ch)
        if ch >= 1:
            E(ch - 1)
        Y(ch)
    E(7)


def glu(P, C, l, D, xT, xTr, pm):
    wv = D.ssm_w_glu.t[l].rearrange("(c p) n -> p c n", p=128)
    wb = [[P.sbuf([128, 8, 512], BF16, "wg%d%d" % (a, j)) for j in range(2)] for a in range(2)]
    sg = [P.sbuf([128, 512], F32, "sg%d" % j) for j in range(2)]
    n = 0
    for half in range(2):
        P.dma("pool", wb[half][0][:], V(D.ssm_w_glu, wv[:, :, half * 512:(half + 1) * 512]))
        P.dma("pool", wb[half][1][:], V(D.ssm_w_glu, wv[:, :, 1024 + half * 512:1024 + (half + 1) * 512]))
        for tt in range(16):
            b, i = tt // 8, tt % 8
            pv, pg = pm[n % len(pm)], pm[(n + 1) % len(pm)]
            n += 2
            for (ps, w) in ((pv, wb[half][0]), (pg, wb[half][1])):
                for c in range(8):
                    P.I("pe", "matmul", out=ps[:], lhsT=V(xTr[tt], xT.t[:, c, tt * 128:(tt + 1) * 128]), rhs=w[:, c, :],
                        start=(c == 0), stop=(c == 7))
            s = sg[tt % 2]
            P.I("act", "activation", out=s[:], in_=pg[:], func=AF.Sigmoid)
            P.I("dve", "tensor_tensor", out=s[:], in0=s[:], in1=pv[:], op=ALU.mult)
            hv = V(C.h[b][i], C.hb[b].t[:, i, half * 512:(half + 1) * 512])
            P.I("pool", "tensor_tensor", out=hv, in0=hv, in1=s[:], op=ALU.add)


PARAMS = [("mix_norm", [4, 1024]), ("mlp_norm", [4, 1024]), ("mlp_w1", [4, 1024, 4096]), ("mlp_w2", [4, 4096, 1024]),
          ("ssm_log_dt", [2, 64]), ("ssm_a_re", [2, 64, 64]), ("ssm_a_im", [2, 64, 64]),
          ("ssm_b_re", [2, 64, 64, 16]), ("ssm_b_im", [2, 64, 64, 16]), ("ssm_c_re", [2, 64, 16, 64]),
          ("ssm_c_im", [2, 64, 16, 64]), ("ssm_d", [2, 1024]), ("ssm_w_glu", [2, 1024, 2048]), ("kv_norm", [1024]),
          ("w_kvf", [1024, 2064]), ("b_f", [16]), ("attn_wq", [2, 1024, 1024]), ("attn_wo", [2, 1024, 1024]),
          ("final_norm", [1024])]


def declare(P, names):
    D = Ctx()
    for (n, shp) in PARAMS:
        if n in names:
            setattr(D, n, P.dram(n, shp, F32, "ExternalInput"))
    return D


def kv_stage(P, C, D, xT, xTr, KTo, Vo, Flo, Fto):
    P.push()
    pst = [P.psum([128, 8, 128], BF16, "pst%d" % j) for j in range(2)]
    gain = load_bcast(P, D.kv_norm.t[:], D.kv_norm, 1024, "kg")
    rmsnorm_T(P, C, gain, xT, xTr, pst)
    P.pop()
    P.push()
    pm = [P.psum([128, 512], F32, "pm%d" % j) for j in range(4)]
    wsrc = D.w_kvf.t.rearrange("(c p) n -> p c n", p=128)
    wk = P.sbuf([128, 8, 1024], BF16, "wk")
    wv = P.sbuf([128, 8, 1024], BF16, "wv")
    wf = P.sbuf([128, 8, 16], BF16, "wf")
    P.dma("pool", wk[:], V(D.w_kvf, wsrc[:, :, 0:1024]))
    P.dma("pool", wv[:], V(D.w_kvf, wsrc[:, :, 1024:2048]))
    P.dma("pool", wf[:], V(D.w_kvf, wsrc[:, :, 2048:2064]))
    ktb = [P.sbuf([128, 2048], BF16, "ktb%d" % j) for j in range(2)]
    vb = [P.sbuf([128, 1024], BF16, "vb%d" % j) for j in range(2)]
    n = 0
    for hp in range(8):
        kt = ktb[hp % 2]
        for tg in range(4):
            ps = pm[n % 4]
            n += 1
            for c in range(8):
                P.I("pe", "matmul", out=ps[:], lhsT=wk[:, c, hp * 128:(hp + 1) * 128],
                    rhs=xT[:, c, tg * 512:(tg + 1) * 512], start=(c == 0), stop=(c == 7), xr=xTr[tg * 4:tg * 4 + 4])
            if n % 2:
                P.I("act", "activation", out=kt[:, tg * 512:(tg + 1) * 512], in_=ps[:], func=AF.Copy)
            else:
                P.I("dve", "tensor_copy", out=kt[:, tg * 512:(tg + 1) * 512], in_=ps[:])
        P.dma("sp", V(KTo, KTo.t[2 * hp]), kt[0:64, :])
        P.dma("sp", V(KTo, KTo.t[2 * hp + 1]), kt[64:128, :])
    for tt in range(16):
        v_ = vb[tt % 2]
        for half in range(2):
            ps = pm[n % 4]
            n += 1
            for c in range(8):
                P.I("pe", "matmul", out=ps[:], lhsT=V(xTr[tt], xT.t[:, c, tt * 128:(tt + 1) * 128]),
                    rhs=wv[:, c, half * 512:(half + 1) * 512], start=(c == 0), stop=(c == 7))
            if n % 2:
                P.I("act", "activation", out=v_[:, half * 512:(half + 1) * 512], in_=ps[:], func=AF.Copy)
            else:
                P.I("dve", "tensor_copy", out=v_[:, half * 512:(half + 1) * 512], in_=ps[:])
        P.dma("sp", V(Vo, Vo.t[tt * 128:(tt + 1) * 128, :]), v_[:])
    LF = P.sbuf([16, 2048], F32, "LF")
    nbf = P.sbuf([16, 1], F32, "nbf")
    P.dma("sp", nbf[:], V(D.b_f, D.b_f.t.rearrange("(h o) -> h o", o=1)))
    P.I("dve", "tensor_scalar", out=nbf[:], in0=nbf[:], scalar1=-1.0, scalar2=None, op0=ALU.mult)
    et = [P.sbuf([16, 512], F32, "et%d" % j) for j in range(2)]
    for tg in range(4):
        ps = pm[n % 4]
        n += 1
        for c in range(8):
            P.I("pe", "matmul", out=ps[0:16, :], lhsT=wf[:, c, :], rhs=xT[:, c, tg * 512:(tg + 1) * 512],
                start=(c == 0), stop=(c == 7), xr=xTr[tg * 4:tg * 4 + 4])
        P.I("act", "activation", out=et[tg % 2][:], in_=ps[0:16, :], func=AF.Exp, bias=nbf[:, 0:1], scale=-1.0)
        P.I("act", "activation", out=LF[:, tg * 512:(tg + 1) * 512], in_=et[tg % 2][:], func=AF.Ln, bias=1.0, scale=1.0)
    ones = P.sbuf([16, 128], F32, "ones16")
    _memset(P, "pool", ones[:], 1.0)
    PI = P.sbuf([16, 2, 128], F32, "PI")
    EX = P.sbuf([16, 2, 128], F32, "EX")
    L4 = LF.t[:, :].rearrange("h (b i k) -> h b i k", b=2, i=8)
    for b in range(2):
        for i in range(1, 8):
            P.I("dve", "tensor_tensor", out=LF.v(L4[:, b, i, :]), in0=LF.v(L4[:, b, i, :]), in1=LF.v(L4[:, b, i - 1, :]),
                op=ALU.add)
        init = 0.0 if b == 0 else PI[:, 0, 127:128]
        P.I("dve", "tensor_tensor_scan", out=PI[:, b, :], data0=ones[:], data1=LF.v(L4[:, b, 7, :]), initial=init,
            op0=ALU.mult, op1=ALU.add)
        P.I("dve", "tensor_tensor", out=EX[:, b, :], in0=PI[:, b, :], in1=LF.v(L4[:, b, 7, :]), op=ALU.subtract)
        P.I("dve", "tensor_tensor", out=LF.v(L4[:, b, :, :]), in0=LF.v(L4[:, b, :, :]),
            in1=EX.v(EX.t[:, b, :].unsqueeze(1).to_broadcast([16, 8, 128])), op=ALU.add)
    P.I("dve", "tensor_scalar", out=LF[:], in0=LF[:], scalar1=-1.0, scalar2=None, op0=ALU.mult)
    P.dma("sp", Flo[:], LF[:])
    P.dma("sp", Fto[:], LF[:, 2047:2048], allow_slow_non_contiguous=True)
    P.pop()


def split3(P, src, dst_dram_views, tmp):
    hi, r1, mid, lo = tmp
    n = None
    P.I("dve", "tensor_copy", out=hi[:], in_=src)
    P.I("dve", "tensor_tensor", out=r1[:], in0=src, in1=hi[:], op=ALU.subtract)
    P.I("dve", "tensor_copy", out=mid[:], in_=r1[:])
    P.I("dve", "tensor_tensor", out=r1[:], in0=r1[:], in1=mid[:], op=ALU.subtract)
    P.I("dve", "tensor_copy", out=lo[:], in_=r1[:])
    for piece, dv in zip((hi, mid, lo), dst_dram_views):
        P.dma("sp", dv, piece[:])


def att_prep(P, C, A):
    P.push()
    ft = P.sbuf([16, 4], F32, "ft")
    sq = P.sbuf([16, 4], F32, "sq")
    off = P.sbuf([16, 4], F32, "off")
    oo = P.sbuf([16, 1], F32, "oo")
    P.dma("sp", ft[:], A.Ftg[:])
    P.dma("sp", sq[:], A.selq[:])
    _memset(P, "pool", off[:], 0.0)
    for j in range(1, 4):
        P.I("dve", "tensor_tensor", out=off[:, j:j + 1], in0=off[:, j - 1:j], in1=ft[:, j - 1:j], op=ALU.add)
    P.I("dve", "tensor_tensor", out=sq[:], in0=sq[:], in1=ft[:], op=ALU.mult)
    P.I("dve", "tensor_reduce", out=oo[:], in_=sq[:], axis=mybir.AxisListType.X, op=ALU.add)
    fl = [P.sbuf([16, 2048], F32, "fl%d" % j) for j in range(2)]
    tmp = (P.sbuf([16, 2048], BF16, "s_hi"), P.sbuf([16, 2048], F32, "s_r1"), P.sbuf([16, 2048], BF16, "s_mid"),
           P.sbuf([16, 2048], BF16, "s_lo"))
    for j in range(4):
        f = fl[j % 2]
        P.dma("act", f[:], V(A.Flg, A.Flg.t[j]))
        P.I("dve", "tensor_scalar", out=f[:], in0=f[:], scalar1=off[:, j:j + 1], scalar2=-1.0, op0=ALU.add, op1=ALU.mult)
        split3(P, f[:], [V(A.Hd, A.Hd.t[:, r, j * 2048:(j + 1) * 2048]) for r in range(3)], tmp)
    f = fl[0]
    P.dma("act", f[:], A.Fown[:])
    P.I("dve", "tensor_scalar", out=f[:], in0=f[:], scalar1=oo[:, 0:1], scalar2=None, op0=ALU.add)
    split3(P, f[:], [V(A.Gd, A.Gd.t[:, r, :]) for r in range(3)], tmp)
    P.pop()


def attention(P, C, j, D, A, xT, xTr):
    P.push()
    pss = [P.psum([128, 512], F32, "pss%d" % i) for i in range(4)]
    ppo = [P.psum([128, 512], F32, "ppo%d" % i) for i in range(2)]
    ps2 = [P.psum([128, 512], F32, "ps2%d" % i) for i in range(2)]
    psb = ps2[1]
    AB = P.sbuf([128, 32], F32, "AB")
    P.dma("sp", AB[:], A.AB[:])
    KA = [P.sbuf([128, 8192], BF16, "KA%d" % i) for i in range(2)]
    VA = [P.sbuf([128, 64, 65], BF16, "VA%d" % i) for i in range(2)]
    QA = [P.sbuf([128, 2048], BF16, "QA%d" % i) for i in range(2)]
    for i in range(2):
        _memset(P, "pool", KA[i][:], 0.0)
        _memset(P, "pool", QA[i][:], 0.0)
        _memset(P, "pool", KA[i][64:70, :], 1.0)
        _memset(P, "pool", QA[i][64:70, :], 1.0)
        _memset(P, "pool", VA[i][:, :, 64:65], 1.0)
    zt = P.sbuf([128, 512], BF16, "zt")
    _memset(P, "pool", zt[:], 0.0)
    TRI = P.sbuf([128, 8, 2, 512], BF16, "TRI")
    for ik in range(8):
        for a in range(2):
            P.I("pool", "affine_select", out=TRI.v(TRI.t[:, ik, a, :].rearrange("p (i k) -> p i k", i=4)),
                in_=zt.v(zt.t[:, :].rearrange("p (i k) -> p i k", i=4)), pattern=[[1, 4], [8, 128]], base=4 * a - ik,
                channel_multiplier=-8, compare_op=ALU.is_ge, fill=NEG)
    ones1 = P.sbuf([128, 64], F32, "ones1")
    _memset(P, "pool", ones1[:], 1.0)
    wq = [P.sbuf([128, 8, 64], BF16, "wq%d" % i) for i in range(2)]
    wo = [P.sbuf([128, 1024], BF16, "wo%d" % i) for i in range(2)]
    for i in range(2):
        _memset(P, "pool", wo[i][:], 0.0)
    tmpf = [P.sbuf([128, 512], F32, "tmpf%d" % i) for i in range(4)]
    pT = [P.sbuf([128, 512], BF16, "pT%d" % i) for i in range(8)]
    osb = [P.sbuf([128, 512], F32, "osb%d" % i) for i in range(2)]
    rc = [P.sbuf([128, 512], F32, "rc%d" % i) for i in range(2)]
    otn = [P.sbuf([128, 512], BF16, "otn%d" % i) for i in range(2)]
    for i in range(2):
        _memset(P, "pool", otn[i][:], 0.0)
    wqv = D.attn_wq.t[j].rearrange("(c p) n -> p c n", p=128)
    vgv = A.Vg.t.rearrange("(kt p) (h d) -> p kt h d", p=128, h=16)
    def prep(h):
        ka, va, qa = KA[h % 2], VA[h % 2], QA[h % 2]
        P.dma("sp", ka[0:64, :], V(A.KTg, A.KTg.t[h]))
        P.dma("sp", ka[67:70, :], V(A.Hd, A.Hd.t[h]))
        for q4 in range(4):
            P.dma("act", va[:, q4 * 16:(q4 + 1) * 16, 0:64], V(A.Vg, vgv[:, q4 * 16:(q4 + 1) * 16, h, :]))
        P.dma("sp", qa[64:67, :], V(A.Gd, A.Gd.t[h]))
        P.dma("pool", wq[h % 2][:], V(D.attn_wq, wqv[:, :, h * 64:(h + 1) * 64]))
        P.dma("pool", wo[h % 2][0:64, :], V(D.attn_wo, D.attn_wo.t[j, h * 64:(h + 1) * 64, :]))
        for tg in range(4):
            ps = ps2[tg % 2]
            for c in range(8):
                P.I("pe", "matmul", out=ps[0:64, :], lhsT=wq[h % 2][:, c, :], rhs=xT[:, c, tg * 512:(tg + 1) * 512],
                    start=(c == 0), stop=(c == 7), xr=xTr[tg * 4:tg * 4 + 4])
            P.I("act", "activation", out=qa[0:64, tg * 512:(tg + 1) * 512], in_=ps[0:64, :], func=AF.Copy, scale=0.125)

    def score(h, qg, kt, idx):
        ka, qa = KA[h % 2], QA[h % 2]
        b, a = qg // 2, qg % 2
        kb, ik = kt // 8, kt % 8
        ps, tf, pt_ = pss[idx % 4], tmpf[idx % 4], pT[idx % 8]
        P.I("pe", "matmul", out=ps[:], lhsT=ka[:, kt * 128:(kt + 1) * 128], rhs=qa[:, qg * 512:(qg + 1) * 512],
            start=True, stop=True)
        ci = kb * 2 + b
        if kb % 2 == b:
            P.I("dve", "scalar_tensor_tensor", out=tf[:], in0=TRI[:, ik, a, :], scalar=AB[:, ci:ci + 1], in1=ps[:],
                op0=ALU.mult, op1=ALU.add)
            P.I("act", "activation", out=pt_[:], in_=tf[:], func=AF.Exp, bias=AB[:, 16 + ci:17 + ci], scale=1.0)
        else:
            P.I("act", "activation", out=pt_[:], in_=ps[:], func=AF.Exp, bias=AB[:, 16 + ci:17 + ci], scale=1.0)

    def pv(h, qg, kt, idx):
        P.I("pe", "matmul", out=ppo[qg % 2][0:65, :], lhsT=VA[h % 2][:, kt, :], rhs=pT[idx % 8][:], start=(kt == 0),
            stop=(kt == 63))

    def ep1(h, qg):
        P.I("act", "activation", out=osb[qg % 2][0:65, :], in_=ppo[qg % 2][0:65, :], func=AF.Copy)
        P.I("dve", "reciprocal", out=rc[qg % 2][64:65, :], in_=osb[qg % 2][64:65, :])

    def ep2(h, qg):
        P.I("pe", "matmul", out=psb[0:64, :], lhsT=ones1[64:65, 0:64], rhs=rc[qg % 2][64:65, :], start=True, stop=True)
        P.I("dve", "tensor_tensor", out=otn[qg % 2][0:64, :], in0=osb[qg % 2][0:64, :], in1=psb[0:64, :], op=ALU.mult)

    def ep3(h, qg):
        on = otn[qg % 2]
        for t4 in range(4):
            tt = qg * 4 + t4
            bb, ii = tt // 8, tt % 8
            for half in range(2):
                p2 = ps2[(t4 * 2 + half) % 2]
                P.I("pe", "matmul", out=p2[:], lhsT=on[:, t4 * 128:(t4 + 1) * 128],
                    rhs=wo[h % 2][:, half * 512:(half + 1) * 512], start=True, stop=True)
                hv = V(C.h[bb][ii], C.hb[bb].t[:, ii, half * 512:(half + 1) * 512])
                P.I("dve", "tensor_tensor", out=hv, in0=hv, in1=p2[:], op=ALU.add)

    units = [(h, qg, kt) for h in range(16) for qg in range(4) for kt in range(64)]
    n = len(units)
    DEPTH = 6
    events = {}

    def at(i_, fn):
        events.setdefault(i_, []).append(fn)
    prep(0)
    for idx in range(n + DEPTH + 8):
        for fn in events.pop(idx, []):
            fn()
        if idx < n:
            h, qg, kt = units[idx]
            if qg == 0 and kt == 0 and h + 1 < 16:
                at(idx + 96, lambda h=h: prep(h + 1))
            score(h, qg, kt, idx)
        jx = idx - DEPTH
        if 0 <= jx < n:
            h, qg, kt = units[jx]
            pv(h, qg, kt, jx)
            if kt == 63:
                ep1(h, qg)
                at(idx + 2, lambda h=h, qg=qg: ep2(h, qg))
                at(idx + 4, lambda h=h, qg=qg: ep3(h, qg))
    assert not events
    P.pop()


def final_norm(P, C, D, yout):
    P.push()
    gain = load_bcast(P, D.final_norm.t[:], D.final_norm, 1024, "fg")
    ss = P.sbuf([128, 16], F32, "fss")
    rs = P.sbuf([128, 16], F32, "frs")
    junk = P.sbuf([128, 1024], F32, "fjunk")
    ob = [P.sbuf([128, 1024], F32, "fob%d" % j) for j in range(2)]
    for b in range(2):
        for i in range(8):
            jx = b * 8 + i
            P.I("act", "activation", out=junk[:], in_=V(C.h[b][i], C.hb[b].t[:, i, :]), func=AF.Square,
                accum_out=ss[:, jx:jx + 1])
    P.I("dve", "tensor_scalar", out=rs[:], in0=ss[:], scalar1=1.0 / 1024, scalar2=1e-6, op0=ALU.mult, op1=ALU.add)
    P.I("act", "activation", out=rs[:], in_=rs[:], func=AF.Sqrt)
    P.I("dve", "reciprocal", out=rs[:], in_=rs[:])
    yv = yout.t.rearrange("(b k i) d -> b k i d", b=2, k=128)
    for b in range(2):
        for i in range(8):
            jx = b * 8 + i
            o = ob[jx % 2]
            P.I("dve", "scalar_tensor_tensor", out=o[:], in0=V(C.h[b][i], C.hb[b].t[:, i, :]), scalar=rs[:, jx:jx + 1],
                in1=gain[:], op0=ALU.mult, op1=ALU.mult)
            P.dma("sp", V(yout, yv[b, :, i, :]), o[:])
    P.pop()


PARAMS = [("mix_norm", [4, 1024]), ("mlp_norm", [4, 1024]), ("mlp_w1", [4, 1024, 4096]), ("mlp_w2", [4, 4096, 1024]),
          ("ssm_log_dt", [2, 64]), ("ssm_a_re", [2, 64, 64]), ("ssm_a_im", [2, 64, 64]),
          ("ssm_b_re", [2, 64, 64, 16]), ("ssm_b_im", [2, 64, 64, 16]), ("ssm_c_re", [2, 64, 16, 64]),
          ("ssm_c_im", [2, 64, 16, 64]), ("ssm_d", [2, 1024]), ("ssm_w_glu", [2, 1024, 2048]), ("kv_norm", [1024]),
          ("w_kvf", [1024, 2064]), ("b_f", [16]), ("attn_wq", [2, 1024, 1024]), ("attn_wo", [2, 1024, 1024]),
          ("final_norm", [1024])]
S5_NAMES = ["mix_norm", "mlp_norm", "mlp_w1", "mlp_w2", "ssm_log_dt", "ssm_a_re", "ssm_a_im", "ssm_b_re",
            "ssm_b_im", "ssm_c_re", "ssm_c_im", "ssm_d", "ssm_w_glu"]


def declare(P, names):
    D = Ctx()
    for (n, shp) in PARAMS:
        if n in names:
            setattr(D, n, P.dram(n, shp, F32, "ExternalInput"))
    return D


def build_s5(stage):
    nc = bass.Bass("TRN2", target_bir_lowering=False)
    with ExitStack() as st:
        P = Prog(nc, st)
        C = Ctx()
        names = list(S5_NAMES) + (["kv_norm", "w_kvf", "b_f"] if stage == 3 else [])
        if stage == 1:
            names = [n for n in names if n not in ("mlp_norm", "mlp_w1", "mlp_w2", "ssm_w_glu")]
        D = declare(P, names)
        hin = P.dram("hin", [2048, 1024], F32, "ExternalInput")
        make_consts(P, C)
        load_h(P, C, hin)
        xT = P.sbuf([128, 8, 2048], BF16, "xT")
        xTr = [xT.sub("xTr%d" % j) for j in range(16)]
        if stage == 1:
            Fout = P.dram("Fout", [128, 64], F32, "ExternalOutput")
            lA = 0
        else:
            Fall = P.dram("Fall", [8, 128, 64], F32, "ExternalInput")
            selS = P.dram("selS", [128, 24], F32, "ExternalInput")
            hout = P.dram("hout", [2048, 1024], F32, "ExternalOutput")
            lB = stage - 2
            if stage == 2:
                Fout = P.dram("Fout", [128, 64], F32, "ExternalOutput")
                lA = 1
            else:
                KTo = P.dram("KTo", [16, 64, 2048], BF16, "ExternalOutput")
                Vo = P.dram("Vo", [2048, 1024], BF16, "ExternalOutput")
                Flo = P.dram("Flo", [16, 2048], F32, "ExternalOutput")
                Fto = P.dram("Fto", [16, 1], F32, "ExternalOutput")

        need = {1: [0], 2: [0, 1], 3: [1]}[stage]
        SP_ = {l: s5_params(P, C, l, D) for l in need}

        def stageA(l):
            P.push()
            pst = [P.psum([128, 8, 128], BF16, "pst%d" % j) for j in range(2)]
            gain = load_bcast(P, D.mix_norm.t[l, :], D.mix_norm, 1024, "g")
            rmsnorm_T(P, C, gain, xT, xTr, pst)
            P.pop()
            return SP_[l]

        if stage >= 2:
            S = stageA(lB)
            P.push()
            s5_core(P, C, lB, D, S, xT, xTr, "B", Fall=Fall, selS=selS)
            P.pop()
            P.push()
            pm = [P.psum([128, 512], F32, "pm%d" % j) for j in range(4)]
            glu(P, C, lB, D, xT, xTr, pm)
            P.pop()
            P.push()
            pst = [P.psum([128, 8, 128], BF16, "pst%d" % j) for j in range(2)]
            pm = [P.psum([128, 512], F32, "pm%d" % j) for j in range(4)]
            mlp(P, C, lB, D, xT, xTr, pst, pm)
            P.pop()
            store_h(P, C, hout)
        if stage <= 2:
            S = stageA(lA)
            P.push()
            s5_core(P, C, lA, D, S, xT, xTr, "A", Fout=Fout)
            P.pop()
        else:
            kv_stage(P, C, D, xT, xTr, KTo, Vo, Flo, Fto)
        P.finish()
    return nc, names


def build_att():
    nc = bass.Bass("TRN2", target_bir_lowering=False)
    with ExitStack() as st:
        P = Prog(nc, st)
        C = Ctx()
        names = ["mix_norm", "mlp_norm", "mlp_w1", "mlp_w2", "attn_wq", "attn_wo", "final_norm"]
        D = declare(P, names)
        A = Ctx()
        hin = P.dram("hin", [2048, 1024], F32, "ExternalInput")
        A.KTg = P.dram("KTg", [16, 64, 8192], BF16, "ExternalInput")
        A.Vg = P.dram("Vg", [8192, 1024], BF16, "ExternalInput")
        A.Flg = P.dram("Flg", [4, 16, 2048], F32, "ExternalInput")
        A.Ftg = P.dram("Ftg", [16, 4], F32, "ExternalInput")
        A.Fown = P.dram("Fown", [16, 2048], F32, "ExternalInput")
        A.selq = P.dram("selq", [16, 4], F32, "ExternalInput")
        A.AB = P.dram("ABsel", [128, 32], F32, "ExternalInput")
        A.Hd = P.dram("Hd", [16, 3, 8192], BF16, "Internal")
        A.Gd = P.dram("Gd", [16, 3, 2048], BF16, "Internal")
        yout = P.dram("yout", [2048, 1024], F32, "ExternalOutput")
        make_consts(P, C)
        load_h(P, C, hin)
        xT = P.sbuf([128, 8, 2048], BF16, "xT")
        xTr = [xT.sub("xTr%d" % j) for j in range(16)]
        att_prep(P, C, A)
        for j in range(2):
            l = 2 + j
            P.push()
            pst = [P.psum([128, 8, 128], BF16, "pst%d" % k) for k in range(2)]
            gain = load_bcast(P, D.mix_norm.t[l, :], D.mix_norm, 1024, "g")
            rmsnorm_T(P, C, gain, xT, xTr, pst)
            P.pop()
            attention(P, C, j, D, A, xT, xTr)
            P.push()
            pst = [P.psum([128, 8, 128], BF16, "pst%d" % k) for k in range(2)]
            pm = [P.psum([128, 512], F32, "pm%d" % k) for k in range(4)]
            mlp(P, C, l, D, xT, xTr, pst, pm)
            P.pop()
        final_norm(P, C, D, yout)
        P.finish()
    return nc, names


def build_fused(debug=False):
    nc = bass.Bass("TRN2", target_bir_lowering=False)
    with ExitStack() as st:
        P = Prog(nc, st)
        C = Ctx()
        names = [n for n, _ in PARAMS]
        D = declare(P, names)
        xfull = P.dram("xfull", [8192, 1024], F32, "ExternalInput")
        selseg = P.dram("selseg", [128, 4], F32, "ExternalInput")
        A = Ctx()
        A.selq = P.dram("selq", [16, 4], F32, "ExternalInput")
        A.AB = P.dram("ABsel", [128, 32], F32, "ExternalInput")
        A.KTg = P.dram("KTd", [16, 64, 8192], BF16, "Internal")
        A.Vg = P.dram("Vd", [8192, 1024], BF16, "Internal")
        dk = "ExternalOutput" if debug else "Internal"
        A.Flg = P.dram("Fld", [4, 16, 2048], F32, dk)
        A.Ftg = P.dram("Ftd", [16, 4], F32, dk)
        A.Fown = P.dram("Fownd", [16, 2048], F32, dk)
        A.Hd = P.dram("Hd", [16, 3, 8192], BF16, "Internal")
        A.Gd = P.dram("Gd", [16, 3, 2048], BF16, "Internal")
        H2d = P.dram("H2d", [4, 2048, 1024], F32, dk)
        TabW = [P.dram("TabW%d" % l, [8, 128, 2048], BF16, "Internal") for l in range(2)]
        TabC = [P.dram("TabC%d" % l, [8, 128, 2304], BF16, "Internal") for l in range(2)]
        TabK = [P.dram("TabK%d" % l, [8, 128, 256], BF16, "Internal") for l in range(2)]
        yout = P.dram("yout", [2048, 1024], F32, "ExternalOutput")
        make_consts(P, C)
        xT = P.sbuf([128, 8, 2048], BF16, "xT")
        xTr = [xT.sub("xTr%d" % j) for j in range(16)]
        C.hb = [P.sbuf([128, 8, 1024], F32, "h%d" % b) for b in range(2)]
        C.h = [[C.hb[b].sub("h%d_%d" % (b, i)) for i in range(8)] for b in range(2)]
        P.push()
        SPr_ = {l: s5_params(P, C, l, D) for l in range(2)}
        for l in range(2):
            s5_tables(P, C, SPr_[l], TabW[l], TabC[l], TabK[l])
        carry = [P.sbuf([128, 32, 2], F32, "carry%d" % l) for l in range(2)]
        for l in range(2):
            _memset(P, "pool", carry[l][:], 0.0)
        for seg in range(4):
            xs = Buf(xfull.t[seg * 2048:(seg + 1) * 2048, :], "xseg%d" % seg)
            load_h(P, C, xs)
            for l in range(2):
                P.push()
                pst = [P.psum([128, 8, 128], BF16, "pst%d" % j) for j in range(2)]
                gain = load_bcast(P, D.mix_norm.t[l, :], D.mix_norm, 1024, "g")
                rmsnorm_T(P, C, gain, xT, xTr, pst)
                P.pop()
                P.push()
                s5_core_pipe(P, C, l, D, SPr_[l], xT, carry[l], TabW[l], TabC[l], TabK[l])
                P.pop()
                P.push()
                pm = [P.psum([128, 512], F32, "pm%d" % j) for j in range(4)]
                glu(P, C, l, D, xT, xTr, pm)
                P.pop()
                P.push()
                pst = [P.psum([128, 8, 128], BF16, "pst%d" % j) for j in range(2)]
                pm = [P.psum([128, 512], F32, "pm%d" % j) for j in range(4)]
                mlp(P, C, l, D, xT, xTr, pst, pm)
                P.pop()
            kv_stage(P, C, D, xT, xTr,
                     Buf(A.KTg.t[:, :, seg * 2048:(seg + 1) * 2048], "ktseg"), Buf(A.Vg.t[seg * 2048:(seg + 1) * 2048, :], "vseg"),
                     Buf(A.Flg.t[seg], "flseg"), Buf(A.Ftg.t[:, seg:seg + 1], "ftseg"))
            store_h(P, C, Buf(H2d.t[seg], "h2seg"))
            P.barrier()
        P.pop()
        P.push()
        sel = P.sbuf([128, 4], F32, "sel")
        P.dma("sp", sel[:], selseg[:])
        tmp = [P.sbuf([128, 1024], F32, "seltmp%d" % j) for j in range(3)]
        n = 0
        for b in range(2):
            for i in range(8):
                hv = V(C.h[b][i], C.hb[b].t[:, i, :])
                for seg in range(4):
                    t = tmp[n % 3]
                    n += 1
                    sv = H2d.t[seg].rearrange("(b k i) d -> b k i d", b=2, k=128)[b, :, i, :]
                    P.dma(("sp", "act")[n % 2], t[:], V(H2d, sv))
                    if seg == 0:
                        P.I("dve", "tensor_scalar", out=hv, in0=t[:], scalar1=sel[:, 0:1], scalar2=None, op0=ALU.mult)
                    else:
                        P.I("dve", "scalar_tensor_tensor", out=hv, in0=t[:], scalar=sel[:, seg:seg + 1], in1=hv,
                            op0=ALU.mult, op1=ALU.add)
        facc = P.sbuf([16, 2048], F32, "facc")
        ftmp = [P.sbuf([16, 2048], F32, "ftmp%d" % j) for j in range(2)]
        for seg in range(4):
            t = ftmp[seg % 2]
            P.dma("sp", t[:], V(A.Flg, A.Flg.t[seg]))
            if seg == 0:
                P.I("dve", "tensor_scalar", out=facc[:], in0=t[:], scalar1=sel[0:16, 0:1], scalar2=None, op0=ALU.mult)
            else:
                P.I("dve", "scalar_tensor_tensor", out=facc[:], in0=t[:], scalar=sel[0:16, seg:seg + 1], in1=facc[:],
                    op0=ALU.mult, op1=ALU.add)
        P.dma("sp", A.Fown[:], facc[:])
        P.pop()
        att_prep(P, C, A)
        for j in range(2):
            l = 2 + j
            P.push()
            pst = [P.psum([128, 8, 128], BF16, "pst%d" % k) for k in range(2)]
            gain = load_bcast(P, D.mix_norm.t[l, :], D.mix_norm, 1024, "g")
            rmsnorm_T(P, C, gain, xT, xTr, pst)
            P.pop()
            attention(P, C, j, D, A, xT, xTr)
            P.push()
            pst = [P.psum([128, 8, 128], BF16, "pst%d" % k) for k in range(2)]
            pm = [P.psum([128, 512], F32, "pm%d" % k) for k in range(4)]
            mlp(P, C, l, D, xT, xTr, pst, pm)
            P.pop()
        final_norm(P, C, D, yout)
        P.finish()
    return nc, names


def sel_state(core):
    s = np.zeros((8, 3), np.float32)
    b, q = core // 4, core % 4
    for qq in range(q):
        s[4 * b + qq, q - 1 - qq] = 1.0
    return np.tile(s.reshape(1, 24), (128, 1))


def sel_att(core):
    q = core % 4
    selq = np.zeros((16, 4), np.float32)
    selq[:, :q] = 1.0
    ab = np.zeros((32,), np.float32)
    for kb in range(8):
        for qb in range(2):
            g = 2 * q + qb
            ab[kb * 2 + qb] = 1.0 if kb == g else 0.0
            ab[16 + kb * 2 + qb] = NEG if kb > g else 0.0
    return selq, np.tile(ab.reshape(1, 32), (128, 1))


_CACHE = {}


def get_prog(key, fn):
    if key not in _CACHE:
        _CACHE[key] = fn()
    return _CACHE[key]


def run(nc, in_maps):
    res = run_bass_kernel_spmd(nc, in_maps, core_ids=list(range(8)))
    return res.results


def kernel_unfused(**inputs):
    inp = {k: np.ascontiguousarray(np.asarray(v, dtype=np.float32)) for k, v in inputs.items()}
    x = inp["x"].reshape(8, 2048, 1024)
    nc1, n1 = get_prog("s1", lambda: build_s5(1))
    r1 = run(nc1, [dict({n: inp[n] for n in n1}, hin=x[c]) for c in range(8)])
    F0 = np.stack([r["Fout"] for r in r1])
    nc2, n2 = get_prog("s2", lambda: build_s5(2))
    r2 = run(nc2, [dict({n: inp[n] for n in n2}, hin=x[c], Fall=F0, selS=sel_state(c)) for c in range(8)])
    F1 = np.stack([r["Fout"] for r in r2])
    h1 = [r["hout"] for r in r2]
    nc3, n3 = get_prog("s3", lambda: build_s5(3))
    r3 = run(nc3, [dict({n: inp[n] for n in n3}, hin=h1[c], Fall=F1, selS=sel_state(c)) for c in range(8)])
    nc4, n4 = get_prog("s4", build_att2)
    r4 = run(nc4, [dict({n: inp[n] for n in n4}, **att2_inputs(c, r3)) for c in range(8)])
    y = np.zeros((2, 8, 1024, 1024), np.float32)
    for c in range(8):
        bt, q = c // 4, c % 4
        y[bt, q] = r4[c]["yout"][:1024]
        y[bt, 7 - q] = r4[c]["yout"][1024:]
    y = y.reshape(2, 8192, 1024)
    return y


NSLOT = 12


def att_prep2(P, C, A):
    P.push()
    ft = P.sbuf([16, 4], F32, "ft")
    off = P.sbuf([16, 4], F32, "off")
    sk = P.sbuf([16, NSLOT * 4], F32, "sk")
    sq = P.sbuf([16, 8], F32, "sq")
    offs = P.sbuf([16, NSLOT + 2], F32, "offs")
    P.dma("sp", ft[:], A.Ftg[:])
    P.dma("sp", sk[:], A.selk[:])
    P.dma("sp", sq[:], A.selq[:])
    _memset(P, "pool", off[:], 0.0)
    for j in range(1, 4):
        P.I("dve", "tensor_tensor", out=off[:, j:j + 1], in0=off[:, j - 1:j], in1=ft[:, j - 1:j], op=ALU.add)
    for s_ in range(NSLOT + 2):
        src = sk[:, s_ * 4:(s_ + 1) * 4] if s_ < NSLOT else sq[:, (s_ - NSLOT) * 4:(s_ - NSLOT + 1) * 4]
        P.I("dve", "tensor_tensor", out=src, in0=src, in1=ft[:], op=ALU.mult)
        P.I("dve", "tensor_reduce", out=offs[:, s_:s_ + 1], in_=src, axis=mybir.AxisListType.X, op=ALU.add)
    fl = [P.sbuf([16, 1024], F32, "fl%d" % j) for j in range(2)]
    tmp = (P.sbuf([16, 1024], BF16, "s_hi"), P.sbuf([16, 1024], F32, "s_r1"), P.sbuf([16, 1024], BF16, "s_mid"),
           P.sbuf([16, 1024], BF16, "s_lo"))
    for s_ in range(NSLOT):
        f = fl[s_ % 2]
        P.dma("act", f[:], V(A.Fkg, A.Fkg.t[s_]))
        P.I("dve", "tensor_scalar", out=f[:], in0=f[:], scalar1=offs[:, s_:s_ + 1], scalar2=-1.0, op0=ALU.add, op1=ALU.mult)
        split3(P, f[:], [V(A.Hd, A.Hd.t[:, r, s_ * 1024:(s_ + 1) * 1024]) for r in range(3)], tmp)
    for lb in range(2):
        f = fl[lb % 2]
        P.dma("act", f[:], V(A.Fq, A.Fq.t[:, lb * 1024:(lb + 1) * 1024]))
        P.I("dve", "tensor_scalar", out=f[:], in0=f[:], scalar1=offs[:, NSLOT + lb:NSLOT + lb + 1], scalar2=None,
            op0=ALU.add)
        split3(P, f[:], [V(A.Gd, A.Gd.t[:, r, lb * 1024:(lb + 1) * 1024]) for r in range(3)], tmp)
    P.pop()


def attention2(P, C, j, D, A, xT, xTr):
    P.push()
    NK = NSLOT * 1024
    pss = [P.psum([128, 2, 512], F32, "pss%d" % i) for i in range(2)]
    ppo = [P.psum([128, 512], F32, "ppo%d" % i) for i in range(2)]
    pmi = [P.psum([128, 512], F32, "pmi%d" % i) for i in range(2)]
    BM = P.sbuf([128, NSLOT], F32, "BM")
    P.dma("sp", BM[:], A.Bm[:])
    KA = [P.sbuf([70, 8192], BF16, "KA%d" % i) for i in range(2)]
    VA = [P.sbuf([128, 64, 65], BF16, "VA%d" % i) for i in range(2)]
    QA = [P.sbuf([70, 2048], BF16, "QA%d" % i) for i in range(2)]
    for i in range(2):
        _memset(P, "pool", KA[i][64:70, :], 1.0)
        _memset(P, "pool", QA[i][64:70, :], 1.0)
        _memset(P, "pool", VA[i][:, :, 64:65], 1.0)
    zt = P.sbuf([128, 512], BF16, "zt")
    _memset(P, "pool", zt[:], 0.0)
    TRI = P.sbuf([128, 2, 8, 512], BF16, "TRI")
    for ik in range(8):
        for a in range(2):
            P.I("pool", "affine_select", out=TRI.v(TRI.t[:, a, ik, :].rearrange("p (i k) -> p i k", i=4)),
                in_=zt.v(zt.t[:, :].rearrange("p (i k) -> p i k", i=4)), pattern=[[1, 4], [8, 128]], base=4 * a - ik,
                channel_multiplier=-8, compare_op=ALU.is_ge, fill=NEG)
    ones1 = P.sbuf([128, 64], F32, "ones1")
    _memset(P, "pool", ones1[:], 1.0)
    wq = [P.sbuf([128, 8, 64], BF16, "wq%d" % i) for i in range(2)]
    wo = [P.sbuf([64, 1024], BF16, "wo%d" % i) for i in range(2)]
    tmpf = [P.sbuf([128, 2, 512], F32, "tmpf0")] * 2
    NPT = 4
    pT = [P.sbuf([128, 2, 512], BF16, "pT%d" % i) for i in range(NPT)]
    osb = [P.sbuf([128, 512], F32, "osb%d" % i) for i in range(2)]
    rc = [P.sbuf([128, 512], F32, "rc0")] * 2
    otn = [P.sbuf([64, 512], BF16, "otn%d" % i) for i in range(2)]
    wqv = D.attn_wq.t[j].rearrange("(c p) n -> p c n", p=128)
    vgv = A.Vg.t.rearrange("(kt p) (h d) -> p kt h d", p=128, h=16)

    def prep(p):
        h, lb = p // 2, p % 2
        ka, va = KA[p % 2], VA[p % 2]
        s0, ns = (0, 4) if lb == 0 else (4, 8)
        P.dma("sp", ka[0:64, 0:ns * 1024], V(A.KTg, A.KTg.t[h, :, s0 * 1024:(s0 + ns) * 1024]))
        P.dma("sp", ka[67:70, 0:ns * 1024], V(A.Hd, A.Hd.t[h, :, s0 * 1024:(s0 + ns) * 1024]))
        for q4 in range(ns // 2):
            P.dma("act", va[:, q4 * 16:(q4 + 1) * 16, 0:64], V(A.Vg, vgv[:, s0 * 8 + q4 * 16:s0 * 8 + (q4 + 1) * 16, h, :]))
        if lb == 1:
            return
        qa = QA[h % 2]
        P.dma("sp", qa[64:67, :], V(A.Gd, A.Gd.t[h]))
        P.dma("pool", wq[h % 2][:], V(D.attn_wq, wqv[:, :, h * 64:(h + 1) * 64]))
        P.dma("pool", wo[h % 2][:], V(D.attn_wo, D.attn_wo.t[j, h * 64:(h + 1) * 64, :]))
        for tg in range(4):
            ps = pmi[tg % 2]
            for c in range(8):
                P.I("pe", "matmul", out=ps[0:64, :], lhsT=wq[h % 2][:, c, :], rhs=xT[:, c, tg * 512:(tg + 1) * 512],
                    start=(c == 0), stop=(c == 7), xr=xTr[tg * 4:tg * 4 + 4])
            P.I("act", "activation", out=qa[0:64, tg * 512:(tg + 1) * 512], in_=ps[0:64, :], func=AF.Copy, scale=0.125)

    units = []
    for h in range(16):
        for lb in range(2):
            slots = list(range(0, 4)) if lb == 0 else list(range(4, 12))
            for a in range(2):
                qg = lb * 2 + a
                lst = [(s_, ikp) for s_ in slots for ikp in range(4)]
                for n_, (s_, ikp) in enumerate(lst):
                    units.append((h, qg, s_, ikp, n_ == 0, n_ == len(lst) - 1))

    def score(u, idx):
        h, qg, s_, ikp, first, last = u
        lb = qg // 2
        ka, qa = KA[(h * 2 + lb) % 2], QA[h % 2]
        a = qg % 2
        ps, pt_ = pss[idx % 2], pT[idx % NPT]
        for e in range(2):
            kt = (s_ - 4 * lb) * 8 + ikp * 2 + e
            P.I("pe", "matmul", out=ps[:, e, :], lhsT=ka[0:70, kt * 128:(kt + 1) * 128],
                rhs=qa[0:70, qg * 512:(qg + 1) * 512], start=True, stop=True)
        if s_ in (0, 4):
            tf = tmpf[idx % 2]
            P.I("dve", "tensor_tensor", out=tf[:], in0=ps[:], in1=TRI[:, a, ikp * 2:ikp * 2 + 2, :], op=ALU.add)
            P.I("act", "activation", out=pt_[:], in_=tf[:], func=AF.Exp, bias=BM[:, s_:s_ + 1], scale=1.0)
        else:
            P.I("act", "activation", out=pt_[:], in_=ps[:], func=AF.Exp, bias=BM[:, s_:s_ + 1], scale=1.0)

    def pv(u, idx):
        h, qg, s_, ikp, first, last = u
        lb = qg // 2
        for e in range(2):
            kt = (s_ - 4 * lb) * 8 + ikp * 2 + e
            P.I("pe", "matmul", out=ppo[qg % 2][0:65, :], lhsT=VA[(h * 2 + lb) % 2][:, kt, :], rhs=pT[idx % NPT][:, e, :],
                start=(first and e == 0), stop=(last and e == 1))

    def ep1(h, qg):
        P.I("act", "activation", out=osb[qg % 2][0:65, :], in_=ppo[qg % 2][0:65, :], func=AF.Copy)
        P.I("dve", "reciprocal", out=rc[qg % 2][64:65, :], in_=osb[qg % 2][64:65, :])

    def ep2(h, qg):
        P.I("pe", "matmul", out=pmi[0][0:64, :], lhsT=ones1[64:65, 0:64], rhs=rc[qg % 2][64:65, :], start=True, stop=True)
        P.I("dve", "tensor_tensor", out=otn[qg % 2][:], in0=osb[qg % 2][0:64, :], in1=pmi[0][0:64, :], op=ALU.mult)

    def ep3(h, qg):
        on = otn[qg % 2]
        for t4 in range(4):
            tt = qg * 4 + t4
            bb, ii = tt // 8, tt % 8
            for half in range(2):
                p2 = pmi[(t4 * 2 + half) % 2]
                P.I("pe", "matmul", out=p2[:], lhsT=on[:, t4 * 128:(t4 + 1) * 128],
                    rhs=wo[h % 2][:, half * 512:(half + 1) * 512], start=True, stop=True)
                hv = V(C.h[bb][ii], C.hb[bb].t[:, ii, half * 512:(half + 1) * 512])
                P.I("dve", "tensor_tensor", out=hv, in0=hv, in1=p2[:], op=ALU.add)

    n = len(units)
    DEPTH = 3
    events = {}

    def at(i_, fn):
        events.setdefault(i_, []).append(fn)
    prep(0)
    prev_p = -1
    for idx in range(n + DEPTH + 8):
        for fn in events.pop(idx, []):
            fn()
        if idx < n:
            u = units[idx]
            p = u[0] * 2 + u[1] // 2
            if p != prev_p:
                prev_p = p
                if p + 1 < 32:
                    at(idx + 10, lambda p=p: prep(p + 1))
            score(u, idx)
        jx = idx - DEPTH
        if 0 <= jx < n:
            u = units[jx]
            pv(u, jx)
            if u[5]:
                ep1(u[0], u[1])
                at(idx + 2, lambda h=u[0], qg=u[1]: ep2(h, qg))
                at(idx + 4, lambda h=u[0], qg=u[1]: ep3(h, qg))
    assert not events
    P.pop()


def build_att2():
    nc = bass.Bass("TRN2", target_bir_lowering=False)
    with ExitStack() as st:
        P = Prog(nc, st)
        C = Ctx()
        names = ["mix_norm", "mlp_norm", "mlp_w1", "mlp_w2", "attn_wq", "attn_wo", "final_norm"]
        D = declare(P, names)
        A = Ctx()
        NK = NSLOT * 1024
        hin = P.dram("hin", [2048, 1024], F32, "ExternalInput")
        A.KTg = P.dram("KTg", [16, 64, NK], BF16, "ExternalInput")
        A.Vg = P.dram("Vg", [NK, 1024], BF16, "ExternalInput")
        A.Fkg = P.dram("Fkg", [NSLOT, 16, 1024], F32, "ExternalInput")
        A.Ftg = P.dram("Ftg", [16, 4], F32, "ExternalInput")
        A.Fq = P.dram("Fq", [16, 2048], F32, "ExternalInput")
        A.selk = P.dram("selk", [16, NSLOT * 4], F32, "ExternalInput")
        A.selq = P.dram("selq", [16, 8], F32, "ExternalInput")
        A.Bm = P.dram("Bm", [128, NSLOT], F32, "ExternalInput")
        A.Hd = P.dram("Hd", [16, 3, NK], BF16, "Internal")
        A.Gd = P.dram("Gd", [16, 3, 2048], BF16, "Internal")
        yout = P.dram("yout", [2048, 1024], F32, "ExternalOutput")
        make_consts(P, C)
        load_h(P, C, hin)
        xT = P.sbuf([128, 8, 2048], BF16, "xT")
        xTr = [xT.sub("xTr%d" % j) for j in range(16)]
        att_prep2(P, C, A)
        for j in range(2):
            l = 2 + j
            P.push()
            pst = [P.psum([128, 8, 128], BF16, "pst%d" % k) for k in range(2)]
            gain = load_bcast(P, D.mix_norm.t[l, :], D.mix_norm, 1024, "g")
            rmsnorm_T(P, C, gain, xT, xTr, pst)
            P.pop()
            attention2(P, C, j, D, A, xT, xTr)
            P.push()
            pst = [P.psum([128, 8, 128], BF16, "pst%d" % k) for k in range(2)]
            pm = [P.psum([128, 512], F32, "pm%d" % k) for k in range(4)]
            mlp(P, C, l, D, xT, xTr, pst, pm)
            P.pop()
        final_norm(P, C, D, yout)
        P.finish()
    return nc, names


def att2_inputs(c, r3):
    bt, q = c // 4, c % 4
    own = [q, 7 - q]

    def src(g):
        return r3[4 * bt + g // 2], slice((g % 2) * 1024, (g % 2 + 1) * 1024)
    slots = [own[0] - r for r in range(4)] + [own[1] - r for r in range(8)]
    kt, vg, fk = [], [], []
    selk = np.zeros((NSLOT, 4), np.float32)
    bm = np.zeros((NSLOT,), np.float32)
    for s_, g in enumerate(slots):
        if g < 0:
            kt.append(np.zeros((16, 64, 1024), ml_dtypes.bfloat16))
            vg.append(np.zeros((1024, 1024), ml_dtypes.bfloat16))
            fk.append(np.zeros((16, 1024), np.float32))
            bm[s_] = NEG
        else:
            r, sl = src(g)
            kt.append(r["KTo"][:, :, sl])
            vg.append(r["Vo"][sl, :])
            fk.append(r["Flo"][:, sl])
            selk[s_, :g // 2] = 1.0
    selq = np.zeros((2, 4), np.float32)
    hin, fq = [], []
    for lb, g in enumerate(own):
        r, sl = src(g)
        hin.append(r["hout"][sl, :])
        fq.append(r["Flo"][:, sl])
        selq[lb, :g // 2] = 1.0
    grp = [r3[4 * bt + k] for k in range(4)]
    return dict(hin=np.ascontiguousarray(np.concatenate(hin, 0)),
                KTg=np.ascontiguousarray(np.concatenate(kt, 2)), Vg=np.ascontiguousarray(np.concatenate(vg, 0)),
                Fkg=np.ascontiguousarray(np.stack(fk, 0)), Ftg=np.ascontiguousarray(np.concatenate([g["Fto"] for g in grp], 1)),
                Fq=np.ascontiguousarray(np.concatenate(fq, 1)), selk=np.tile(selk.reshape(1, -1), (16, 1)),
                selq=np.tile(selq.reshape(1, -1), (16, 1)), Bm=np.tile(bm.reshape(1, -1), (128, 1)))


def kernel(**inputs):
    inp = {k: np.ascontiguousarray(np.asarray(v, dtype=np.float32)) for k, v in inputs.items()}
    nc, names = get_prog("fused", build_fused)
    maps = []
    for c in range(8):
        bt, q = c // 4, c % 4
        selseg = np.zeros((128, 4), np.float32)
        selseg[:, q] = 1.0
        selq, ab = sel_att(c)
        maps.append(dict({n: inp[n] for n in names}, xfull=np.ascontiguousarray(inp["x"][bt]), selseg=selseg, selq=selq,
                         ABsel=ab))
    r = run(nc, maps)
    return np.stack([q["yout"] for q in r]).reshape(2, 8192, 1024).astype(np.float32)
```
